# Optimizing a Trainium2 kernel written in Bass

```python
import math
import jax, jax.numpy as jnp
from jax import lax
import numpy as np

D_MODEL = 1024
BATCH = 8
SEQ = 8192
DEPTH = 1
DEC_BATCH = 128
DEC_SEQ = 1
PAST_LEN = 8192
PAGE_SIZE = 128

EPS = 1e-6
D_FF = ((8 * D_MODEL // 3 + 127) // 128) * 128
D_PLE = 256
D_INNER = 2 * D_MODEL
SSM_HEAD_DIM = 64
SSM_HEADS = D_INNER // SSM_HEAD_DIM
SSM_GROUPS = 4
SSM_HPG = SSM_HEADS // SSM_GROUPS
SSM_STATE = 128
CONV_W = 4
CONV_DIM = D_INNER + 2 * SSM_GROUPS * SSM_STATE
SSD_CHUNK = 128
DIL_GROUPS = ((128, 1), (512, 4), (2048, 16))
N_DIL = len(DIL_GROUPS)
HPG = 8
ATTN_HEAD_DIM = 64
ATTN_HEADS = N_DIL * HPG
ATTN_WIDTH = ATTN_HEADS * ATTN_HEAD_DIM
ATTN_OUT = HPG * ATTN_HEAD_DIM
ATTN_SCALE = ATTN_HEAD_DIM ** -0.5
ROPE_THETA = 10000.0
IN_SPLITS = tuple(int(c) for c in np.cumsum([D_INNER, CONV_DIM, SSM_HEADS, ATTN_WIDTH, ATTN_WIDTH, ATTN_WIDTH, D_MODEL]))
N_IN = IN_SPLITS[-1] + D_MODEL

kernel_name = 'hybrid_ssd_dilated_attn_decode_step'


def rms_norm(x, gain):
    xf = x.astype(jnp.float32)
    y = xf * lax.rsqrt(jnp.mean(xf * xf, axis=-1, keepdims=True) + EPS)
    return (y * gain.astype(jnp.float32)).astype(x.dtype)


def swiglu(u, w_gate, w_up, w_down):
    return (jax.nn.silu(u @ w_gate) * (u @ w_up)) @ w_down


def rotary(x, pos):
    half = x.shape[-1] // 2
    inv_freq = ROPE_THETA ** (-jnp.arange(half, dtype=jnp.float32) / half)
    ang = pos.astype(jnp.float32)[:, None] * inv_freq[None, :]
    cos = jnp.cos(ang)[None, :, None, :]
    sin = jnp.sin(ang)[None, :, None, :]
    xf = x.astype(jnp.float32)
    x1, x2 = xf[..., :half], xf[..., half:]
    return jnp.concatenate([x1 * cos - x2 * sin, x2 * cos + x1 * sin], axis=-1).astype(x.dtype)


def causal_conv(xbc, prev, w, b):
    s = xbc.shape[1]
    xp = jnp.concatenate([prev.astype(xbc.dtype), xbc], axis=1)
    out = b
    for tap in range(CONV_W):
        out = out + xp[:, tap:tap + s] * w[tap]
    return jax.nn.silu(out), xp[:, xp.shape[1] - (CONV_W - 1):]


def ssd_chunked(x, dt, a, bm, cm):
    b, s, g, hg, p = x.shape
    n = bm.shape[-1]
    L = SSD_CHUNK
    nc = s // L

    def chunks(t):
        return jnp.moveaxis(t.reshape((b, nc, L) + t.shape[2:]), 1, 0)

    causal = jnp.tril(jnp.ones((L, L), dtype=bool))[None, :, :, None, None]

    def step(state, inp):
        xc, dtc, bc, cc = inp
        acs = jnp.cumsum(dtc * a, axis=1)
        seg = acs[:, :, None] - acs[:, None, :]
        decay = jnp.exp(jnp.where(causal, seg, -jnp.inf))
        cb = jnp.einsum('blgn,bsgn->blsg', cc, bc)
        wts = cb[..., None] * decay * dtc[:, None]
        y = jnp.einsum('blsgh,bsghp->blghp', wts, xc)
        y = y + jnp.einsum('blgn,bghpn->blghp', cc, state) * jnp.exp(acs)[..., None]
        to_end = jnp.exp(acs[:, -1:] - acs) * dtc
        state = jnp.exp(acs[:, -1])[..., None, None] * state + jnp.einsum('bsgh,bsgn,bsghp->bghpn', to_end, bc, xc)
        return state, y

    init = jnp.zeros((b, g, hg, p, n), jnp.float32)
    final, ys = lax.scan(step, init, (chunks(x), chunks(dt), chunks(bm), chunks(cm)))
    return jnp.moveaxis(ys, 0, 1).reshape(b, s, g, hg, p), final


def ssd_recurrent(x, dt, a, bm, cm, state):
    def step(st, inp):
        xt, dtt, bt, ct = inp
        st = jnp.exp(dtt * a)[..., None, None] * st + jnp.einsum('bgh,bgn,bghp->bghpn', dtt, bt, xt)
        return st, jnp.einsum('bgn,bghpn->bghp', ct, st)

    final, ys = lax.scan(step, state, (jnp.moveaxis(x, 1, 0), jnp.moveaxis(dt, 1, 0), jnp.moveaxis(bm, 1, 0), jnp.moveaxis(cm, 1, 0)))
    return jnp.moveaxis(ys, 0, 1), final


def ssd_branch(z, xbc, dt_raw, conv_prev, ssm_prev, conv_w, conv_b, dt_bias, a_log, d_skip, norm_g):
    b, s, _ = z.shape
    xbc, conv_new = causal_conv(xbc, conv_prev, conv_w, conv_b)
    xs, bm, cm = jnp.split(xbc, [D_INNER, D_INNER + SSM_GROUPS * SSM_STATE], axis=-1)
    xs = xs.reshape(b, s, SSM_GROUPS, SSM_HPG, SSM_HEAD_DIM).astype(jnp.float32)
    bm = bm.reshape(b, s, SSM_GROUPS, SSM_STATE).astype(jnp.float32)
    cm = cm.reshape(b, s, SSM_GROUPS, SSM_STATE).astype(jnp.float32)
    dt = jax.nn.softplus(dt_raw.astype(jnp.float32) + dt_bias.astype(jnp.float32)).reshape(b, s, SSM_GROUPS, SSM_HPG)
    a = -jnp.exp(a_log.astype(jnp.float32)).reshape(SSM_GROUPS, SSM_HPG)
    if ssm_prev is None:
        y, st = ssd_chunked(xs, dt, a, bm, cm)
    else:
        st0 = ssm_prev.astype(jnp.float32).reshape(b, SSM_GROUPS, SSM_HPG, SSM_HEAD_DIM, SSM_STATE)
        y, st = ssd_recurrent(xs, dt, a, bm, cm, st0)
    y = y + d_skip.astype(jnp.float32).reshape(SSM_GROUPS, SSM_HPG)[:, :, None] * xs
    y = y.reshape(b, s, D_INNER).astype(z.dtype)
    y = rms_norm(y * jax.nn.silu(z), norm_g)
    return y, conv_new, st.reshape(b, SSM_HEADS, SSM_HEAD_DIM, SSM_STATE)


def dilated_attention_prompt(q, k, v, window, dil):
    b, s, h, dh = q.shape
    nk = window // dil
    sp = -(-s // window) * window
    nb = sp // window

    def to_blocks(t):
        t = jnp.pad(t.astype(jnp.float32), ((0, 0), (0, sp - s), (0, 0), (0, 0)))
        t = t.reshape(b, sp // dil, dil, h, dh).transpose(0, 2, 1, 3, 4)
        return t.reshape(b, dil, nb, nk, h, dh)

    qb, kb, vb = to_blocks(q), to_blocks(k), to_blocks(v)

    def with_prev(t):
        prev = jnp.pad(t, ((0, 0), (0, 0), (1, 0), (0, 0), (0, 0), (0, 0)))[:, :, :-1]
        return jnp.concatenate([prev, t], axis=3)

    kk, vv = with_prev(kb), with_prev(vb)
    scores = jnp.einsum('brnihd,brnjhd->brnhij', qb, kk) * ATTN_SCALE
    qi = jnp.arange(nk)[:, None]
    kj = jnp.arange(2 * nk)[None, :]
    dist = qi + nk - kj
    band = (dist >= 0) & (dist <= nk)
    has_prev = (jnp.arange(nb) > 0)[:, None, None] | (kj >= nk)[None]
    mask = band[None] & has_prev
    scores = jnp.where(mask[None, None, :, None], scores, -jnp.inf)
    m = jnp.max(scores, axis=-1, keepdims=True)
    p = jnp.exp(scores - m)
    l = jnp.sum(p, axis=-1, keepdims=True)
    o = jnp.einsum('brnhij,brnjhd->brnihd', p / l, vv)
    lse = (m + jnp.log(l))[..., 0]
    o = o.reshape(b, dil, sp // dil, h, dh).transpose(0, 2, 1, 3, 4).reshape(b, sp, h, dh)[:, :s]
    lse = lse.transpose(0, 1, 2, 4, 3).reshape(b, dil, sp // dil, h).transpose(0, 2, 1, 3).reshape(b, sp, h)[:, :s]
    return o, lse


def dilated_attention_sample(q, k, v, cache_kv, window, dil):
    lc = cache_kv.shape[1]
    t = q.shape[1]
    nk = window // dil
    keys = jnp.concatenate([cache_kv[:, :, 0].astype(jnp.float32), k.astype(jnp.float32)], axis=1)
    vals = jnp.concatenate([cache_kv[:, :, 1].astype(jnp.float32), v.astype(jnp.float32)], axis=1)
    idx = lc + jnp.arange(t)[:, None] - dil * jnp.arange(nk + 1)[None, :]
    valid = idx >= 0
    idx = jnp.maximum(idx, 0)
    kg = keys[:, idx]
    vg = vals[:, idx]
    scores = jnp.einsum('bthd,btkhd->bthk', q.astype(jnp.float32), kg) * ATTN_SCALE
    scores = jnp.where(valid[None, :, None, :], scores, -jnp.inf)
    m = jnp.max(scores, axis=-1, keepdims=True)
    p = jnp.exp(scores - m)
    l = jnp.sum(p, axis=-1, keepdims=True)
    o = jnp.einsum('bthk,btkhd->bthd', p / l, vg)
    lse = (m + jnp.log(l))[..., 0]
    return o, lse


def decoder_layer(x, p_emb, pos, conv_prev, ssm_prev, kv_prev, prm):
    b, s, _ = x.shape
    h = x + 0.5 * swiglu(rms_norm(x, prm['norm_ffn1']), prm['w_ffn1_gate'], prm['w_ffn1_up'], prm['w_ffn1_down'])
    u = rms_norm(h, prm['norm_mix'])
    z, xbc, dt_raw, q, k, v, gate_ssm, gate_attn = jnp.split(u @ prm['w_in'], IN_SPLITS, axis=-1)
    if conv_prev is None:
        conv_prev = jnp.zeros((b, CONV_W - 1, CONV_DIM), x.dtype)
    y_ssm, conv_new, ssm_new = ssd_branch(z, xbc, dt_raw, conv_prev, ssm_prev, prm['conv_w'], prm['conv_b'],
                                          prm['dt_bias'], prm['a_log'], prm['d_skip'], prm['norm_ssm'])
    q = rotary(q.reshape(b, s, ATTN_HEADS, ATTN_HEAD_DIM), pos)
    k = rotary(k.reshape(b, s, ATTN_HEADS, ATTN_HEAD_DIM), pos)
    v = v.reshape(b, s, ATTN_HEADS, ATTN_HEAD_DIM)
    outs, lses, kv_new = [], [], []
    for gi, (window, dil) in enumerate(DIL_GROUPS):
        sl = slice(gi * HPG, (gi + 1) * HPG)
        qg, kg, vg = q[:, :, sl], k[:, :, sl], v[:, :, sl]
        if kv_prev is None:
            o, lse = dilated_attention_prompt(qg, kg, vg, window, dil)
            keep = min(window, s)
            kv_new.append(jnp.stack([kg[:, s - keep:], vg[:, s - keep:]], axis=2))
        else:
            o, lse = dilated_attention_sample(qg, kg, vg, kv_prev[gi], window, dil)
            kv_new.append(jnp.stack([kg, vg], axis=2))
        outs.append(o)
        lses.append(lse)
    alpha = jax.nn.softmax(jnp.stack(lses), axis=0)
    attn = jnp.einsum('gbsh,gbshd->bshd', alpha, jnp.stack(outs)).reshape(b, s, ATTN_OUT).astype(x.dtype)
    merged = jax.nn.sigmoid(gate_ssm) * (y_ssm @ prm['w_o_ssm']) + jax.nn.sigmoid(gate_attn) * (attn @ prm['w_o_attn'])
    h = h + merged @ prm['w_out']
    h = h + 0.5 * swiglu(rms_norm(h, prm['norm_ffn2']), prm['w_ffn2_gate'], prm['w_ffn2_up'], prm['w_ffn2_down'])
    h = h + jax.nn.sigmoid(rms_norm(h, prm['norm_ple']) @ prm['w_ple_gate']) * (p_emb @ prm['w_ple_proj'])
    return h, kv_new, conv_new, ssm_new


def setup_inputs(seed: int = 0) -> dict:
    key = jax.random.key(seed)
    ks = iter(jax.random.split(key, 48))
    f32 = jnp.float32
    L = DEPTH

    def nrm(shape, scale):
        return jax.random.normal(next(ks), shape, f32) * scale

    def gain(shape):
        return 1.0 + nrm(shape, 0.01)

    dt0 = jnp.exp(jax.random.uniform(next(ks), (L, SSM_HEADS), f32, math.log(1e-3), math.log(1e-1)))
    dt_bias = dt0 + jnp.log(-jnp.expm1(-dt0))
    a_log = jnp.log(jax.random.uniform(next(ks), (L, SSM_HEADS), f32, 1.0, 16.0))
    kv_len = [min(w, PAST_LEN) for (w, _) in DIL_GROUPS]
    return {
        'x_prompt': nrm((BATCH, SEQ, D_MODEL), 1.0),
        'x_sample': nrm((DEC_BATCH, DEC_SEQ, D_MODEL), 1.0),
        'cache_kv_w128': nrm((L, DEC_BATCH, kv_len[0], 2, HPG, ATTN_HEAD_DIM), 1.0),
        'cache_kv_w512': nrm((L, DEC_BATCH, kv_len[1], 2, HPG, ATTN_HEAD_DIM), 1.0),
        'cache_kv_w2048': nrm((L, DEC_BATCH, kv_len[2], 2, HPG, ATTN_HEAD_DIM), 1.0),
        'state_conv': nrm((L, DEC_BATCH, CONV_W - 1, CONV_DIM), 1.0),
        'state_ssm': nrm((L, DEC_BATCH, SSM_HEADS, SSM_HEAD_DIM, SSM_STATE), 0.1),
        'p_prompt': nrm((L, BATCH, SEQ, D_PLE), 1.0),
        'p_sample': nrm((L, DEC_BATCH, DEC_SEQ, D_PLE), 1.0),
        'norm_ffn1': gain((L, D_MODEL)),
        'w_ffn1_gate': nrm((L, D_MODEL, D_FF), D_MODEL ** -0.5),
        'w_ffn1_up': nrm((L, D_MODEL, D_FF), D_MODEL ** -0.5),
        'w_ffn1_down': nrm((L, D_FF, D_MODEL), D_FF ** -0.5),
        'norm_mix': gain((L, D_MODEL)),
        'w_in': nrm((L, D_MODEL, N_IN), D_MODEL ** -0.5),
        'conv_w': nrm((L, CONV_W, CONV_DIM), CONV_W ** -0.5),
        'conv_b': nrm((L, CONV_DIM), 0.01),
        'dt_bias': dt_bias,
        'a_log': a_log,
        'd_skip': gain((L, SSM_HEADS)),
        'norm_ssm': gain((L, D_INNER)),
        'w_o_ssm': nrm((L, D_INNER, D_MODEL), D_INNER ** -0.5),
        'w_o_attn': nrm((L, ATTN_OUT, D_MODEL), ATTN_OUT ** -0.5),
        'w_out': nrm((L, D_MODEL, D_MODEL), D_MODEL ** -0.5),
        'norm_ffn2': gain((L, D_MODEL)),
        'w_ffn2_gate': nrm((L, D_MODEL, D_FF), D_MODEL ** -0.5),
        'w_ffn2_up': nrm((L, D_MODEL, D_FF), D_MODEL ** -0.5),
        'w_ffn2_down': nrm((L, D_FF, D_MODEL), D_FF ** -0.5),
        'norm_ple': gain((L, D_MODEL)),
        'w_ple_gate': nrm((L, D_MODEL, D_MODEL), D_MODEL ** -0.5),
        'w_ple_proj': nrm((L, D_PLE, D_MODEL), D_PLE ** -0.5),
        'norm_final': gain((D_MODEL,)),
    }


def reference(x_prompt, x_sample, cache_kv_w128, cache_kv_w512, cache_kv_w2048, state_conv, state_ssm,
              p_prompt, p_sample, norm_ffn1, w_ffn1_gate, w_ffn1_up, w_ffn1_down, norm_mix, w_in,
              conv_w, conv_b, dt_bias, a_log, d_skip, norm_ssm, w_o_ssm, w_o_attn, w_out,
              norm_ffn2, w_ffn2_gate, w_ffn2_up, w_ffn2_down, norm_ple, w_ple_gate, w_ple_proj, norm_final):
    pos_prompt = jnp.arange(x_prompt.shape[1], dtype=jnp.int32)
    pos_sample = PAST_LEN + jnp.arange(x_sample.shape[1], dtype=jnp.int32)
    hp, hs = x_prompt, x_sample
    kvp_l, convp_l, ssmp_l, kvs_l, convs_l, ssms_l = [], [], [], [], [], []
    for i in range(DEPTH):
        prm = {
            'norm_ffn1': norm_ffn1[i], 'w_ffn1_gate': w_ffn1_gate[i], 'w_ffn1_up': w_ffn1_up[i],
            'w_ffn1_down': w_ffn1_down[i], 'norm_mix': norm_mix[i], 'w_in': w_in[i],
            'conv_w': conv_w[i], 'conv_b': conv_b[i], 'dt_bias': dt_bias[i], 'a_log': a_log[i],
            'd_skip': d_skip[i], 'norm_ssm': norm_ssm[i], 'w_o_ssm': w_o_ssm[i], 'w_o_attn': w_o_attn[i],
            'w_out': w_out[i], 'norm_ffn2': norm_ffn2[i], 'w_ffn2_gate': w_ffn2_gate[i],
            'w_ffn2_up': w_ffn2_up[i], 'w_ffn2_down': w_ffn2_down[i], 'norm_ple': norm_ple[i],
            'w_ple_gate': w_ple_gate[i], 'w_ple_proj': w_ple_proj[i],
        }
        hp, kvp, convp, ssmp = decoder_layer(hp, p_prompt[i], pos_prompt, None, None, None, prm)
        hs, kvs, convs, ssms = decoder_layer(hs, p_sample[i], pos_sample, state_conv[i], state_ssm[i],
                                             (cache_kv_w128[i], cache_kv_w512[i], cache_kv_w2048[i]), prm)
        kvp_l.append(kvp); convp_l.append(convp); ssmp_l.append(ssmp)
        kvs_l.append(kvs); convs_l.append(convs); ssms_l.append(ssms)
    y_prompt = rms_norm(hp, norm_final)
    y_sample = rms_norm(hs, norm_final)
    kv128_p = jnp.stack([kv[0] for kv in kvp_l])
    kv512_p = jnp.stack([kv[1] for kv in kvp_l])
    kv2048_p = jnp.stack([kv[2] for kv in kvp_l])
    kv128_s = jnp.stack([kv[0] for kv in kvs_l])
    kv512_s = jnp.stack([kv[1] for kv in kvs_l])
    kv2048_s = jnp.stack([kv[2] for kv in kvs_l])
    conv_p = jnp.stack(convp_l)
    ssm_p = jnp.stack(ssmp_l)
    conv_s = jnp.stack(convs_l)
    ssm_s = jnp.stack(ssms_l)
    return (y_prompt, y_sample, kv128_p, kv512_p, kv2048_p, conv_p, ssm_p, kv128_s, kv512_s, kv2048_s, conv_s, ssm_s)
```

```python
import numpy as np
from contextlib import ExitStack
import concourse.bass as bass
import concourse.mybir as mybir
from concourse.bass_utils import run_bass_kernel_spmd

F32 = mybir.dt.float32
BF16 = mybir.dt.bfloat16
ALU = mybir.AluOpType
AF = mybir.ActivationFunctionType
AX = mybir.AxisListType

D = 1024
DFF = 2816
import os
T = int(os.environ.get('KT_T', 8192))
NS = 16
NCORES = 8
NIN = 11808
EPS = 1e-6


class Buf:
    __slots__ = ("name", "w", "r", "wd", "rd")

    def __init__(self, name):
        self.name = name
        self.w = {}
        self.r = {}
        self.wd = []
        self.rd = []


class Op:
    __slots__ = ("eng", "fn", "deps", "needs_inc", "inc_idx", "is_dma", "dsem", "dval", "dprev")

    def __init__(self, eng, fn, is_dma=False):
        self.eng = eng
        self.fn = fn
        self.deps = []
        self.needs_inc = False
        self.inc_idx = None
        self.is_dma = is_dma
        self.dsem = None
        self.dval = None
        self.dprev = None


ENGS = ("pe", "act", "dve", "pool", "sp")
NDSEM = 8


class Sched:
    def __init__(self):
        self.ops = {e: [] for e in ENGS}
        self.ndma = {e: 0 for e in ENGS}
        self.all_bufs = []

    def buf(self, name):
        b = Buf(name)
        self.all_bufs.append(b)
        return b

    def add(self, eng, fn, reads=(), writes=(), dma=False):
        op = Op(eng, fn, dma)
        deps = []
        raw = []
        for b in reads:
            raw.extend(b.w.values())
            raw.extend(b.wd)
        deps.extend(raw)
        for b in writes:
            deps.extend(b.w.values())
            deps.extend(b.wd)
            deps.extend(b.r.values())
            deps.extend(b.rd)
        seen = set()
        raw_ids = set(id(d) for d in raw)
        for d in deps:
            if d is op or id(d) in seen:
                continue
            seen.add(id(d))
            if d.eng == eng and not d.is_dma and not dma and id(d) not in raw_ids and os.environ.get('KNOSTRICT'):
                continue
            op.deps.append(d)
        for b in reads:
            if dma:
                b.rd.append(op)
                if len(b.rd) > 24:
                    b.rd.pop(0)
            else:
                b.r[eng] = op
        for b in writes:
            if dma:
                b.wd.append(op)
                if len(b.wd) > 24:
                    b.wd.pop(0)
            else:
                b.w[eng] = op
        if dma:
            i = self.ndma[eng]
            self.ndma[eng] += 1
            op.dsem = i % NDSEM
            op.dval = 16 * (i // NDSEM + 1)
        self.ops[eng].append(op)
        return op

    def barrier(self):
        last = []
        for e in ENGS:
            ops = self.ops[e]
            for o in reversed(ops):
                if o.fn is not None and not o.is_dma:
                    last.append(o)
                    break
            cnt = 0
            for o in reversed(ops):
                if o.is_dma:
                    last.append(o)
                    cnt += 1
                    if cnt >= NDSEM:
                        break
        for e in ENGS:
            op = Op(e, None)
            op.deps = [d for d in last]
            self.ops[e].append(op)
        for b in self.all_bufs:
            b.w = {}
            b.r = {}
            b.wd = []
            b.rd = []
        self.all_bufs = []

    def finalize(self):
        for e in ENGS:
            for op in self.ops[e]:
                for d in op.deps:
                    if not d.is_dma:
                        d.needs_inc = True
        for e in ENGS:
            c = 0
            for op in self.ops[e]:
                if op.needs_inc and not op.is_dma:
                    c += 1
                    op.inc_idx = c

    def emit(self, eng, handle, sems, dsems):
        waited = {}
        for op in self.ops[eng]:
            need = {}
            for d in op.deps:
                if d.is_dma:
                    key = ("d", d.eng, d.dsem)
                    val = d.dval
                else:
                    if d.eng == eng and d.inc_idx is None:
                        continue
                    key = ("c", d.eng)
                    val = d.inc_idx
                if need.get(key, 0) < val:
                    need[key] = val
            if op.is_dma:
                if op.dval > 16:
                    key = ("d", eng, op.dsem)
                    if need.get(key, 0) < op.dval - 16:
                        need[key] = op.dval - 16
            for key, val in need.items():
                if waited.get(key, 0) >= val:
                    continue
                waited[key] = val
                if key[0] == "c":
                    handle.wait_ge(sems[key[1]], val)
                else:
                    handle.wait_ge(dsems[key[1]][key[2]], val)
            if op.fn is None:
                continue
            inst = op.fn(handle)
            if op.is_dma:
                inst.then_inc(dsems[eng][op.dsem], 16)
            elif op.needs_inc:
                inst.then_inc(sems[eng], 1)


class Prog:
    def __init__(self, debug_outs=()):
        self.nc = bass.Bass("TRN2", target_bir_lowering=False)
        self.S = Sched()
        self.debug_outs = set(debug_outs)
        self.ins = {}
        self.outs = {}
        self.scr = {}

    def din(self, name, shape, dt=F32):
        t = self.nc.dram_tensor(name, list(shape), dt, kind="ExternalInput").ap()
        self.ins[name] = t
        return t

    def dout(self, name, shape, dt=F32):
        t = self.nc.dram_tensor(name, list(shape), dt, kind="ExternalOutput").ap()
        self.outs[name] = t
        return t

    def dscr(self, name, shape, dt):
        kind = "ExternalOutput" if name in self.debug_outs else "Internal"
        t = self.nc.dram_tensor(name, list(shape), dt, kind=kind).ap()
        self.scr[name] = t
        return t


import os
NPOOL = 104448
TA = T + NS
W_G = (128, 512, 2048)
DIL = (1, 4, 16)


def build(debug_outs=(), phases=("F1", "A1", "B", "A2", "C", "D1", "D2", "D3")):
    P = Prog(debug_outs)
    nc = P.nc
    S = P.S
    es = ExitStack()

    x_p = P.din("x_prompt", [T, D])
    x_s = P.din("x_sample", [NS, D])
    p_p = P.din("p_prompt", [T, 256])
    p_s = P.din("p_sample", [NS, 256])
    norm_ffn1 = P.din("norm_ffn1", [1, D])
    w1g = P.din("w_ffn1_gate", [D, DFF])
    w1u = P.din("w_ffn1_up", [D, DFF])
    w1d = P.din("w_ffn1_down", [DFF, D])
    norm_mix = P.din("norm_mix", [1, D])
    w_in = P.din("w_in", [D, NIN])
    conv_wT = P.din("conv_wT", [128, 24, 4])
    conv_bT = P.din("conv_bT", [128, 24])
    dt_bias = P.din("dt_bias", [1, 32])
    a_log = P.din("a_log", [1, 32])
    d_skip = P.din("d_skip", [1, 32])
    norm_ssm = P.din("norm_ssm", [1, 2048])
    w_o_ssm = P.din("w_o_ssm", [2048, D])
    w_o_attn = P.din("w_o_attn", [512, D])
    w_out = P.din("w_out", [D, D])
    norm_ffn2 = P.din("norm_ffn2", [1, D])
    w2g = P.din("w_ffn2_gate", [D, DFF])
    w2u = P.din("w_ffn2_up", [D, DFF])
    w2d = P.din("w_ffn2_down", [DFF, D])
    norm_ple = P.din("norm_ple", [1, D])
    w_ple_gate = P.din("w_ple_gate", [D, D])
    w_ple_proj = P.din("w_ple_proj", [256, D])
    norm_final = P.din("norm_final", [1, D])
    state_conv = P.din("state_conv", [NS * 3, 3072])
    state_ssm = P.din("state_ssm", [NS, 2048, 128])
    cache_kv = [P.din(f"cache_kv{g}", [NS, W_G[g], 1024]) for g in range(3)]
    ident_d = P.din("c_ident", [128, 128])
    perm_d = P.din("c_perm", [128, 128])
    cos_d = P.din("c_cos", [128, TA])
    sin_d = P.din("c_sin", [128, TA])
    cmask_d = P.din("c_masks", [128, 5 * 128])
    expand_d = P.din("c_expand", [32, 2048])
    dt_biasT = P.din("dt_biasT", [32, 1])
    a_logT = P.din("a_logT", [32, 1])
    d_skipT = P.din("d_skipT", [32, 1])
    norm_ssmT = P.din("norm_ssmT", [128, 16])

    H1 = P.dscr("H1", [TA, D], F32)
    U = P.dscr("U", [D, TA], BF16)
    XC = P.dscr("XC", [3072, TA], BF16)
    QT = P.dscr("QT", [1536, TA], BF16)
    KT = P.dscr("KT", [1536, TA], BF16)
    V = P.dscr("V", [TA, 1536], BF16)
    MS = P.dscr("MS", [D, TA], BF16)
    AO = [P.dscr(f"AO{g}", [T, 520], F32) for g in range(3)]
    AOS = P.dscr("AOS", [NS, 520], F32)
    QKS = P.dscr("QKS", [NS, 3072], F32)
    BCS = P.dscr("BCS", [NS, 1024], F32)
    H2 = P.dscr("H2", [TA, D], F32)
    H3 = P.dscr("H3", [TA, D], F32)

    y_p = P.dout("y_prompt", [T, D])
    y_s = P.dout("y_sample", [NS, D])
    kv_p = [P.dout(f"kv{g}_p", [W_G[g], 2, 512]) for g in range(3)]
    conv_p = P.dout("conv_p", [3, 3072])
    ssm_p = P.dout("ssm_p", [2048, 128])
    kv_s = [P.dout(f"kv{g}_s", [NS, 2, 512]) for g in range(3)]
    conv_s = P.dout("conv_s", [NS, 3, 3072])
    ssm_s = P.dout("ssm_s", [NS, 2048, 128])

    POOL = es.enter_context(nc.sbuf_tensor("pool", [128, NPOOL], BF16))
    CP = es.enter_context(nc.sbuf_tensor("cpool", [128, 256], F32))
    ident_f = es.enter_context(nc.sbuf_tensor("ident_f", [128, 128], F32))
    ident = es.enter_context(nc.sbuf_tensor("ident_b", [128, 128], BF16))
    PS = [es.enter_context(nc.psum_tensor(f"ps{i}", [128, 512], F32)) for i in range(8)]

    class Alloc:
        def __init__(self):
            self.off = 0

        def bf(self, n):
            n2 = (n + 1) // 2 * 2
            v = POOL[:, self.off:self.off + n]
            self.off += n2
            assert self.off <= NPOOL, self.off
            return v

        def f32(self, n):
            v = POOL[:, self.off:self.off + 2 * n].bitcast(F32)
            self.off += 2 * n
            assert self.off <= NPOOL, self.off
            return v

    b_ident = S.buf("ident")
    bi0 = S.buf("identf")
    S.add("sp", lambda e: e.dma_start(out=ident_f[:], in_=ident_d[:, :]), writes=[bi0], dma=True)
    S.add("dve", lambda e: e.tensor_copy(out=ident[:], in_=ident_f[:]), reads=[bi0], writes=[b_ident])
    mhalf = CP[:, 255:256]
    b_mhalf = S.buf("mhalf")
    S.add("pool", lambda e: e.memset(mhalf, -0.5), writes=[b_mhalf])
    b_pool0 = S.buf("pool0")
    NQ = NPOOL // 4
    for qi, en in enumerate(("dve", "pool", "act", "dve")):
        if en == "act":
            S.add(en, (lambda e, qi=qi: e.memzero(POOL[:, qi * NQ:(qi + 1) * NQ])), writes=[b_pool0])
        else:
            S.add(en, (lambda e, qi=qi: e.memset(POOL[:, qi * NQ:(qi + 1) * NQ], 0.0)), writes=[b_pool0])
    S.add("dve", lambda e: e.memset(CP[:, 0:255], 0.0), writes=[b_pool0])
    S.barrier()
    cvt_rr = [0]

    def new_ps():
        return [S.buf(f"ps{i}") for i in range(8)]

    def load_weight(dst_ap_fn, src, kchunks, c_lo, ncols, wbuf, stage_bufs, stage_aps, CH=1024):
        srcv = src.rearrange("(k p) n -> p k n", p=128)
        i = 0
        for k in range(kchunks):
            for c0 in range(0, ncols, CH):
                cw = min(CH, ncols - c0)
                sb = stage_bufs[i % 2]
                sa = stage_aps[i % 2]
                q = "sp" if i % 2 == 0 else "pool"
                S.add(q, (lambda e, sa=sa, k=k, c0=c0, cw=cw: e.dma_start(out=sa[:, 0:cw], in_=srcv[:, k, c_lo + c0:c_lo + c0 + cw])),
                      writes=[sb], dma=True)
                ce = ("act", "dve")[cvt_rr[0] % 2]
                cvt_rr[0] += 1
                dst = dst_ap_fn(k, c0, cw)
                if ce == "act":
                    S.add("act", (lambda e, dst=dst, sa=sa, cw=cw: e.copy(out=dst, in_=sa[:, 0:cw])), reads=[sb], writes=[wbuf])
                else:
                    S.add("dve", (lambda e, dst=dst, sa=sa, cw=cw: e.tensor_copy(out=dst, in_=sa[:, 0:cw])), reads=[sb], writes=[wbuf])
                i += 1

    def load_bc(dst, src_row, n, buf, q="sp"):
        S.add(q, lambda e: e.dma_start(out=dst, in_=src_row[0:1, :].to_broadcast([128, n])), writes=[buf], dma=True)

    def rstd_ops(h, b_h, r, ssv, b_ssv, junk, b_junk, n):
        S.add("act", (lambda e: e.activation(out=junk[0:r, 0:n], in_=h[0:r, 0:n], func=AF.Square, accum_out=ssv[0:r, :])),
              reads=[b_h], writes=[b_junk, b_ssv])
        S.add("dve", (lambda e: e.tensor_scalar(out=ssv[0:r, :], in0=ssv[0:r, :], scalar1=1.0 / n, scalar2=EPS, op0=ALU.mult, op1=ALU.add)),
              reads=[b_ssv], writes=[b_ssv])
        S.add("pool", (lambda e: e.tensor_tensor(out=ssv[0:r, :], in0=ssv[0:r, :], in1=mhalf[0:r, :], op=ALU.pow)),
              reads=[b_ssv, b_mhalf], writes=[b_ssv])

    def norm_T(h, b_h, r, gbc, b_g, ssv, b_ssv, junk, b_junk, xn, b_xn, pt, b_pt, dstT, b_dst, c0, nk=8):
        n = nk * 128
        rstd_ops(h, b_h, r, ssv, b_ssv, junk, b_junk, n)
        S.add("dve", (lambda e: e.scalar_tensor_tensor(out=xn[0:r, 0:n], in0=h[0:r, 0:n], scalar=ssv[0:r, :], in1=gbc[0:r, 0:n], op0=ALU.mult, op1=ALU.mult)),
              reads=[b_h, b_ssv, b_g], writes=[b_xn])
        trans_T(xn, b_xn, r, pt, b_pt, dstT, b_dst, c0, nk)

    def trans_T(xn, b_xn, r, pt, b_pt, dstT, b_dst, c0, nk):
        for k0 in range(0, nk, 8):
            kk = min(8, nk - k0)
            ptv = pt[:].bitcast(BF16)[:, 0:1024].rearrange("p (k t) -> p k t", k=8)

            def tr(e, k0=k0, kk=kk, ptv=ptv):
                ins = None
                for k in range(kk):
                    ins = e.transpose(out=ptv[:, k, 0:r], in_=xn[0:r, (k0 + k) * 128:(k0 + k + 1) * 128], identity=ident[0:r, 0:r])
                return ins
            S.add("pe", tr, reads=[b_xn, b_ident], writes=[b_pt])
            S.add("act", (lambda e, k0=k0, kk=kk, ptv=ptv: e.copy(out=dstT[:, k0:k0 + kk, c0:c0 + r], in_=ptv[:, 0:kk, 0:r])),
                  reads=[b_pt], writes=[b_dst])

    def tiles_of(TT):
        return [(True, t0, TT) for t0 in range(0, T, TT)] + [(False, 0, NS)]

    def ffn_phase(src_p, src_s, gain_in, wg_d, wu_d, wd_d, post_init, post_sub, post_tile):
        TT = 256
        A = Alloc()
        Wg = A.bf(8 * DFF).rearrange("p (k n) -> p k n", k=8)
        Wu = A.bf(8 * DFF).rearrange("p (k n) -> p k n", k=8)
        Wd = A.bf(22 * D).rearrange("p (k n) -> p k n", k=22)
        st = A.f32(4096)
        xs_all = [[st[:, 0:1024], st[:, 1024:2048]], [st[:, 2048:3072], st[:, 3072:4096]]]
        stage = xs_all[0]
        hs_ = A.f32(2048)
        hs = [hs_[:, 0:1024], hs_[:, 1024:2048]]
        gbc = A.f32(1024)
        xn_ = A.bf(2048)
        xn = [xn_[:, 0:1024], xn_[:, 1024:2048]]
        uT_all = [A.bf(8 * TT).rearrange("p (k t) -> p k t", k=8) for _ in range(2)]
        sg_ = A.bf(2 * TT)
        sg = [sg_[:, 0:TT], sg_[:, TT:2 * TT]]
        actT = A.bf(22 * TT).rearrange("p (k t) -> p k t", k=22)
        junk = A.bf(1024)
        b_w = S.buf("w")
        b_xs_all = [[S.buf("st0"), S.buf("st1")], [S.buf("st2"), S.buf("st3")]]
        b_xs = b_xs_all[0]
        load_weight(lambda k, c0, cw: Wg[:, k, c0:c0 + cw], wg_d, 8, 0, DFF, b_w, b_xs, stage)
        load_weight(lambda k, c0, cw: Wu[:, k, c0:c0 + cw], wu_d, 8, 0, DFF, b_w, b_xs, stage)
        load_weight(lambda k, c0, cw: Wd[:, k, c0:c0 + cw], wd_d, 22, 0, D, b_w, b_xs, stage)
        b_g = S.buf("g")
        load_bc(gbc, gain_in, D, b_g)
        b_hs = [S.buf("h0"), S.buf("h1")]
        b_xn = [S.buf("xn0"), S.buf("xn1")]
        b_uT_all = [S.buf("uT0"), S.buf("uT1")]
        b_sg = [S.buf("sg0"), S.buf("sg1")]
        b_actT = S.buf("actT")
        b_ss = [S.buf("ss0"), S.buf("ss1")]
        b_ps = new_ps()
        ss = [CP[:, i:i + 1] for i in range(2)]
        b_junk = S.buf("junk")
        env = dict(A=A, PS=PS, b_ps=b_ps, ss=ss, b_ss=b_ss, junk=junk, b_junk=b_junk, TT=TT)
        post_init(env)
        for ti, (is_p, t0, tt) in enumerate(tiles_of(TT)):
            xs, b_xs = xs_all[ti % 2], b_xs_all[ti % 2]
            uT, b_uT = uT_all[ti % 2], b_uT_all[ti % 2]
            src = src_p if is_p else src_s
            nsub = (tt + 127) // 128
            row0 = t0 if is_p else T
            for s in range(nsub):
                r = min(128, tt - s * 128)
                xa = xs[s]
                S.add("sp", (lambda e, xa=xa, r=r, s=s, src=src, t0=t0: e.dma_start(out=xa[0:r, :], in_=src[t0 + s * 128:t0 + s * 128 + r, :])),
                      writes=[b_xs[s]], dma=True)
                norm_T(xa, b_xs[s], r, gbc, b_g, ss[s], b_ss[s], junk, b_junk, xn[s], b_xn[s], PS[6 + s], b_ps[6 + s], uT, b_uT, s * 128)
            for m in range(22):
                pg = PS[(m % 2) * 2]
                pu = PS[(m % 2) * 2 + 1]
                bg_, bu_ = b_ps[(m % 2) * 2], b_ps[(m % 2) * 2 + 1]

                def mm(e, m=m, pg=pg, pu=pu, tt=tt, uT=uT):
                    ins = None
                    for k in range(8):
                        ins = e.matmul(pg[:, 0:tt], lhsT=Wg[:, k, m * 128:(m + 1) * 128], rhs=uT[:, k, 0:tt], start=(k == 0), stop=(k == 7))
                    for k in range(8):
                        ins = e.matmul(pu[:, 0:tt], lhsT=Wu[:, k, m * 128:(m + 1) * 128], rhs=uT[:, k, 0:tt], start=(k == 0), stop=(k == 7))
                    return ins
                S.add("pe", mm, reads=[b_w, b_uT], writes=[bg_, bu_])
                sgt = sg[m % 2]
                S.add("act", (lambda e, pg=pg, sgt=sgt, tt=tt: e.activation(out=sgt[:, 0:tt], in_=pg[:, 0:tt], func=AF.Silu)),
                      reads=[bg_], writes=[b_sg[m % 2]])
                S.add("dve", (lambda e, pu=pu, sgt=sgt, tt=tt, m=m: e.tensor_tensor(out=actT[:, m, 0:tt], in0=pu[:, 0:tt], in1=sgt[:, 0:tt], op=ALU.mult)),
                      reads=[bu_, b_sg[m % 2]], writes=[b_actT])
            for s in range(nsub):
                r = min(128, tt - s * 128)
                for nh in range(2):
                    pd = PS[4 + nh]

                    def mmd(e, s=s, r=r, nh=nh, pd=pd):
                        ins = None
                        for k in range(22):
                            ins = e.matmul(pd[0:r, :], lhsT=actT[:, k, s * 128:s * 128 + r], rhs=Wd[:, k, nh * 512:(nh + 1) * 512], start=(k == 0), stop=(k == 21))
                        return ins
                    S.add("pe", mmd, reads=[b_w, b_actT], writes=[b_ps[4 + nh]])
                    S.add("dve", (lambda e, s=s, r=r, nh=nh, pd=pd, xs=xs: e.scalar_tensor_tensor(out=hs[s][0:r, nh * 512:(nh + 1) * 512], in0=pd[0:r, :], scalar=0.5, in1=xs[s][0:r, nh * 512:(nh + 1) * 512], op0=ALU.mult, op1=ALU.add)),
                          reads=[b_ps[4 + nh], b_xs[s]], writes=[b_hs[s]])
                post_sub(env, s, r, row0 + s * 128, hs[s], b_hs[s])
            post_tile(env, row0, tt)
        S.barrier()

    def f1_init(env):
        A = env["A"]
        env["gbc2"] = A.f32(1024)
        env["b_g2"] = S.buf("g2")
        load_bc(env["gbc2"], norm_mix, D, env["b_g2"])
        x2 = A.bf(2048)
        env["xn2"] = [x2[:, 0:1024], x2[:, 1024:2048]]
        env["b_xn2"] = [S.buf("xn2a"), S.buf("xn2b")]
        env["u2T"] = A.bf(8 * env["TT"]).rearrange("p (k t) -> p k t", k=8)
        env["b_u2T"] = S.buf("u2T")

    def f1_sub(env, s, r, row, h, b_h):
        S.add("pool", (lambda e: e.dma_start(out=H1[row:row + r, :], in_=h[0:r, :])), reads=[b_h], dma=True)
        norm_T(h, b_h, r, env["gbc2"], env["b_g2"], env["ss"][s], env["b_ss"][s], env["junk"], env["b_junk"],
               env["xn2"][s], env["b_xn2"][s], env["PS"][6 + s], env["b_ps"][6 + s], env["u2T"], env["b_u2T"], s * 128)

    def f1_tile(env, row0, tt):
        Uv = U.rearrange("(k p) t -> p k t", p=128)
        u2T = env["u2T"]
        S.add("sp", (lambda e: e.dma_start(out=Uv[:, :, row0:row0 + tt], in_=u2T[:, :, 0:tt])), reads=[env["b_u2T"]], dma=True)

    if "F1" in phases:
        ffn_phase(x_p, x_s, norm_ffn1, w1g, w1u, w1d, f1_init, f1_sub, f1_tile)

    def phase_a1():
        TT = 512
        A = Alloc()
        Wx = A.bf(8 * 3072).rearrange("p (k n) -> p k n", k=8)
        st = A.f32(2048)
        stage = [st[:, 0:1024], st[:, 1024:2048]]
        cw = A.f32(96).rearrange("p (m t) -> p m t", m=24)
        cb = A.f32(24)
        uT = A.bf(8 * TT).rearrange("p (k t) -> p k t", k=8)
        xraw = A.bf(24 * (TT + 4)).rearrange("p (m t) -> p m t", m=24)
        xc = A.bf(24 * TT).rearrange("p (m t) -> p m t", m=24)
        ac_ = A.f32(2 * TT)
        acc = [ac_[:, 0:TT], ac_[:, TT:2 * TT]]
        tok = A.f32(3072)
        prevT = A.f32(24 * 48).rearrange("p (m c) -> p m c", m=24)
        sc_tok = A.f32(3072)
        b_w = S.buf("w")
        b_st = [S.buf("st0"), S.buf("st1")]
        load_weight(lambda k, c0, c: Wx[:, k, c0:c0 + c], w_in, 8, 2048, 3072, b_w, b_st, stage)
        b_cw = S.buf("cw")
        S.add("sp", lambda e: e.dma_start(out=cw, in_=conv_wT[:, :, :]), writes=[b_cw], dma=True)
        S.add("sp", lambda e: e.dma_start(out=cb, in_=conv_bT[:, :]), writes=[b_cw], dma=True)
        b_uT = S.buf("uT")
        b_xraw = S.buf("xraw")
        b_xc = S.buf("xc")
        b_acc = [S.buf("acc0"), S.buf("acc1")]
        b_tok = S.buf("tok")
        b_ps = new_ps()
        Uv = U.rearrange("(k p) t -> p k t", p=128)
        XCv = XC.rearrange("(m p) t -> p m t", p=128)
        S.add("dve", lambda e: e.memset(xraw[:, :, 0:3], 0.0), writes=[b_xraw])

        def tokmajor(e_uT_cols, r, dst_tok, dst_dram):
            c_lo, c_hi = e_uT_cols
            for nb in range(6):
                pb = PS[4 + nb % 2]

                def mmt(e, nb=nb, pb=pb):
                    ins = None
                    for k in range(8):
                        ins = e.matmul(pb[0:r, :], lhsT=uT[:, k, c_lo:c_hi], rhs=Wx[:, k, nb * 512:(nb + 1) * 512], start=(k == 0), stop=(k == 7))
                    return ins
                S.add("pe", mmt, reads=[b_w, b_uT], writes=[b_ps[4 + nb % 2]])
                S.add("act", (lambda e, nb=nb, pb=pb: e.copy(out=dst_tok[0:r, nb * 512:(nb + 1) * 512], in_=pb[0:r, :])),
                      reads=[b_ps[4 + nb % 2]], writes=[b_tok])
            S.add("pool", (lambda e: e.dma_start(out=dst_dram, in_=dst_tok[0:r, :])), reads=[b_tok], dma=True)

        for (is_p, t0, tt) in tiles_of(TT):
            row0 = t0 if is_p else T
            S.add("sp", (lambda e, row0=row0, tt=tt: e.dma_start(out=uT[:, :, 0:tt], in_=Uv[:, :, row0:row0 + tt])), writes=[b_uT], dma=True)
            if not is_p:
                scv = state_conv
                S.add("sp", (lambda e: e.dma_start(out=sc_tok[0:48, :], in_=scv[:, :])), writes=[b_tok], dma=True)
                for g3 in range(3):
                    pb = PS[6]

                    def trp(e, g3=g3, pb=pb):
                        ins = None
                        for j in range(8):
                            m = g3 * 8 + j
                            ins = e.transpose(out=pb[:, j * 48:(j + 1) * 48], in_=sc_tok[0:48, m * 128:(m + 1) * 128], identity=ident_f[0:48, 0:48])
                        return ins
                    S.add("pe", trp, reads=[b_tok], writes=[b_ps[6]])
                    S.add("dve", (lambda e, g3=g3, pb=pb: e.tensor_copy(out=prevT[:, g3 * 8:(g3 + 1) * 8, :], in_=pb[:, 0:384].rearrange("p (j c) -> p j c", j=8))),
                          reads=[b_ps[6]], writes=[b_xraw])
            for m in range(24):
                pb = PS[m % 2]

                def mm(e, m=m, pb=pb, tt=tt):
                    ins = None
                    for k in range(8):
                        ins = e.matmul(pb[:, 0:tt], lhsT=Wx[:, k, m * 128:(m + 1) * 128], rhs=uT[:, k, 0:tt], start=(k == 0), stop=(k == 7))
                    return ins
                S.add("pe", mm, reads=[b_w, b_uT], writes=[b_ps[m % 2]])
                ac = acc[m % 2]
                b_ac = b_acc[m % 2]
                S.add("act", (lambda e, m=m, pb=pb, tt=tt, ac=ac: e.activation(out=ac[:, 0:tt], in_=pb[:, 0:tt], func=AF.Identity, scale=cw[:, m, 3:4], bias=cb[:, m:m + 1])),
                      reads=[b_ps[m % 2], b_cw], writes=[b_ac])
                if is_p:
                    S.add("act", (lambda e, m=m, pb=pb, tt=tt: e.copy(out=xraw[:, m, 3:3 + tt], in_=pb[:, 0:tt])),
                          reads=[b_ps[m % 2]], writes=[b_xraw])
                    for j in range(1, 4):
                        S.add("dve", (lambda e, m=m, j=j, tt=tt, ac=ac: e.scalar_tensor_tensor(out=ac[:, 0:tt], in0=xraw[:, m, 3 - j:3 - j + tt], scalar=cw[:, m, 3 - j:4 - j], in1=ac[:, 0:tt], op0=ALU.mult, op1=ALU.add)),
                              reads=[b_xraw, b_ac, b_cw], writes=[b_ac])
                else:
                    for j in range(3):
                        S.add("dve", (lambda e, m=m, j=j, tt=tt, ac=ac: e.scalar_tensor_tensor(out=ac[:, 0:tt], in0=prevT[:, m, j:48:3], scalar=cw[:, m, j:j + 1], in1=ac[:, 0:tt], op0=ALU.mult, op1=ALU.add)),
                              reads=[b_xraw, b_ac, b_cw], writes=[b_ac])
                S.add("act", (lambda e, m=m, tt=tt, ac=ac: e.activation(out=xc[:, m, 0:tt], in_=ac[:, 0:tt], func=AF.Silu)),
                      reads=[b_ac], writes=[b_xc])
            S.add("pool", (lambda e, row0=row0, tt=tt: e.dma_start(out=XCv[:, :, row0:row0 + tt], in_=xc[:, :, 0:tt])), reads=[b_xc], dma=True)
            if is_p:
                S.add("dve", (lambda e, tt=tt: e.tensor_copy(out=xraw[:, :, 0:3], in_=xraw[:, :, tt:tt + 3])), reads=[b_xraw], writes=[b_xraw])
            if is_p and t0 + tt == T:
                tokmajor((tt - 3, tt), 3, tok, conv_p[:, :])
            if not is_p:
                tokmajor((0, NS), NS, tok, conv_s[:, 2, :])
                scv3 = state_conv.rearrange("(b j) c -> b j c", j=3)
                S.add("sp", (lambda e: e.dma_start(out=conv_s[:, 0:2, :], in_=scv3[:, 1:3, :])), dma=True)
        S.barrier()

    if "A1" in phases:
        phase_a1()

    def phase_b():
        TT = 512
        A = Alloc()
        Wq = A.bf(8 * 4608).rearrange("p (k n) -> p k n", k=8)
        st = A.f32(2048)
        stage = [st[:, 0:1024], st[:, 1024:2048]]
        perm = A.bf(128)
        uT = A.bf(8 * TT).rearrange("p (k t) -> p k t", k=8)
        cosT = A.f32(TT)
        sinT = A.f32(TT)
        qb_ = A.bf(2 * TT)
        qb = [qb_[:, 0:TT], qb_[:, TT:2 * TT]]
        t1_ = A.f32(2 * TT)
        t1 = [t1_[:, 0:TT], t1_[:, TT:2 * TT]]
        qf_ = A.f32(2 * TT)
        qf = [qf_[:, 0:TT], qf_[:, TT:2 * TT]]
        b_qf = [S.buf("qf0"), S.buf("qf1")]
        t2_ = A.f32(2 * TT)
        t2 = [t2_[:, 0:TT], t2_[:, TT:2 * TT]]
        qk = A.bf(24 * TT).rearrange("p (m t) -> p m t", m=24)
        vtok_ = A.bf(2 * 1536)
        vtok = [vtok_[:, 0:1536], vtok_[:, 1536:3072]]
        stg_ = A.f32(2 * 512)
        stg = [stg_[:, 0:512], stg_[:, 512:1024]]
        b_w = S.buf("w")
        b_st = [S.buf("st0"), S.buf("st1")]
        load_weight(lambda k, c0, c: Wq[:, k, c0:c0 + c], w_in, 8, 5152, 4608, b_w, b_st, stage)
        b_perm = S.buf("perm")
        S.add("sp", lambda e: e.dma_start(out=stage[0][:, 0:128], in_=perm_d[:, :]), writes=[b_st[0]], dma=True)
        S.add("dve", lambda e: e.tensor_copy(out=perm, in_=stage[0][:, 0:128]), reads=[b_st[0]], writes=[b_perm])
        b_uT = S.buf("uT")
        b_cs = S.buf("cs")
        if os.environ.get('MSCOS'):
            S.add("dve", lambda e: e.memset(cosT, 0.5), writes=[b_cs])
            S.add("dve", lambda e: e.memset(sinT, 0.5), writes=[b_cs])
        b_qb = [S.buf("qb0"), S.buf("qb1")]
        b_t1 = [S.buf("t10"), S.buf("t11")]
        b_t2 = [S.buf("t20"), S.buf("t21")]
        b_qk = S.buf("qk")
        b_vt = [S.buf("vt0"), S.buf("vt1")]
        b_stg = [S.buf("sg0"), S.buf("sg1")]
        b_ps = new_ps()
        Uv = U.rearrange("(k p) t -> p k t", p=128)
        QTv = QT.rearrange("(m p) t -> p m t", p=128)
        KTv = KT.rearrange("(m p) t -> p m t", p=128)
        stg_i = [0]

        for (is_p, t0, tt) in tiles_of(TT):
            row0 = t0 if is_p else T
            nsub = (tt + 127) // 128
            S.add("sp", (lambda e, row0=row0, tt=tt: e.dma_start(out=uT[:, :, 0:tt], in_=Uv[:, :, row0:row0 + tt])), writes=[b_uT], dma=True)
            S.add("sp", (lambda e, row0=row0, tt=tt: e.dma_start(out=cosT[:, 0:tt], in_=cos_d[:, row0:row0 + tt])), writes=[b_cs], dma=True)
            S.add("sp", (lambda e, row0=row0, tt=tt: e.dma_start(out=sinT[:, 0:tt], in_=sin_d[:, row0:row0 + tt])), writes=[b_cs], dma=True)
            for m in range(0 if not os.environ.get('NOROT') else 24, 24):
                pa = PS[m % 2]
                pb = PS[2 + m % 2]
                i2 = m % 2

                def mm(e, m=m, pa=pa, tt=tt):
                    ins = None
                    for k in range(8):
                        ins = e.matmul(pa[:, 0:tt], lhsT=Wq[:, k, m * 128:(m + 1) * 128], rhs=uT[:, k, 0:tt], start=(k == 0), stop=(k == 7))
                    return ins
                S.add("pe", mm, reads=[b_w, b_uT], writes=[b_ps[m % 2]])
                S.add("act", (lambda e, pa=pa, i2=i2, tt=tt: e.copy(out=qb[i2][:, 0:tt], in_=pa[:, 0:tt])), reads=[b_ps[m % 2]], writes=[b_qb[i2]])
                S.add("pe", (lambda e, pb=pb, i2=i2, tt=tt: e.matmul(pb[:, 0:tt], lhsT=perm, rhs=qb[i2][:, 0:tt], start=True, stop=True)),
                      reads=[b_qb[i2], b_perm], writes=[b_ps[2 + m % 2]])
                S.add("act", (lambda e, pa=pa, i2=i2, tt=tt: e.copy(out=qf[i2][:, 0:tt], in_=pa[:, 0:tt])), reads=[b_ps[m % 2]], writes=[b_qf[i2]])
                S.add("pool", (lambda e, i2=i2, tt=tt: e.tensor_tensor(out=t1[i2][:, 0:tt], in0=qf[i2][:, 0:tt], in1=(cosT if not os.environ.get('NOCOS') else qf[i2])[:, 0:tt], op=ALU.mult)),
                      reads=[b_qf[i2], b_cs], writes=[b_t1[i2]])
                S.add("dve", (lambda e, pb=pb, i2=i2, tt=tt: e.tensor_tensor(out=t2[i2][:, 0:tt], in0=pb[:, 0:tt], in1=(sinT if not os.environ.get('NOCOS') else qf[i2])[:, 0:tt], op=ALU.mult)),
                      reads=[b_ps[2 + m % 2], b_cs], writes=[b_t2[i2]])
                S.add("dve", (lambda e, m=m, i2=i2, tt=tt: e.tensor_tensor(out=qk[:, m, 0:tt], in0=t1[i2][:, 0:tt], in1=t2[i2][:, 0:tt], op=ALU.add)),
                      reads=[b_t1[i2], b_t2[i2]], writes=[b_qk])
            S.add("pool", (lambda e, row0=row0, tt=tt: e.dma_start(out=QTv[:, :, row0:row0 + tt], in_=qk[:, 0:12, 0:tt])), reads=[b_qk], dma=True)
            S.add("pool", (lambda e, row0=row0, tt=tt: e.dma_start(out=KTv[:, :, row0:row0 + tt], in_=qk[:, 12:24, 0:tt])), reads=[b_qk], dma=True)
            if not is_p:
                for c6 in range(6):
                    pt = PS[6 + c6 % 2]
                    ptv = pt[:].bitcast(BF16)[:, 0:512]

                    def trq(e, c6=c6, ptv=ptv):
                        ins = None
                        for c in range(4):
                            ins = e.transpose(out=ptv[0:NS, c * 128:(c + 1) * 128], in_=qk[:, c6 * 4 + c, 0:NS], identity=ident[:, :])
                        return ins
                    S.add("pe", trq, reads=[b_qk, b_ident], writes=[b_ps[6 + c6 % 2]])
                    si = stg_i[0] % 2
                    stg_i[0] += 1
                    S.add("dve", (lambda e, ptv=ptv, si=si: e.tensor_copy(out=stg[si][0:NS, :], in_=ptv[0:NS, :])),
                          reads=[b_ps[6 + c6 % 2]], writes=[b_stg[si]])
                    S.add("sp", (lambda e, c6=c6, si=si: e.dma_start(out=QKS[:, c6 * 512:(c6 + 1) * 512], in_=stg[si][0:NS, :])), reads=[b_stg[si]], dma=True)
            for s in range(nsub if not os.environ.get('NOV') else 0):
                r = min(128, tt - s * 128)
                tok0 = t0 + s * 128
                vt = vtok[s % 2]
                for g in range(3):
                    pv = PS[4 + g % 2]

                    def mmv(e, g=g, pv=pv, s=s, r=r):
                        ins = None
                        for k in range(8):
                            ins = e.matmul(pv[0:r, :], lhsT=uT[:, k, s * 128:s * 128 + r], rhs=Wq[:, k, 3072 + g * 512:3072 + (g + 1) * 512], start=(k == 0), stop=(k == 7))
                        return ins
                    S.add("pe", mmv, reads=[b_w, b_uT], writes=[b_ps[4 + g % 2]])
                    S.add("act", (lambda e, g=g, pv=pv, r=r, vt=vt: e.copy(out=vt[0:r, g * 512:(g + 1) * 512], in_=pv[0:r, :])),
                          reads=[b_ps[4 + g % 2]], writes=[b_vt[s % 2]])
                    inwin = ((not is_p) or (tok0 >= T - W_G[g])) and not os.environ.get('NOKV')
                    if inwin:
                        if is_p:
                            dk = kv_p[g][tok0 - (T - W_G[g]):tok0 - (T - W_G[g]) + r, 0, :]
                            dv = kv_p[g][tok0 - (T - W_G[g]):tok0 - (T - W_G[g]) + r, 1, :]
                        else:
                            dk = kv_s[g][0:r, 0, :]
                            dv = kv_s[g][0:r, 1, :]
                        si = stg_i[0] % 2
                        stg_i[0] += 1
                        S.add("act", (lambda e, pv=pv, r=r, si=si: e.copy(out=stg[si][0:r, :], in_=pv[0:r, :])),
                              reads=[b_ps[4 + g % 2]], writes=[b_stg[si]])
                        S.add("sp", (lambda e, dv=dv, r=r, si=si: e.dma_start(out=dv, in_=stg[si][0:r, :])), reads=[b_stg[si]], dma=True)
                        pt = PS[6 + g % 2]
                        ptv = pt[:].bitcast(BF16)[:, 0:512]

                        def trk(e, g=g, s=s, r=r, ptv=ptv):
                            ins = None
                            for c in range(4):
                                ins = e.transpose(out=ptv[0:r, c * 128:(c + 1) * 128], in_=qk[:, 12 + g * 4 + c, s * 128:s * 128 + r], identity=ident[:, :])
                            return ins
                        S.add("pe", trk, reads=[b_qk, b_ident], writes=[b_ps[6 + g % 2]])
                        si = stg_i[0] % 2
                        stg_i[0] += 1
                        S.add("dve", (lambda e, ptv=ptv, r=r, si=si: e.tensor_copy(out=stg[si][0:r, :], in_=ptv[0:r, :])),
                              reads=[b_ps[6 + g % 2]], writes=[b_stg[si]])
                        S.add("sp", (lambda e, dk=dk, r=r, si=si: e.dma_start(out=dk, in_=stg[si][0:r, :])), reads=[b_stg[si]], dma=True)
                S.add("pool", (lambda e, vt=vt, r=r, row0=row0, s=s: e.dma_start(out=V[row0 + s * 128:row0 + s * 128 + r, :], in_=vt[0:r, :])),
                      reads=[b_vt[s % 2]], dma=True)
        S.barrier()

    if "B" in phases:
        phase_b()

    def phase_a2():
        A = Alloc()
        Wz = A.bf(8 * 2048).rearrange("p (k n) -> p k n", k=8)
        Wdt = A.bf(8 * 32).rearrange("p (k n) -> p k n", k=8)
        Wgs = A.bf(8 * D).rearrange("p (k n) -> p k n", k=8)
        Wo = A.bf(16 * D).rearrange("p (k n) -> p k n", k=16)
        cm = A.f32(640)
        triu, strict, ones = cm[:, 0:128], cm[:, 256:384], cm[:, 512:640]
        dtb = A.f32(32)
        abc = A.f32(32)
        dsk = A.f32(32)
        gss = A.f32(2048)
        A.off_after_consts = A.off
        st_off = A.off
        st = A.f32(2048)
        stage = [st[:, 0:1024], st[:, 1024:2048]]
        b_w = S.buf("w")
        b_st = [S.buf("st0"), S.buf("st1")]
        load_weight(lambda k, c0, c: Wz[:, k, c0:c0 + c], w_in, 8, 0, 2048, b_w, b_st, stage)
        load_weight(lambda k, c0, c: Wdt[:, k, c0:c0 + c], w_in, 8, 5120, 32, b_w, b_st, stage)
        load_weight(lambda k, c0, c: Wgs[:, k, c0:c0 + c], w_in, 8, 9760, 1024, b_w, b_st, stage)
        load_weight(lambda k, c0, c: Wo[:, k, c0:c0 + c], w_o_ssm, 16, 0, 1024, b_w, b_st, stage)
        b_c = S.buf("consts")
        S.add("sp", lambda e: e.dma_start(out=cm, in_=cmask_d[:, :]), writes=[b_c], dma=True)
        load_bc(dtb, dt_bias, 32, b_c)
        load_bc(abc, a_log, 32, b_c)
        load_bc(dsk, d_skip, 32, b_c)
        load_bc(gss, norm_ssm, 2048, b_c)
        S.add("act", lambda e: e.activation(out=abc, in_=abc, func=AF.Exp), reads=[b_c], writes=[b_c])
        S.add("dve", lambda e: e.tensor_scalar(out=abc, in0=abc, scalar1=-1.0, scalar2=None, op0=ALU.mult), reads=[b_c], writes=[b_c])
        S.barrier()
        b_w = S.buf("w")
        b_c = S.buf("consts")
        A.off = st_off
        A_mark = A.off_after_consts
        A.off = A.off_after_consts
        F = []
        for i in range(2):
            f = dict(uT=A.bf(1024).rearrange("p (k t) -> p k t", k=8), xcT=A.bf(24 * 128).rearrange("p (m t) -> p m t", m=24), xtok=A.bf(2560),
                     dt=A.f32(32), dta=A.f32(32), eacs=A.f32(32), dA=A.f32(32), zs=A.bf(2048), decT=A.bf(4096), cbm=A.bf(512))
            f["B"] = {n: S.buf(n + str(i)) for n in ("uT", "xcT", "xtok", "dt", "dta", "eacs", "dA", "zs", "decT", "cbm")}
            F.append(f)
        Rm = A.f32(2048)
        wts = A.bf(4096)
        xdt = A.bf(2048)
        xw = A.bf(2048)
        stT = A.f32(2048)
        stB = A.bf(2048)
        y = A.f32(2048)
        tmp = A.f32(512)
        tmp2 = A.f32(512)
        yn = A.bf(2048)
        ynT = A.bf(2048).rearrange("p (k t) -> p k t", k=16)
        sg = A.bf(128)
        MSc = A.bf(1024).rearrange("p (k t) -> p k t", k=8)
        nm = ["Rm", "wts", "xdt", "xw", "stT", "stB", "y", "tmp", "tmp2", "yn", "ynT", "sg", "MSc", "ss"]
        B = {n: S.buf(n) for n in nm}
        b_ps = new_ps()
        ssv = CP[:, 16:17]
        Uv = U.rearrange("(k p) t -> p k t", p=128)
        XCv = XC.rearrange("(m p) t -> p m t", p=128)
        MSv = MS.rearrange("(k p) t -> p k t", p=128)
        S.add("dve", lambda e: e.memset(stT, 0.0), writes=[B["stT"]])
        S.add("dve", lambda e: e.memset(stB, 0.0), writes=[B["stB"]])
        bc3 = lambda ap, n1, n2: ap.unsqueeze(2).to_broadcast([128, n1, n2])

        def front(c):
            f = F[c % 2]
            FB = f["B"]
            uT, xcT, xtok, dt, dta, eacs, dA, zs, decT, cbm = (f[k] for k in ("uT", "xcT", "xtok", "dt", "dta", "eacs", "dA", "zs", "decT", "cbm"))
            c0 = c * 128
            S.add("sp", (lambda e: e.dma_start(out=uT, in_=Uv[:, :, c0:c0 + 128])), writes=[FB["uT"]], dma=True)
            S.add("sp", (lambda e: e.dma_start(out=xcT, in_=XCv[:, :, c0:c0 + 128])), writes=[FB["xcT"]], dma=True)
            for g3 in range(3):
                nck = 8 if g3 < 2 else 4
                ptv = PS[3][:].bitcast(BF16)[:, 0:1024]

                def tr(e, g3=g3, nck=nck, ptv=ptv):
                    ins = None
                    for j in range(nck):
                        ins = e.transpose(out=ptv[:, j * 128:(j + 1) * 128], in_=xcT[:, g3 * 8 + j, :], identity=ident[:, :])
                    return ins
                S.add("pe", tr, reads=[FB["xcT"], b_ident], writes=[b_ps[3]])
                S.add("act", (lambda e, g3=g3, nck=nck, ptv=ptv: e.copy(out=xtok[:, g3 * 1024:g3 * 1024 + nck * 128], in_=ptv[:, 0:nck * 128])), reads=[b_ps[3]], writes=[FB["xtok"]])

            def mmdt(e):
                ins = None
                for k in range(8):
                    ins = e.matmul(PS[0][:, 0:32], lhsT=uT[:, k, :], rhs=Wdt[:, k, :], start=(k == 0), stop=(k == 7))
                return ins
            S.add("pe", mmdt, reads=[b_w, FB["uT"]], writes=[b_ps[0]])
            S.add("dve", lambda e: e.tensor_tensor(out=dt, in0=PS[0][:, 0:32], in1=dtb, op=ALU.add), reads=[b_ps[0], b_c], writes=[FB["dt"]])
            S.add("act", lambda e: e.activation(out=dt, in_=dt, func=AF.Exp), reads=[FB["dt"]], writes=[FB["dt"]])
            S.add("act", lambda e: e.activation(out=dt, in_=dt, func=AF.Ln, bias=1.0), reads=[FB["dt"]], writes=[FB["dt"]])
            S.add("dve", lambda e: e.tensor_tensor(out=dta, in0=dt, in1=abc, op=ALU.mult), reads=[FB["dt"], b_c], writes=[FB["dta"]])
            S.add("pe", lambda e: e.matmul(PS[0][:, 32:64], lhsT=triu, rhs=dta, start=True, stop=True), reads=[FB["dta"], b_c], writes=[b_ps[0]])
            S.add("act", lambda e: e.activation(out=eacs, in_=PS[0][:, 32:64], func=AF.Exp), reads=[b_ps[0]], writes=[FB["eacs"]])
            S.add("pe", lambda e: e.matmul(PS[0][:, 64:96], lhsT=ones, rhs=dta, start=True, stop=True), reads=[FB["dta"], b_c], writes=[b_ps[0]])
            S.add("act", lambda e: e.activation(out=dA, in_=PS[0][:, 64:96], func=AF.Exp), reads=[b_ps[0]], writes=[FB["dA"]])
            for h2 in range(2):
                S.add("pool", (lambda e, h2=h2: e.tensor_tensor(out=Rm.rearrange("p (h l) -> p h l", h=16), in0=bc3(dta[:, h2 * 16:(h2 + 1) * 16], 16, 128), in1=triu.unsqueeze(1).to_broadcast([128, 16, 128]), op=ALU.mult)),
                      reads=[FB["dta"], b_c], writes=[B["Rm"]])
                for hq in range(4):
                    hb = h2 * 4 + hq
                    pb = PS[2]
                    S.add("pe", (lambda e, hq=hq, pb=pb: e.matmul(pb[:, :], lhsT=strict, rhs=Rm[:, hq * 512:(hq + 1) * 512], start=True, stop=True)), reads=[B["Rm"], b_c], writes=[b_ps[2]])
                    S.add("act", (lambda e, hb=hb, pb=pb: e.activation(out=decT[:, hb * 512:(hb + 1) * 512], in_=pb[:, :], func=AF.Exp)), reads=[b_ps[2]], writes=[FB["decT"]])

            def mmcb(e):
                ins = None
                for g in range(4):
                    ins = e.matmul(PS[1][:, g * 128:(g + 1) * 128], lhsT=xcT[:, 16 + g, :], rhs=xcT[:, 20 + g, :], start=True, stop=True)
                return ins
            S.add("pe", mmcb, reads=[FB["xcT"]], writes=[b_ps[1]])
            S.add("dve", lambda e: e.tensor_tensor(out=cbm.rearrange("p (g l) -> p g l", g=4), in0=PS[1][:, :].rearrange("p (g l) -> p g l", g=4), in1=triu.unsqueeze(1).to_broadcast([128, 4, 128]), op=ALU.mult),
                  reads=[b_ps[1], b_c], writes=[FB["cbm"]])
            for nb in range(4):
                pz = PS[3]

                def mmz(e, nb=nb, pz=pz):
                    ins = None
                    for k in range(8):
                        ins = e.matmul(pz[:, :], lhsT=uT[:, k, :], rhs=Wz[:, k, nb * 512:(nb + 1) * 512], start=(k == 0), stop=(k == 7))
                    return ins
                S.add("pe", mmz, reads=[b_w, FB["uT"]], writes=[b_ps[3]])
                S.add("act", (lambda e, nb=nb, pz=pz: e.activation(out=zs[:, nb * 512:(nb + 1) * 512], in_=pz[:, :], func=AF.Silu)), reads=[b_ps[3]], writes=[FB["zs"]])

        def back(c):
            f = F[c % 2]
            FB = f["B"]
            uT, xcT, xtok, dt, dta, eacs, dA, zs, decT, cbm = (f[k] for k in ("uT", "xcT", "xtok", "dt", "dta", "eacs", "dA", "zs", "decT", "cbm"))
            c0 = c * 128
            S.add("dve", lambda e: e.tensor_tensor(out=wts.rearrange("p (g h l) -> p g h l", g=4, h=8), in0=decT.rearrange("p (g h l) -> p g h l", g=4, h=8),
                                                    in1=cbm.rearrange("p (g l) -> p g l", g=4).unsqueeze(2).to_broadcast([128, 4, 8, 128]), op=ALU.mult),
                  reads=[FB["decT"], FB["cbm"]], writes=[B["wts"]])
            S.add("pool", lambda e: e.tensor_tensor(out=xdt.rearrange("p (h e) -> p h e", h=32), in0=xtok[:, 0:2048].rearrange("p (h e) -> p h e", h=32), in1=bc3(dt, 32, 64), op=ALU.mult),
                  reads=[FB["xtok"], FB["dt"]], writes=[B["xdt"]])
            S.add("pool", lambda e: e.tensor_tensor(out=xw.rearrange("p (h e) -> p h e", h=32), in0=xdt.rearrange("p (h e) -> p h e", h=32),
                                                     in1=decT.rearrange("p (h l) -> p h l", h=32)[:, :, 127:128].to_broadcast([128, 32, 64]), op=ALU.mult),
                  reads=[B["xdt"], FB["decT"]], writes=[B["xw"]])
            for g in range(4):
                pa, pb = PS[4], PS[5]

                def mmy(e, g=g, pa=pa, pb=pb):
                    ins = None
                    for hh in range(8):
                        h = g * 8 + hh
                        ins = e.matmul(pa[:, hh * 64:(hh + 1) * 64], lhsT=wts[:, h * 128:(h + 1) * 128], rhs=xdt[:, h * 64:(h + 1) * 64], start=True, stop=True)
                    ins = e.matmul(pb[:, :], lhsT=xcT[:, 20 + g, :], rhs=stB[:, g * 512:(g + 1) * 512], start=True, stop=True)
                    return ins
                S.add("pe", mmy, reads=[B["wts"], B["xdt"], FB["xcT"], B["stB"]], writes=[b_ps[4], b_ps[5]])
                S.add("dve", (lambda e, g=g, pb=pb: e.tensor_tensor(out=tmp.rearrange("p (h e) -> p h e", h=8), in0=pb[:, :].rearrange("p (h e) -> p h e", h=8), in1=bc3(eacs[:, g * 8:(g + 1) * 8], 8, 64), op=ALU.mult)),
                      reads=[b_ps[5], FB["eacs"]], writes=[B["tmp"]])
                S.add("dve", (lambda e, g=g, pa=pa: e.tensor_tensor(out=y[:, g * 512:(g + 1) * 512], in0=pa[:, :], in1=tmp, op=ALU.add)), reads=[b_ps[4], B["tmp"]], writes=[B["y"]])
                S.add("pool", (lambda e, g=g: e.tensor_tensor(out=tmp2.rearrange("p (h e) -> p h e", h=8), in0=xtok[:, g * 512:(g + 1) * 512].rearrange("p (h e) -> p h e", h=8), in1=bc3(dsk[:, g * 8:(g + 1) * 8], 8, 64), op=ALU.mult)),
                      reads=[FB["xtok"], b_c], writes=[B["tmp2"]])
                S.add("pool", (lambda e, g=g: e.tensor_tensor(out=y[:, g * 512:(g + 1) * 512], in0=y[:, g * 512:(g + 1) * 512], in1=tmp2, op=ALU.add)), reads=[B["y"], B["tmp2"]], writes=[B["y"]])
            for g in range(4):
                pc = PS[7]
                S.add("pe", (lambda e, g=g, pc=pc: e.matmul(pc[:, :], lhsT=xtok[:, 2048 + g * 128:2048 + (g + 1) * 128], rhs=xw[:, g * 512:(g + 1) * 512], start=True, stop=True)),
                      reads=[FB["xtok"], B["xw"]], writes=[b_ps[7]])
                sv = stT[:, g * 512:(g + 1) * 512]
                S.add("dve", (lambda e, g=g, sv=sv: e.tensor_tensor(out=sv.rearrange("p (h e) -> p h e", h=8), in0=sv.rearrange("p (h e) -> p h e", h=8), in1=bc3(dA[:, g * 8:(g + 1) * 8], 8, 64), op=ALU.mult)),
                      reads=[B["stT"], FB["dA"]], writes=[B["stT"]])
                S.add("dve", (lambda e, sv=sv, pc=pc: e.tensor_tensor(out=sv, in0=pc[:, :], in1=sv, op=ALU.add)), reads=[b_ps[7], B["stT"]], writes=[B["stT"]])
                S.add("act", (lambda e, g=g, sv=sv: e.copy(out=stB[:, g * 512:(g + 1) * 512], in_=sv)), reads=[B["stT"]], writes=[B["stB"]])
            S.add("dve", lambda e: e.tensor_tensor(out=y, in0=y, in1=zs, op=ALU.mult), reads=[B["y"], FB["zs"]], writes=[B["y"]])
            norm_T(y, B["y"], 128, gss, b_c, ssv, B["ss"], yn, B["yn"], yn, B["yn"], PS[6], b_ps[6], ynT, B["ynT"], 0, nk=16)
            for m in range(8):
                pa, pb = PS[4], PS[5]

                def mmo(e, m=m, pa=pa, pb=pb):
                    ins = None
                    for k in range(16):
                        ins = e.matmul(pa[:, 0:128], lhsT=Wo[:, k, m * 128:(m + 1) * 128], rhs=ynT[:, k, :], start=(k == 0), stop=(k == 15))
                    for k in range(8):
                        ins = e.matmul(pb[:, 0:128], lhsT=Wgs[:, k, m * 128:(m + 1) * 128], rhs=uT[:, k, :], start=(k == 0), stop=(k == 7))
                    return ins
                S.add("pe", mmo, reads=[b_w, B["ynT"], FB["uT"]], writes=[b_ps[4], b_ps[5]])
                S.add("act", (lambda e, pb=pb: e.activation(out=sg, in_=pb[:, 0:128], func=AF.Sigmoid)), reads=[b_ps[5]], writes=[B["sg"]])
                S.add("dve", (lambda e, m=m, pa=pa: e.tensor_tensor(out=MSc[:, m, :], in0=pa[:, 0:128], in1=sg, op=ALU.mult)), reads=[b_ps[4], B["sg"]], writes=[B["MSc"]])
            S.add("pool", (lambda e: e.dma_start(out=MSv[:, :, c0:c0 + 128], in_=MSc)), reads=[B["MSc"]], dma=True)

        NCH = T // 128
        front(0)
        for c in range(NCH):
            if c + 1 < NCH:
                front(c + 1)
            back(c)
        for j in range(16):
            pt = PS[j % 2]
            S.add("pe", (lambda e, j=j, pt=pt: e.transpose(out=pt[:, 0:128], in_=stT[:, j * 128:(j + 1) * 128], identity=ident_f[:, :])), reads=[B["stT"]], writes=[b_ps[j % 2]])
            S.add("act", (lambda e, j=j, pt=pt: e.copy(out=Rm[:, j * 128:(j + 1) * 128], in_=pt[:, 0:128])), reads=[b_ps[j % 2]], writes=[B["Rm"]])
        S.add("sp", lambda e: e.dma_start(out=ssm_p.rearrange("(j p) n -> p j n", p=128), in_=Rm[:, 0:2048].rearrange("p (j n) -> p j n", j=16)), reads=[B["Rm"]], dma=True)
        S.barrier()

        A.off = A_mark
        uTs = A.bf(128).rearrange("p (k t) -> p k t", k=8)
        xcs = A.bf(24 * NS).rearrange("p (m t) -> p m t", m=24)
        xsf = A.f32(256).rearrange("p (j b) -> p j b", j=16)
        Ex = A.f32(2048)
        cols = A.f32(4)
        dtT = A.f32(16)
        decs = A.f32(16)
        dtx = A.f32(256).rearrange("p (j b) -> p j b", j=16)
        decx = A.f32(256).rearrange("p (j b) -> p j b", j=16)
        dskx = A.f32(16)
        xdtx = A.f32(256).rearrange("p (j b) -> p j b", j=16)
        bctok = A.f32(1024)
        bcbc = [A.f32(1024), A.f32(1024)]
        stt = [A.f32(2048).rearrange("p (j n) -> p j n", j=16) for _ in range(2)]
        tA = A.f32(512).rearrange("p (j n) -> p j n", j=4)
        tB = A.f32(512).rearrange("p (j n) -> p j n", j=4)
        ycol = A.f32(256).rearrange("p (j b) -> p j b", j=16)
        zsT = A.f32(256).rearrange("p (j b) -> p j b", j=16)
        yg = A.f32(256).rearrange("p (j b) -> p j b", j=16)
        sq = A.f32(256).rearrange("p (j b) -> p j b", j=16)
        red = A.f32(16)
        rstd = A.f32(16)
        gsT = A.f32(16)
        ynTs = A.bf(256).rearrange("p (k t) -> p k t", k=16)
        sgs = A.bf(16)
        MSs = A.bf(128).rearrange("p (k t) -> p k t", k=8)
        nm2 = ["uTs", "xcs", "xsf", "Ex", "cols", "dtT", "decs", "dtx", "decx", "dskx", "xdtx", "bctok", "bc0", "bc1", "st0", "st1", "tA", "tB", "ycol", "zsT", "yg", "sq", "red", "rstd", "gsT", "ynTs", "sgs", "MSs"]
        Bs = {n: S.buf(n) for n in nm2}
        b_w2 = S.buf("w2")
        b_ps = new_ps()
        S.add("sp", lambda e: e.dma_start(out=uTs, in_=Uv[:, :, T:T + NS]), writes=[Bs["uTs"]], dma=True)
        S.add("sp", lambda e: e.dma_start(out=xcs, in_=XCv[:, :, T:T + NS]), writes=[Bs["xcs"]], dma=True)
        S.add("sp", lambda e: e.dma_start(out=Ex[0:32, :], in_=expand_d[:, :]), writes=[Bs["Ex"]], dma=True)
        S.add("sp", lambda e: e.dma_start(out=cols[0:32, 0:1], in_=dt_biasT[:, :]), writes=[Bs["cols"]], dma=True)
        S.add("sp", lambda e: e.dma_start(out=cols[0:32, 1:2], in_=a_logT[:, :]), writes=[Bs["cols"]], dma=True)
        S.add("sp", lambda e: e.dma_start(out=cols[0:32, 2:3], in_=d_skipT[:, :]), writes=[Bs["cols"]], dma=True)
        S.add("sp", lambda e: e.dma_start(out=gsT, in_=norm_ssmT[:, :]), writes=[Bs["gsT"]], dma=True)
        S.add("act", lambda e: e.activation(out=cols[0:32, 1:2], in_=cols[0:32, 1:2], func=AF.Exp), reads=[Bs["cols"]], writes=[Bs["cols"]])
        S.add("dve", lambda e: e.tensor_scalar(out=cols[0:32, 1:2], in0=cols[0:32, 1:2], scalar1=-1.0, scalar2=None, op0=ALU.mult), reads=[Bs["cols"]], writes=[Bs["cols"]])

        def mmdts(e):
            ins = None
            for k in range(8):
                ins = e.matmul(PS[0][0:32, 0:NS], lhsT=Wdt[:, k, :], rhs=uTs[:, k, :], start=(k == 0), stop=(k == 7))
            return ins
        S.add("pe", mmdts, reads=[b_w2, Bs["uTs"]], writes=[b_ps[0]])
        S.add("act", lambda e: e.activation(out=dtT[0:32, :], in_=PS[0][0:32, 0:NS], func=AF.Exp, bias=cols[0:32, 0:1]), reads=[b_ps[0], Bs["cols"]], writes=[Bs["dtT"]])
        S.add("act", lambda e: e.activation(out=dtT[0:32, :], in_=dtT[0:32, :], func=AF.Ln, bias=1.0), reads=[Bs["dtT"]], writes=[Bs["dtT"]])
        S.add("act", lambda e: e.activation(out=decs[0:32, :], in_=dtT[0:32, :], func=AF.Exp, scale=cols[0:32, 1:2]), reads=[Bs["dtT"], Bs["cols"]], writes=[Bs["decs"]])

        def mmex(e):
            ins = None
            for j in range(16):
                ins = e.matmul(PS[1][:, j * NS:(j + 1) * NS], lhsT=Ex[0:32, j * 128:(j + 1) * 128], rhs=dtT[0:32, :], start=True, stop=True)
                ins = e.matmul(PS[1][:, 256 + j * NS:256 + (j + 1) * NS], lhsT=Ex[0:32, j * 128:(j + 1) * 128], rhs=decs[0:32, :], start=True, stop=True)
                ins = e.matmul(PS[2][:, j:j + 1], lhsT=Ex[0:32, j * 128:(j + 1) * 128], rhs=cols[0:32, 2:3], start=True, stop=True)
            return ins
        S.add("pe", mmex, reads=[Bs["Ex"], Bs["dtT"], Bs["decs"], Bs["cols"]], writes=[b_ps[1], b_ps[2]])
        S.add("dve", lambda e: e.tensor_copy(out=dtx, in_=PS[1][:, 0:256].rearrange("p (j b) -> p j b", j=16)), reads=[b_ps[1]], writes=[Bs["dtx"]])
        S.add("dve", lambda e: e.tensor_copy(out=decx, in_=PS[1][:, 256:512].rearrange("p (j b) -> p j b", j=16)), reads=[b_ps[1]], writes=[Bs["decx"]])
        S.add("dve", lambda e: e.tensor_copy(out=dskx, in_=PS[2][:, 0:16]), reads=[b_ps[2]], writes=[Bs["dskx"]])
        S.add("dve", lambda e: e.tensor_copy(out=xsf, in_=xcs[:, 0:16, :]), reads=[Bs["xcs"]], writes=[Bs["xsf"]])
        S.add("dve", lambda e: e.tensor_tensor(out=xdtx, in0=xsf, in1=dtx, op=ALU.mult), reads=[Bs["xsf"], Bs["dtx"]], writes=[Bs["xdtx"]])
        for half in range(2):
            ptv = PS[6 + half][:].bitcast(BF16)[:, 0:512]

            def trbc(e, half=half, ptv=ptv):
                ins = None
                for i in range(4):
                    ins = e.transpose(out=ptv[0:NS, i * 128:(i + 1) * 128], in_=xcs[:, 16 + half * 4 + i, :], identity=ident[:, :])
                return ins
            S.add("pe", trbc, reads=[Bs["xcs"], b_ident], writes=[b_ps[6 + half]])
            S.add("dve", (lambda e, half=half, ptv=ptv: e.tensor_copy(out=bctok[0:NS, half * 512:(half + 1) * 512], in_=ptv[0:NS, :])), reads=[b_ps[6 + half]], writes=[Bs["bctok"]])
        S.add("sp", lambda e: e.dma_start(out=BCS[:, :], in_=bctok[0:NS, :]), reads=[Bs["bctok"]], writes=[Bs["bctok"]], dma=True)
        ssm_sv = ssm_s.rearrange("b (j p) n -> b p j n", p=128)
        ssm_iv = state_ssm.rearrange("b (j p) n -> b p j n", p=128)
        for b in range(NS):
            i2 = b % 2
            bcb, stb = bcbc[i2], stt[i2]
            S.add("pool", (lambda e, b=b, bcb=bcb: e.dma_start(out=bcb, in_=BCS[b:b + 1, :].to_broadcast([128, 1024]))), reads=[Bs["bctok"]], writes=[Bs[f"bc{i2}"]], dma=True)
            S.add("sp", (lambda e, b=b, stb=stb: e.dma_start(out=stb, in_=ssm_iv[b])), writes=[Bs[f"st{i2}"]], dma=True)
            for g in range(4):
                js = slice(4 * g, 4 * g + 4)
                Bg = bcb[:, g * 128:(g + 1) * 128].unsqueeze(1).to_broadcast([128, 4, 128])
                Cg = bcb[:, 512 + g * 128:512 + (g + 1) * 128].unsqueeze(1).to_broadcast([128, 4, 128])
                S.add("dve", (lambda e, b=b, js=js, Bg=Bg: e.tensor_tensor(out=tA, in0=xdtx[:, js, b:b + 1].to_broadcast([128, 4, 128]), in1=Bg, op=ALU.mult)),
                      reads=[Bs["xdtx"], Bs[f"bc{i2}"]], writes=[Bs["tA"]])
                S.add("dve", (lambda e, b=b, js=js, stb=stb: e.tensor_tensor(out=stb[:, js, :], in0=stb[:, js, :], in1=decx[:, js, b:b + 1].to_broadcast([128, 4, 128]), op=ALU.mult)),
                      reads=[Bs[f"st{i2}"], Bs["decx"]], writes=[Bs[f"st{i2}"]])
                S.add("dve", (lambda e, js=js, stb=stb: e.tensor_tensor(out=stb[:, js, :], in0=stb[:, js, :], in1=tA, op=ALU.add)),
                      reads=[Bs[f"st{i2}"], Bs["tA"]], writes=[Bs[f"st{i2}"]])
                S.add("pool", (lambda e, js=js, stb=stb, Cg=Cg: e.tensor_tensor(out=tB, in0=stb[:, js, :], in1=Cg, op=ALU.mult)),
                      reads=[Bs[f"st{i2}"], Bs[f"bc{i2}"]], writes=[Bs["tB"]])
                S.add("dve", (lambda e, b=b, js=js: e.tensor_reduce(out=ycol[:, js, b], in_=tB, axis=AX.X, op=ALU.add)), reads=[Bs["tB"]], writes=[Bs["ycol"]])
            S.add("pool", (lambda e, b=b, stb=stb: e.dma_start(out=ssm_sv[b], in_=stb)), reads=[Bs[f"st{i2}"]], dma=True)
        S.add("dve", lambda e: e.tensor_tensor(out=yg, in0=xsf, in1=dskx.unsqueeze(2).to_broadcast([128, 16, NS]), op=ALU.mult), reads=[Bs["xsf"], Bs["dskx"]], writes=[Bs["yg"]])
        S.add("dve", lambda e: e.tensor_tensor(out=yg, in0=yg, in1=ycol, op=ALU.add), reads=[Bs["yg"], Bs["ycol"]], writes=[Bs["yg"]])
        for j in range(16):
            def mmzs(e, j=j):
                ins = None
                for k in range(8):
                    ins = e.matmul(PS[3][:, j * NS:(j + 1) * NS], lhsT=Wz[:, k, j * 128:(j + 1) * 128], rhs=uTs[:, k, :], start=(k == 0), stop=(k == 7))
                return ins
            S.add("pe", mmzs, reads=[b_w2, Bs["uTs"]], writes=[b_ps[3]])
        S.add("act", lambda e: e.activation(out=zsT, in_=PS[3][:, 0:256].rearrange("p (j b) -> p j b", j=16), func=AF.Silu), reads=[b_ps[3]], writes=[Bs["zsT"]])
        S.add("dve", lambda e: e.tensor_tensor(out=yg, in0=yg, in1=zsT, op=ALU.mult), reads=[Bs["yg"], Bs["zsT"]], writes=[Bs["yg"]])
        S.add("dve", lambda e: e.tensor_tensor(out=sq, in0=yg, in1=yg, op=ALU.mult), reads=[Bs["yg"]], writes=[Bs["sq"]])
        S.add("dve", lambda e: e.tensor_reduce(out=red, in_=sq.rearrange("p j b -> p b j"), axis=AX.X, op=ALU.add), reads=[Bs["sq"]], writes=[Bs["red"]])
        S.add("pe", lambda e: e.matmul(PS[4][:, 0:NS], lhsT=ones, rhs=red, start=True, stop=True), reads=[Bs["red"]], writes=[b_ps[4]])
        S.add("dve", lambda e: e.tensor_scalar(out=rstd, in0=PS[4][:, 0:NS], scalar1=1.0 / 2048, scalar2=EPS, op0=ALU.mult, op1=ALU.add), reads=[b_ps[4]], writes=[Bs["rstd"]])
        S.add("pool", lambda e: e.tensor_tensor(out=rstd, in0=rstd, in1=mhalf.to_broadcast([128, NS]), op=ALU.pow), reads=[Bs["rstd"]], writes=[Bs["rstd"]])
        S.add("dve", lambda e: e.tensor_tensor(out=yg, in0=yg, in1=rstd.unsqueeze(1).to_broadcast([128, 16, NS]), op=ALU.mult), reads=[Bs["yg"], Bs["rstd"]], writes=[Bs["yg"]])
        S.add("dve", lambda e: e.tensor_tensor(out=ynTs, in0=yg, in1=gsT.unsqueeze(2).to_broadcast([128, 16, NS]), op=ALU.mult), reads=[Bs["yg"], Bs["gsT"]], writes=[Bs["ynTs"]])
        for m in range(8):
            pa, pb = PS[m % 2], PS[2 + m % 2]

            def mmos(e, m=m, pa=pa, pb=pb):
                ins = None
                for k in range(16):
                    ins = e.matmul(pa[:, 0:NS], lhsT=Wo[:, k, m * 128:(m + 1) * 128], rhs=ynTs[:, k, :], start=(k == 0), stop=(k == 15))
                for k in range(8):
                    ins = e.matmul(pb[:, 0:NS], lhsT=Wgs[:, k, m * 128:(m + 1) * 128], rhs=uTs[:, k, :], start=(k == 0), stop=(k == 7))
                return ins
            S.add("pe", mmos, reads=[b_w2, Bs["ynTs"], Bs["uTs"]], writes=[b_ps[m % 2], b_ps[2 + m % 2]])
            S.add("act", (lambda e, pb=pb: e.activation(out=sgs, in_=pb[:, 0:NS], func=AF.Sigmoid)), reads=[b_ps[2 + m % 2]], writes=[Bs["sgs"]])
            S.add("dve", (lambda e, m=m, pa=pa: e.tensor_tensor(out=MSs[:, m, :], in0=pa[:, 0:NS], in1=sgs, op=ALU.mult)), reads=[b_ps[m % 2], Bs["sgs"]], writes=[Bs["MSs"]])
        S.add("pool", lambda e: e.dma_start(out=MSv[:, :, T:T + NS], in_=MSs), reads=[Bs["MSs"]], dma=True)
        S.barrier()

    if "A2" in phases:
        phase_a2()

    def phase_c():
        A = Alloc()
        WMAX = 2048
        QTs = A.bf(4 * WMAX)
        KTs = [A.bf(4 * WMAX), A.bf(4 * WMAX)]
        Vb = [A.bf(16 * 520), A.bf(16 * 520)]
        Pt = [A.bf(2048), A.bf(2048)]
        maskb = A.bf(2048)
        mstage = A.f32(256)
        Osb = [A.f32(520), A.f32(520)]
        b_q = S.buf("q")
        b_k = [S.buf("k0"), S.buf("k1")]
        b_v = [S.buf("v0"), S.buf("v1")]
        b_pt = [S.buf("pt0"), S.buf("pt1")]
        b_mask = S.buf("mask")
        b_ms = S.buf("mstage")
        b_o = [S.buf("o0"), S.buf("o1")]
        b_ps = new_ps()
        S.add("sp", lambda e: e.dma_start(out=mstage[:, 0:128], in_=cmask_d[:, 128:256]), writes=[b_ms], dma=True)
        S.add("sp", lambda e: e.dma_start(out=mstage[:, 128:256], in_=cmask_d[:, 0:128]), writes=[b_ms], dma=True)
        mv = maskb.rearrange("p (h c) -> p h c", h=8)
        for h in range(8):
            S.add("dve", (lambda e, h=h: e.tensor_copy(out=mv[:, h, :], in_=mstage[:, :])), reads=[b_ms], writes=[b_mask])
        for i in range(2):
            vv = Vb[i].rearrange("p (r h e) -> p r h e", r=16, h=8)
            S.add("dve", (lambda e, vv=vv: e.memset(vv[:, :, :, 64:65], 1.0)), writes=[b_v[i]])
        QTv = QT.rearrange("(m p) t -> p m t", p=128)
        KTv = KT.rearrange("(m p) t -> p m t", p=128)
        blk = [0]
        for g in [int(c) for c in os.environ.get('CG', '012')]:
            W = W_G[g]
            d = DIL[g]
            qs = QTs[:, 0:4 * W].rearrange("p (c t) -> p c t", c=4)
            ks = [KTs[i][:, 0:4 * W].rearrange("p (c t) -> p c t", c=4) for i in range(2)]
            for nb in range(T // W):
                cur = nb % 2
                prv = 1 - cur
                t0 = nb * W
                S.add("sp", (lambda e, qs=qs, g=g, t0=t0, W=W: e.dma_start(out=qs, in_=QTv[:, 4 * g:4 * g + 4, t0:t0 + W])), writes=[b_q], dma=True)
                S.add("sp", (lambda e, kc=ks[cur], g=g, t0=t0, W=W: e.dma_start(out=kc, in_=KTv[:, 4 * g:4 * g + 4, t0:t0 + W])), writes=[b_k[cur]], dma=True)
                vcur = Vb[cur].rearrange("p (r h e) -> p r h e", r=16, h=8)
                vprv = Vb[prv].rearrange("p (r h e) -> p r h e", r=16, h=8)
                for r in range(d):
                    src = V[t0 + r:t0 + W:d, g * 512:(g + 1) * 512].rearrange("m (h e) -> m h e", h=8)
                    S.add("pool", (lambda e, vcur=vcur, r=r, src=src: e.dma_start(out=vcur[:, r, :, 0:64], in_=src)), writes=[b_v[cur]], dma=True)
                for r in range(d):
                    bi = blk[0] % 2
                    blk[0] += 1
                    ptv = Pt[bi].rearrange("p (h c) -> p h c", h=8)
                    halves = (0, 1) if nb > 0 else (1,)

                    def scores(e, qs=qs, kcur=ks[cur], kprv=ks[prv], r=r, d=d, W=W, halves=halves):
                        ins = None
                        for h in range(8):
                            c, pr = h // 2, (h % 2) * 64
                            for hv in halves:
                                kk = kcur if hv == 1 else kprv
                                bkk = (h % 2) * 2 + h // 4
                                oo = ((h // 2) % 2) * 256 + hv * 128
                                ins = e.matmul(PS[bkk][:, oo:oo + 128],
                                               lhsT=kk[pr:pr + 64, c, r:W:d], rhs=qs[pr:pr + 64, c, r:W:d], start=True, stop=True)
                        return ins
                    S.add("pe", scores, reads=[b_q, b_k[0], b_k[1]], writes=[b_ps[0], b_ps[1], b_ps[2], b_ps[3]])
                    if os.environ.get('CSTOP') == '0':
                        continue
                    for bk in range(4):
                        if nb > 0:
                            S.add("act", (lambda e, bk=bk, bi=bi: e.activation(out=Pt[bi][:, bk * 512:(bk + 1) * 512], in_=PS[bk][:, :], func=AF.Exp, scale=0.125)),
                                  reads=[b_ps[bk]], writes=[b_pt[bi]])
                        else:
                            pin = PS[bk][:, :].rearrange("p (h c) -> p h c", h=2)[:, :, 128:256]
                            pout = Pt[bi][:, bk * 512:(bk + 1) * 512].rearrange("p (h c) -> p h c", h=2)[:, :, 128:256]
                            S.add("act", (lambda e, pin=pin, pout=pout: e.activation(out=pout, in_=pin, func=AF.Exp, scale=0.125)),
                                  reads=[b_ps[bk]], writes=[b_pt[bi]])
                    S.add("dve", (lambda e, bi=bi: e.tensor_tensor(out=Pt[bi], in0=Pt[bi], in1=maskb, op=ALU.mult)),
                          reads=[b_pt[bi], b_mask], writes=[b_pt[bi]])
                    ob = 4 + 2 * bi
                    if os.environ.get('CSTOP') == '1':
                        continue

                    def pv(e, ptv=ptv, vcur=vcur, vprv=vprv, r=r, ob=ob, halves=halves):
                        ins = None
                        for h in range(8):
                            dst = PS[ob + h // 4][:, (h % 4) * 65:(h % 4) * 65 + 65]
                            for i, hv in enumerate(halves):
                                vv = vcur if hv == 1 else vprv
                                sl = ((h % 2) * 2 + h // 4) * 2 + (h // 2) % 2
                                ins = e.matmul(dst, lhsT=ptv[:, sl, hv * 128:hv * 128 + 128], rhs=vv[:, r, h, :], start=(i == 0), stop=(i == len(halves) - 1))
                        return ins
                    S.add("pe", pv, reads=[b_pt[bi], b_v[0], b_v[1]], writes=[b_ps[ob], b_ps[ob + 1]])
                    S.add("act", (lambda e, bi=bi, ob=ob: e.copy(out=Osb[bi][:, 0:260], in_=PS[ob][:, 0:260])), reads=[b_ps[ob]], writes=[b_o[bi]])
                    S.add("dve", (lambda e, bi=bi, ob=ob: e.tensor_copy(out=Osb[bi][:, 260:520], in_=PS[ob + 1][:, 0:260])), reads=[b_ps[ob + 1]], writes=[b_o[bi]])
                    dst = AO[g][t0 + r:t0 + W:d, :]
                    S.add("sp", (lambda e, bi=bi, dst=dst: e.dma_start(out=dst, in_=Osb[bi][:, :])), reads=[b_o[bi]], dma=True)
        S.barrier()

    def phase_cs():
        A = Alloc()
        kvt = [A.f32(1024), A.f32(1024)]
        qbc = [A.f32(512), A.f32(512)]
        prod = A.f32(512)
        pvt = A.f32(520)
        sc = A.f32(8)
        selfk = A.f32(512)
        selfv = A.bf(512)
        sprod = A.f32(512)
        spv = A.f32(520)
        ssc = A.f32(8)
        ones = A.f32(1)
        osb = A.f32(520)
        b_kv = [S.buf("kv0"), S.buf("kv1")]
        b_qb = [S.buf("qb0"), S.buf("qb1")]
        b_prod = S.buf("prod")
        b_pv = S.buf("pv")
        b_sc = S.buf("sc")
        b_self = S.buf("self")
        b_sp = S.buf("sprod")
        b_spv = S.buf("spv")
        b_ssc = S.buf("ssc")
        b_ones = S.buf("ones")
        b_osb = S.buf("osb")
        b_ps = new_ps()
        S.add("dve", lambda e: e.memset(ones, 1.0), writes=[b_ones])
        it = 0
        for b in range(NS):
            for g in range(3):
                W, d = W_G[g], DIL[g]
                i2 = it % 2
                it += 1
                S.add("sp", (lambda e, i2=i2, g=g, b=b, W=W, d=d: e.dma_start(out=kvt[i2], in_=cache_kv[g][b, 0:W:d, :])), writes=[b_kv[i2]], dma=True)
                S.add("pool", (lambda e, i2=i2, g=g, b=b: e.dma_start(out=qbc[i2], in_=QKS[b:b + 1, g * 512:(g + 1) * 512].to_broadcast([128, 512]))), writes=[b_qb[i2]], dma=True)
                S.add("sp", (lambda e, g=g, b=b: e.dma_start(out=selfk[0:1, :], in_=QKS[b:b + 1, 1536 + g * 512:1536 + (g + 1) * 512])), writes=[b_self], dma=True)
                S.add("sp", (lambda e, g=g, b=b: e.dma_start(out=selfv[0:1, :], in_=V[T + b:T + b + 1, g * 512:(g + 1) * 512])), writes=[b_self], dma=True)
                S.add("dve", (lambda e, i2=i2: e.tensor_tensor(out=prod, in0=kvt[i2][:, 0:512], in1=qbc[i2], op=ALU.mult)), reads=[b_kv[i2], b_qb[i2]], writes=[b_prod])
                S.add("dve", (lambda e: e.tensor_reduce(out=sc, in_=prod.rearrange("p (h e) -> p h e", h=8), axis=AX.X, op=ALU.add)), reads=[b_prod], writes=[b_sc])
                S.add("act", (lambda e: e.activation(out=pvt[:, 512:520], in_=sc, func=AF.Exp, scale=0.125)), reads=[b_sc], writes=[b_pv])
                S.add("dve", (lambda e, i2=i2: e.tensor_tensor(out=pvt[:, 0:512].rearrange("p (h e) -> p h e", h=8), in0=kvt[i2][:, 512:1024].rearrange("p (h e) -> p h e", h=8),
                                                                in1=pvt[:, 512:520].unsqueeze(2).to_broadcast([128, 8, 64]), op=ALU.mult)), reads=[b_kv[i2], b_pv], writes=[b_pv])
                S.add("dve", (lambda e, i2=i2: e.tensor_tensor(out=sprod[0:1, :], in0=selfk[0:1, :], in1=qbc[i2][0:1, :], op=ALU.mult)), reads=[b_self, b_qb[i2]], writes=[b_sp])
                S.add("dve", (lambda e: e.tensor_reduce(out=ssc[0:1, :], in_=sprod[0:1, :].rearrange("p (h e) -> p h e", h=8), axis=AX.X, op=ALU.add)), reads=[b_sp], writes=[b_ssc])
                S.add("act", (lambda e: e.activation(out=spv[0:1, 512:520], in_=ssc[0:1, :], func=AF.Exp, scale=0.125)), reads=[b_ssc], writes=[b_spv])
                S.add("dve", (lambda e: e.tensor_tensor(out=spv[0:1, 0:512].rearrange("p (h e) -> p h e", h=8), in0=selfv[0:1, :].rearrange("p (h e) -> p h e", h=8),
                                                        in1=spv[0:1, 512:520].unsqueeze(2).to_broadcast([1, 8, 64]), op=ALU.mult)), reads=[b_self, b_spv], writes=[b_spv])

                def red(e, g=g):
                    e.matmul(PS[0][0:1, 0:512], lhsT=ones[:, 0:1], rhs=pvt[:, 0:512], start=(g == 0), stop=False)
                    e.matmul(PS[1][0:1, 0:8], lhsT=ones[:, 0:1], rhs=pvt[:, 512:520], start=(g == 0), stop=False)
                    e.matmul(PS[0][0:1, 0:512], lhsT=ones[0:1, 0:1], rhs=spv[0:1, 0:512], start=False, stop=(g == 2))
                    return e.matmul(PS[1][0:1, 0:8], lhsT=ones[0:1, 0:1], rhs=spv[0:1, 512:520], start=False, stop=(g == 2))
                S.add("pe", red, reads=[b_pv, b_spv, b_ones], writes=[b_ps[0], b_ps[1]])
            ov = osb[0:1, :].rearrange("p (h e) -> p h e", e=65)
            S.add("act", (lambda e, ov=ov: e.copy(out=ov[:, :, 0:64], in_=PS[0][0:1, 0:512].rearrange("p (h e) -> p h e", e=64))), reads=[b_ps[0]], writes=[b_osb])
            S.add("act", (lambda e, ov=ov: e.copy(out=ov[:, :, 64:65], in_=PS[1][0:1, 0:8].unsqueeze(2))), reads=[b_ps[1]], writes=[b_osb])
            S.add("pool", (lambda e, b=b: e.dma_start(out=AOS[b:b + 1, :], in_=osb[0:1, :])), reads=[b_osb], dma=True)
        S.barrier()

    if "C" in phases:
        if not os.environ.get('NOC'):
            phase_c()
        if not os.environ.get('NOCS'):
            phase_cs()

    def phase_d1():
        TT = 512
        A = Alloc()
        Wga = A.bf(8 * D).rearrange("p (k n) -> p k n", k=8)
        Woa = A.bf(4 * D).rearrange("p (k n) -> p k n", k=4)
        Wout = A.bf(8 * D).rearrange("p (k n) -> p k n", k=8)
        st = A.f32(2048)
        stage = [st[:, 0:1024], st[:, 1024:2048]]
        uT = A.bf(8 * TT).rearrange("p (k t) -> p k t", k=8)
        MSt = A.bf(8 * TT).rearrange("p (k t) -> p k t", k=8)
        attnT = A.bf(4 * TT).rearrange("p (k t) -> p k t", k=4)
        merged = A.bf(8 * TT).rearrange("p (k t) -> p k t", k=8)
        ao = [A.f32(520) for _ in range(3)]
        asum = A.f32(520)
        rl = A.f32(8)
        attn_b = A.bf(512)
        sgm = [A.bf(TT), A.bf(TT)]
        tmp = [A.f32(TT), A.f32(TT)]
        h1 = [A.f32(1024), A.f32(1024)]
        h2 = [A.f32(1024), A.f32(1024)]
        b_w = S.buf("w")
        b_st = [S.buf("st0"), S.buf("st1")]
        load_weight(lambda k, c0, c: Wga[:, k, c0:c0 + c], w_in, 8, 10784, 1024, b_w, b_st, stage)
        load_weight(lambda k, c0, c: Woa[:, k, c0:c0 + c], w_o_attn, 4, 0, 1024, b_w, b_st, stage)
        load_weight(lambda k, c0, c: Wout[:, k, c0:c0 + c], w_out, 8, 0, 1024, b_w, b_st, stage)
        b_uT, b_ms, b_at, b_mg = S.buf("uT"), S.buf("ms"), S.buf("attnT"), S.buf("merged")
        b_ao = [S.buf(f"ao{i}") for i in range(3)]
        b_as, b_rl, b_ab = S.buf("asum"), S.buf("rl"), S.buf("attn_b")
        b_sg = [S.buf("sg0"), S.buf("sg1")]
        b_tmp = [S.buf("tmp0"), S.buf("tmp1")]
        b_h1 = [S.buf("h10"), S.buf("h11")]
        b_h2 = [S.buf("h20"), S.buf("h21")]
        b_ps = new_ps()
        Uv = U.rearrange("(k p) t -> p k t", p=128)
        MSv = MS.rearrange("(k p) t -> p k t", p=128)
        for (is_p, t0, tt) in tiles_of(TT):
            row0 = t0 if is_p else T
            nsub = (tt + 127) // 128
            S.add("sp", (lambda e, row0=row0, tt=tt: e.dma_start(out=uT[:, :, 0:tt], in_=Uv[:, :, row0:row0 + tt])), writes=[b_uT], dma=True)
            S.add("sp", (lambda e, row0=row0, tt=tt: e.dma_start(out=MSt[:, :, 0:tt], in_=MSv[:, :, row0:row0 + tt])), writes=[b_ms], dma=True)
            for s in range(nsub):
                r = min(128, tt - s * 128)
                tok0 = t0 + s * 128
                if is_p:
                    for g in range(3):
                        S.add("pool", (lambda e, g=g, tok0=tok0, r=r: e.dma_start(out=ao[g][0:r, :], in_=AO[g][tok0:tok0 + r, :])), writes=[b_ao[g]], dma=True)
                    S.add("pool", (lambda e, r=r: e.tensor_tensor(out=asum[0:r, :], in0=ao[0][0:r, :], in1=ao[1][0:r, :], op=ALU.add)), reads=[b_ao[0], b_ao[1]], writes=[b_as])
                    S.add("pool", (lambda e, r=r: e.tensor_tensor(out=asum[0:r, :], in0=asum[0:r, :], in1=ao[2][0:r, :], op=ALU.add)), reads=[b_as, b_ao[2]], writes=[b_as])
                else:
                    S.add("pool", (lambda e, r=r: e.dma_start(out=asum[0:r, :], in_=AOS[0:r, :])), writes=[b_as], dma=True)
                av = asum.rearrange("p (h e) -> p h e", e=65)
                S.add("dve", (lambda e, r=r, av=av: e.reciprocal(out=rl[0:r, :], in_=av[0:r, :, 64])), reads=[b_as], writes=[b_rl])
                S.add("dve", (lambda e, r=r, av=av: e.tensor_tensor(out=attn_b[0:r, :].rearrange("p (h e) -> p h e", h=8), in0=av[0:r, :, 0:64],
                                                                    in1=rl[0:r, :].unsqueeze(2).to_broadcast([r, 8, 64]), op=ALU.mult)), reads=[b_as, b_rl], writes=[b_ab])
                trans_T(attn_b, b_ab, r, PS[6 + s % 2], b_ps[6 + s % 2], attnT, b_at, s * 128, 4)
            for m in range(8):
                pa, pb = PS[m % 2], PS[2 + m % 2]
                i2 = m % 2

                def mm(e, m=m, pa=pa, pb=pb, tt=tt):
                    ins = None
                    for k in range(4):
                        ins = e.matmul(pa[:, 0:tt], lhsT=Woa[:, k, m * 128:(m + 1) * 128], rhs=attnT[:, k, 0:tt], start=(k == 0), stop=(k == 3))
                    for k in range(8):
                        ins = e.matmul(pb[:, 0:tt], lhsT=Wga[:, k, m * 128:(m + 1) * 128], rhs=uT[:, k, 0:tt], start=(k == 0), stop=(k == 7))
                    return ins
                S.add("pe", mm, reads=[b_w, b_at, b_uT], writes=[b_ps[m % 2], b_ps[2 + m % 2]])
                S.add("act", (lambda e, pb=pb, i2=i2, tt=tt: e.activation(out=sgm[i2][:, 0:tt], in_=pb[:, 0:tt], func=AF.Sigmoid)), reads=[b_ps[2 + m % 2]], writes=[b_sg[i2]])
                S.add("dve", (lambda e, pa=pa, i2=i2, tt=tt: e.tensor_tensor(out=tmp[i2][:, 0:tt], in0=pa[:, 0:tt], in1=sgm[i2][:, 0:tt], op=ALU.mult)),
                      reads=[b_ps[m % 2], b_sg[i2]], writes=[b_tmp[i2]])
                S.add("pool", (lambda e, m=m, i2=i2, tt=tt: e.tensor_tensor(out=merged[:, m, 0:tt], in0=tmp[i2][:, 0:tt], in1=MSt[:, m, 0:tt], op=ALU.add)),
                      reads=[b_tmp[i2], b_ms], writes=[b_mg])
            for s in range(nsub):
                r = min(128, tt - s * 128)
                row = row0 + s * 128
                i2 = s % 2
                S.add("sp", (lambda e, i2=i2, row=row, r=r: e.dma_start(out=h1[i2][0:r, :], in_=H1[row:row + r, :])), writes=[b_h1[i2]], dma=True)
                for nh in range(2):
                    pd = PS[4 + nh]

                    def mmo(e, s=s, r=r, nh=nh, pd=pd):
                        ins = None
                        for k in range(8):
                            ins = e.matmul(pd[0:r, :], lhsT=merged[:, k, s * 128:s * 128 + r], rhs=Wout[:, k, nh * 512:(nh + 1) * 512], start=(k == 0), stop=(k == 7))
                        return ins
                    S.add("pe", mmo, reads=[b_w, b_mg], writes=[b_ps[4 + nh]])
                    S.add("dve", (lambda e, i2=i2, r=r, nh=nh, pd=pd: e.tensor_tensor(out=h2[i2][0:r, nh * 512:(nh + 1) * 512], in0=pd[0:r, :], in1=h1[i2][0:r, nh * 512:(nh + 1) * 512], op=ALU.add)),
                          reads=[b_ps[4 + nh], b_h1[i2]], writes=[b_h2[i2]])
                S.add("pool", (lambda e, i2=i2, row=row, r=r: e.dma_start(out=H2[row:row + r, :], in_=h2[i2][0:r, :])), reads=[b_h2[i2]], dma=True)
        S.barrier()

    if "D1" in phases:
        phase_d1()

    def d2_sub(env, s, r, row, h, b_h):
        S.add("pool", (lambda e: e.dma_start(out=H3[row:row + r, :], in_=h[0:r, :])), reads=[b_h], dma=True)

    if "D2" in phases:
        ffn_phase(H2[0:T, :], H2[T:TA, :], norm_ffn2, w2g, w2u, w2d, lambda env: None, d2_sub, lambda env, row0, tt: None)

    def phase_d3():
        A = Alloc()
        Wpg = A.bf(8 * D).rearrange("p (k n) -> p k n", k=8)
        Wpp = A.bf(2 * D).rearrange("p (k n) -> p k n", k=2)
        st = A.f32(2048)
        stage = [st[:, 0:1024], st[:, 1024:2048]]
        gple = A.f32(1024)
        gfin = A.f32(1024)
        h3 = [A.f32(1024), A.f32(1024)]
        h4 = [A.f32(1024), A.f32(1024)]
        yo = [A.f32(1024), A.f32(1024)]
        pf = [A.f32(256), A.f32(256)]
        pb16 = [A.bf(256), A.bf(256)]
        xn = [A.bf(1024), A.bf(1024)]
        u3T = [A.bf(1024).rearrange("p (k t) -> p k t", k=8) for _ in range(2)]
        pT = [A.bf(256).rearrange("p (k t) -> p k t", k=2) for _ in range(2)]
        sg = [A.f32(512), A.f32(512)]
        tm = [A.f32(512), A.f32(512)]
        junk = A.bf(1024)
        b_w = S.buf("w")
        b_st = [S.buf("st0"), S.buf("st1")]
        load_weight(lambda k, c0, c: Wpg[:, k, c0:c0 + c], w_ple_gate, 8, 0, 1024, b_w, b_st, stage)
        load_weight(lambda k, c0, c: Wpp[:, k, c0:c0 + c], w_ple_proj, 2, 0, 1024, b_w, b_st, stage)
        b_g = S.buf("g")
        load_bc(gple, norm_ple, D, b_g)
        load_bc(gfin, norm_final, D, b_g)
        b_h3 = [S.buf("h30"), S.buf("h31")]
        b_h4 = [S.buf("h40"), S.buf("h41")]
        b_yo = [S.buf("yo0"), S.buf("yo1")]
        b_pf = [S.buf("pf0"), S.buf("pf1")]
        b_pb = [S.buf("pb0"), S.buf("pb1")]
        b_xn = [S.buf("xn0"), S.buf("xn1")]
        b_u3 = [S.buf("u30"), S.buf("u31")]
        b_pT = [S.buf("pT0"), S.buf("pT1")]
        b_sg = [S.buf("sg0"), S.buf("sg1")]
        b_tm = [S.buf("tm0"), S.buf("tm1")]
        b_ss = [S.buf("ss0"), S.buf("ss1")]
        b_ss2 = [S.buf("ss20"), S.buf("ss21")]
        b_junk = S.buf("junk")
        b_ps = new_ps()
        subs = [(True, t0, 128) for t0 in range(0, T, 128)] + [(False, 0, NS)]
        for it, (is_p, t0, r) in enumerate(subs):
            i2 = it % 2
            row = t0 if is_p else T
            ssv = CP[:, 8 + i2:9 + i2]
            ssv2 = CP[:, 12 + i2:13 + i2]
            psrc = p_p[t0:t0 + r, :] if is_p else p_s[0:r, :]
            ydst = y_p[t0:t0 + r, :] if is_p else y_s[0:r, :]
            S.add("sp", (lambda e, i2=i2, row=row, r=r: e.dma_start(out=h3[i2][0:r, :], in_=H3[row:row + r, :])), writes=[b_h3[i2]], dma=True)
            S.add("sp", (lambda e, i2=i2, psrc=psrc, r=r: e.dma_start(out=pf[i2][0:r, :], in_=psrc)), writes=[b_pf[i2]], dma=True)
            norm_T(h3[i2], b_h3[i2], r, gple, b_g, ssv, b_ss[i2], junk, b_junk, xn[i2], b_xn[i2], PS[6], b_ps[6], u3T[i2], b_u3[i2], 0)
            S.add("dve", (lambda e, i2=i2, r=r: e.tensor_copy(out=pb16[i2][0:r, :], in_=pf[i2][0:r, :])), reads=[b_pf[i2]], writes=[b_pb[i2]])
            trans_T(pb16[i2], b_pb[i2], r, PS[7], b_ps[7], pT[i2], b_pT[i2], 0, 2)
            for nh in range(2):
                pg, pp = PS[nh], PS[2 + nh]

                def mm(e, i2=i2, r=r, nh=nh, pg=pg, pp=pp):
                    ins = None
                    for k in range(8):
                        ins = e.matmul(pg[0:r, :], lhsT=u3T[i2][:, k, 0:r], rhs=Wpg[:, k, nh * 512:(nh + 1) * 512], start=(k == 0), stop=(k == 7))
                    for k in range(2):
                        ins = e.matmul(pp[0:r, :], lhsT=pT[i2][:, k, 0:r], rhs=Wpp[:, k, nh * 512:(nh + 1) * 512], start=(k == 0), stop=(k == 1))
                    return ins
                S.add("pe", mm, reads=[b_w, b_u3[i2], b_pT[i2]], writes=[b_ps[nh], b_ps[2 + nh]])
                S.add("act", (lambda e, nh=nh, r=r, pg=pg: e.activation(out=sg[nh][0:r, :], in_=pg[0:r, :], func=AF.Sigmoid)), reads=[b_ps[nh]], writes=[b_sg[nh]])
                S.add("dve", (lambda e, nh=nh, r=r, pp=pp: e.tensor_tensor(out=tm[nh][0:r, :], in0=pp[0:r, :], in1=sg[nh][0:r, :], op=ALU.mult)), reads=[b_ps[2 + nh], b_sg[nh]], writes=[b_tm[nh]])
                S.add("pool", (lambda e, i2=i2, nh=nh, r=r: e.tensor_tensor(out=h4[i2][0:r, nh * 512:(nh + 1) * 512], in0=tm[nh][0:r, :], in1=h3[i2][0:r, nh * 512:(nh + 1) * 512], op=ALU.add)),
                      reads=[b_tm[nh], b_h3[i2]], writes=[b_h4[i2]])
            rstd_ops(h4[i2], b_h4[i2], r, ssv2, b_ss2[i2], junk, b_junk, D)
            S.add("dve", (lambda e, i2=i2, r=r, ssv2=ssv2: e.scalar_tensor_tensor(out=yo[i2][0:r, :], in0=h4[i2][0:r, :], scalar=ssv2[0:r, :], in1=gfin[0:r, :], op0=ALU.mult, op1=ALU.mult)),
                  reads=[b_h4[i2], b_ss2[i2], b_g], writes=[b_yo[i2]])
            S.add("pool", (lambda e, i2=i2, r=r, ydst=ydst: e.dma_start(out=ydst, in_=yo[i2][0:r, :])), reads=[b_yo[i2]], dma=True)
        S.barrier()

    if "D3" in phases:
        phase_d3()

    S.barrier()
    S.finalize()
    sems = {e: es.enter_context(nc.semaphore(f"c_{e}")) for e in ENGS}
    dsems = {e: [es.enter_context(nc.semaphore(f"d_{e}{i}")) for i in range(NDSEM)] for e in ("sp", "pool", "act")}
    with nc.Block() as block:
        @block.sync
        def _(eng):
            S.emit("sp", eng, sems, dsems)

        @block.tensor
        def _(eng):
            S.emit("pe", eng, sems, dsems)

        @block.scalar
        def _(eng):
            S.emit("act", eng, sems, dsems)

        @block.vector
        def _(eng):
            S.emit("dve", eng, sems, dsems)

        @block.gpsimd
        def _(eng):
            S.emit("pool", eng, sems, dsems)
    es.close()
    return P


def make_consts():
    c = {}
    c["c_ident"] = np.eye(128, dtype=np.float32)
    perm = np.zeros((128, 128), np.float32)
    for m in range(128):
        k = (m // 64) * 64 + ((m % 64) + 32) % 64
        perm[k, m] = 1.0
    c["c_perm"] = perm
    pos = np.concatenate([np.arange(T), np.full(NS, T)]).astype(np.float32)
    half = 32
    inv_freq = (np.float32(10000.0) ** (-np.arange(half, dtype=np.float32) / np.float32(half))).astype(np.float32)
    ang = pos[:, None] * inv_freq[None, :]
    cos = np.cos(ang).astype(np.float32)
    sin = np.sin(ang).astype(np.float32)
    cosT = np.zeros((128, TA), np.float32)
    sinT = np.zeros((128, TA), np.float32)
    for p in range(128):
        dh = p % 64
        cosT[p] = cos[:, dh % 32]
        sinT[p] = -sin[:, dh % 32] if dh < 32 else sin[:, dh % 32]
    c["c_cos"] = cosT
    c["c_sin"] = sinT
    i = np.arange(128)
    triu = (i[:, None] <= i[None, :]).astype(np.float32)
    tril = (i[:, None] >= i[None, :]).astype(np.float32)
    strict = (i[:, None] > i[None, :]).astype(np.float32)
    ones = np.ones((128, 128), np.float32)
    c["c_masks"] = np.concatenate([triu, tril, strict, triu, ones], axis=1)
    ex = np.zeros((32, 2048), np.float32)
    for h in range(32):
        ex[h, h * 64:(h + 1) * 64] = 1.0
    c["c_expand"] = ex
    return c


_CACHE = {}


def core_inputs(inputs, c, consts):
    f = lambda a: np.ascontiguousarray(a, dtype=np.float32)
    sl = slice(c * NS, (c + 1) * NS)
    m = {
        "x_prompt": f(inputs["x_prompt"][c]),
        "x_sample": f(inputs["x_sample"][sl, 0]),
        "p_prompt": f(inputs["p_prompt"][0, c]),
        "p_sample": f(inputs["p_sample"][0, sl, 0]),
        "conv_wT": f(inputs["conv_w"][0].reshape(4, 24, 128).transpose(2, 1, 0)),
        "conv_bT": f(inputs["conv_b"][0].reshape(24, 128).T),
        "state_conv": f(inputs["state_conv"][0, sl].reshape(NS * 3, 3072)),
        "state_ssm": f(inputs["state_ssm"][0, sl].reshape(NS, 2048, 128)),
        "cache_kv0": f(inputs["cache_kv_w128"][0, sl].reshape(NS, 128, 1024)),
        "cache_kv1": f(inputs["cache_kv_w512"][0, sl].reshape(NS, 512, 1024)),
        "cache_kv2": f(inputs["cache_kv_w2048"][0, sl].reshape(NS, 2048, 1024)),
        "norm_final": f(inputs["norm_final"].reshape(1, D)),
    }
    m["dt_biasT"] = f(inputs["dt_bias"].reshape(32, 1))
    m["a_logT"] = f(inputs["a_log"].reshape(32, 1))
    m["d_skipT"] = f(inputs["d_skip"].reshape(32, 1))
    m["norm_ssmT"] = f(inputs["norm_ssm"].reshape(16, 128).T)
    for k in ("norm_ffn1", "norm_mix", "dt_bias", "a_log", "d_skip", "norm_ssm", "norm_ffn2", "norm_ple"):
        m[k] = f(inputs[k].reshape(1, -1))
    for k in ("w_ffn1_gate", "w_ffn1_up", "w_ffn1_down", "w_in", "w_o_ssm", "w_o_attn", "w_out",
              "w_ffn2_gate", "w_ffn2_up", "w_ffn2_down", "w_ple_gate", "w_ple_proj"):
        m[k] = f(inputs[k][0])
    m.update(consts)
    return m


def kernel(**inputs):
    if "P" not in _CACHE:
        _CACHE["P"] = build()
    P = _CACHE["P"]
    consts = make_consts()
    in_maps = []
    for c in range(NCORES):
        m = core_inputs(inputs, c, consts)
        in_maps.append({k: m[k] for k in P.ins})
    res = run_bass_kernel_spmd(P.nc, in_maps, core_ids=list(range(NCORES)))
    R = res.results
    g = lambda name: [np.asarray(R[c][name], dtype=np.float32) for c in range(NCORES)]
    y_prompt = np.stack(g("y_prompt"), 0)
    y_sample = np.concatenate(g("y_sample"), 0).reshape(NCORES * NS, 1, D)
    outs = [y_prompt, y_sample]
    for gi in range(3):
        outs.append(np.stack(g(f"kv{gi}_p"), 0).reshape(1, NCORES, W_G[gi], 2, 8, 64))
    outs.append(np.stack(g("conv_p"), 0).reshape(1, NCORES, 3, 3072))
    outs.append(np.stack(g("ssm_p"), 0).reshape(1, NCORES, 32, 64, 128))
    for gi in range(3):
        outs.append(np.concatenate(g(f"kv{gi}_s"), 0).reshape(1, NCORES * NS, 1, 2, 8, 64))
    outs.append(np.concatenate(g("conv_s"), 0).reshape(1, NCORES * NS, 3, 3072))
    outs.append(np.concatenate(g("ssm_s"), 0).reshape(1, NCORES * NS, 32, 64, 128))
    return tuple(outs)
```

```python
import numpy as np
from contextlib import ExitStack
import concourse.bass as bass
import concourse.mybir as mybir
from concourse.bass_utils import run_bass_kernel_spmd

F32 = mybir.dt.float32
BF16 = mybir.dt.bfloat16
ALU = mybir.AluOpType
AF = mybir.ActivationFunctionType
AX = mybir.AxisListType

D = 1024
DFF = 2816
import os
T = int(os.environ.get('KT_T', 8192))
NS = 16
NCORES = 8
NIN = 11808
EPS = 1e-6


class Buf:
    __slots__ = ("name", "w", "r", "wd", "rd")

    def __init__(self, name):
        self.name = name
        self.w = {}
        self.r = {}
        self.wd = []
        self.rd = []


class Op:
    __slots__ = ("eng", "fn", "deps", "needs_inc", "inc_idx", "is_dma", "dsem", "dval", "dprev")

    def __init__(self, eng, fn, is_dma=False):
        self.eng = eng
        self.fn = fn
        self.deps = []
        self.needs_inc = False
        self.inc_idx = None
        self.is_dma = is_dma
        self.dsem = None
        self.dval = None
        self.dprev = None


ENGS = ("pe", "act", "dve", "pool", "sp")
NDSEM = 8


class Sched:
    def __init__(self):
        self.ops = {e: [] for e in ENGS}
        self.ndma = {e: 0 for e in ENGS}
        self.all_bufs = []

    def buf(self, name):
        b = Buf(name)
        self.all_bufs.append(b)
        return b

    def add(self, eng, fn, reads=(), writes=(), dma=False):
        op = Op(eng, fn, dma)
        deps = []
        raw = []
        for b in reads:
            raw.extend(b.w.values())
            raw.extend(b.wd)
        deps.extend(raw)
        for b in writes:
            deps.extend(b.w.values())
            deps.extend(b.wd)
            deps.extend(b.r.values())
            deps.extend(b.rd)
        seen = set()
        raw_ids = set(id(d) for d in raw)
        for d in deps:
            if d is op or id(d) in seen:
                continue
            seen.add(id(d))
            if d.eng == eng and not d.is_dma and not dma and id(d) not in raw_ids and not os.environ.get('KSTRICT'):
                continue
            op.deps.append(d)
        for b in reads:
            if dma:
                b.rd.append(op)
                if len(b.rd) > 24:
                    b.rd.pop(0)
            else:
                b.r[eng] = op
        for b in writes:
            if dma:
                b.wd.append(op)
                if len(b.wd) > 24:
                    b.wd.pop(0)
            else:
                b.w[eng] = op
        if dma:
            i = self.ndma[eng]
            self.ndma[eng] += 1
            op.dsem = i % NDSEM
            op.dval = 16 * (i // NDSEM + 1)
        self.ops[eng].append(op)
        return op

    def barrier(self):
        last = []
        for e in ENGS:
            ops = self.ops[e]
            for o in reversed(ops):
                if o.fn is not None and not o.is_dma:
                    last.append(o)
                    break
            cnt = 0
            for o in reversed(ops):
                if o.is_dma:
                    last.append(o)
                    cnt += 1
                    if cnt >= NDSEM:
                        break
        for e in ENGS:
            op = Op(e, None)
            op.deps = [d for d in last]
            self.ops[e].append(op)
        for b in self.all_bufs:
            b.w = {}
            b.r = {}
            b.wd = []
            b.rd = []
        self.all_bufs = []

    def finalize(self):
        for e in ENGS:
            for op in self.ops[e]:
                for d in op.deps:
                    if not d.is_dma:
                        d.needs_inc = True
        for e in ENGS:
            c = 0
            for op in self.ops[e]:
                if op.needs_inc and not op.is_dma:
                    c += 1
                    op.inc_idx = c

    def emit(self, eng, handle, sems, dsems):
        waited = {}
        for op in self.ops[eng]:
            need = {}
            for d in op.deps:
                if d.is_dma:
                    key = ("d", d.eng, d.dsem)
                    val = d.dval
                else:
                    if d.eng == eng and d.inc_idx is None:
                        continue
                    key = ("c", d.eng)
                    val = d.inc_idx
                if need.get(key, 0) < val:
                    need[key] = val
            if op.is_dma:
                if op.dval > 16:
                    key = ("d", eng, op.dsem)
                    if need.get(key, 0) < op.dval - 16:
                        need[key] = op.dval - 16
            for key, val in need.items():
                if waited.get(key, 0) >= val:
                    continue
                waited[key] = val
                if key[0] == "c":
                    handle.wait_ge(sems[key[1]], val)
                else:
                    handle.wait_ge(dsems[key[1]][key[2]], val)
            if op.fn is None:
                continue
            inst = op.fn(handle)
            if op.is_dma:
                inst.then_inc(dsems[eng][op.dsem], 16)
            elif op.needs_inc:
                inst.then_inc(sems[eng], 1)


class Prog:
    def __init__(self, debug_outs=()):
        self.nc = bass.Bass("TRN2", target_bir_lowering=False)
        self.S = Sched()
        self.debug_outs = set(debug_outs)
        self.ins = {}
        self.outs = {}
        self.scr = {}

    def din(self, name, shape, dt=F32):
        t = self.nc.dram_tensor(name, list(shape), dt, kind="ExternalInput").ap()
        self.ins[name] = t
        return t

    def dout(self, name, shape, dt=F32):
        t = self.nc.dram_tensor(name, list(shape), dt, kind="ExternalOutput").ap()
        self.outs[name] = t
        return t

    def dscr(self, name, shape, dt):
        kind = "ExternalOutput" if name in self.debug_outs else "Internal"
        t = self.nc.dram_tensor(name, list(shape), dt, kind=kind).ap()
        self.scr[name] = t
        return t


import os
NPOOL = 104448
TA = T + NS
W_G = (128, 512, 2048)
DIL = (1, 4, 16)


def build(debug_outs=(), phases=("F1", "A1", "B", "A2", "C", "D1", "D2", "D3")):
    P = Prog(debug_outs)
    nc = P.nc
    S = P.S
    es = ExitStack()

    x_p = P.din("x_prompt", [T, D])
    x_s = P.din("x_sample", [NS, D])
    p_p = P.din("p_prompt", [T, 256])
    p_s = P.din("p_sample", [NS, 256])
    norm_ffn1 = P.din("norm_ffn1", [1, D])
    w1g = P.din("w_ffn1_gate", [D, DFF])
    w1u = P.din("w_ffn1_up", [D, DFF])
    w1d = P.din("w_ffn1_down", [DFF, D])
    norm_mix = P.din("norm_mix", [1, D])
    w_in = P.din("w_in", [D, NIN])
    conv_wT = P.din("conv_wT", [128, 24, 4])
    conv_bT = P.din("conv_bT", [128, 24])
    dt_bias = P.din("dt_bias", [1, 32])
    a_log = P.din("a_log", [1, 32])
    d_skip = P.din("d_skip", [1, 32])
    norm_ssm = P.din("norm_ssm", [1, 2048])
    w_o_ssm = P.din("w_o_ssm", [2048, D])
    w_o_attn = P.din("w_o_attn", [512, D])
    w_out = P.din("w_out", [D, D])
    norm_ffn2 = P.din("norm_ffn2", [1, D])
    w2g = P.din("w_ffn2_gate", [D, DFF])
    w2u = P.din("w_ffn2_up", [D, DFF])
    w2d = P.din("w_ffn2_down", [DFF, D])
    norm_ple = P.din("norm_ple", [1, D])
    w_ple_gate = P.din("w_ple_gate", [D, D])
    w_ple_proj = P.din("w_ple_proj", [256, D])
    norm_final = P.din("norm_final", [1, D])
    state_conv = P.din("state_conv", [NS * 3, 3072])
    state_ssm = P.din("state_ssm", [NS, 2048, 128])
    cache_kv = [P.din(f"cache_kv{g}", [NS, W_G[g], 1024]) for g in range(3)]
    ident_d = P.din("c_ident", [128, 128])
    perm_d = P.din("c_perm", [128, 128])
    cos_d = P.din("c_cos", [128, TA])
    sin_d = P.din("c_sin", [128, TA])
    cmask_d = P.din("c_masks", [128, 5 * 128])
    expand_d = P.din("c_expand", [32, 2048])
    dt_biasT = P.din("dt_biasT", [32, 1])
    a_logT = P.din("a_logT", [32, 1])
    d_skipT = P.din("d_skipT", [32, 1])
    norm_ssmT = P.din("norm_ssmT", [128, 16])

    H1 = P.dscr("H1", [TA, D], F32)
    U = P.dscr("U", [D, TA], BF16)
    XC = P.dscr("XC", [3072, TA], BF16)
    QT = P.dscr("QT", [1536, TA], BF16)
    KT = P.dscr("KT", [1536, TA], BF16)
    V = P.dscr("V", [TA, 1536], BF16)
    MS = P.dscr("MS", [D, TA], BF16)
    AO = [P.dscr(f"AO{g}", [T, 520], F32) for g in range(3)]
    AOS = P.dscr("AOS", [NS, 520], F32)
    QKS = P.dscr("QKS", [NS, 3072], F32)
    BCS = P.dscr("BCS", [NS, 1024], F32)
    H2 = P.dscr("H2", [TA, D], F32)
    H3 = P.dscr("H3", [TA, D], F32)

    y_p = P.dout("y_prompt", [T, D])
    y_s = P.dout("y_sample", [NS, D])
    kv_p = [P.dout(f"kv{g}_p", [W_G[g], 2, 512]) for g in range(3)]
    conv_p = P.dout("conv_p", [3, 3072])
    ssm_p = P.dout("ssm_p", [2048, 128])
    kv_s = [P.dout(f"kv{g}_s", [NS, 2, 512]) for g in range(3)]
    conv_s = P.dout("conv_s", [NS, 3, 3072])
    ssm_s = P.dout("ssm_s", [NS, 2048, 128])

    POOL = es.enter_context(nc.sbuf_tensor("pool", [128, NPOOL], BF16))
    CP = es.enter_context(nc.sbuf_tensor("cpool", [128, 256], F32))
    ident_f = es.enter_context(nc.sbuf_tensor("ident_f", [128, 128], F32))
    ident = es.enter_context(nc.sbuf_tensor("ident_b", [128, 128], BF16))
    PS = [es.enter_context(nc.psum_tensor(f"ps{i}", [128, 512], F32)) for i in range(8)]

    class Alloc:
        def __init__(self):
            self.off = 0

        def bf(self, n):
            n2 = (n + 1) // 2 * 2
            v = POOL[:, self.off:self.off + n]
            self.off += n2
            assert self.off <= NPOOL, self.off
            return v

        def f32(self, n):
            v = POOL[:, self.off:self.off + 2 * n].bitcast(F32)
            self.off += 2 * n
            assert self.off <= NPOOL, self.off
            return v

    b_ident = S.buf("ident")
    bi0 = S.buf("identf")
    S.add("sp", lambda e: e.dma_start(out=ident_f[:], in_=ident_d[:, :]), writes=[bi0], dma=True)
    S.add("dve", lambda e: e.tensor_copy(out=ident[:], in_=ident_f[:]), reads=[bi0], writes=[b_ident])
    mhalf = CP[:, 255:256]
    b_mhalf = S.buf("mhalf")
    S.add("pool", lambda e: e.memset(mhalf, -0.5), writes=[b_mhalf])
    b_pool0 = S.buf("pool0")
    NQ = NPOOL // 4
    for qi, en in enumerate(("dve", "pool", "act", "dve")):
        if en == "act":
            S.add(en, (lambda e, qi=qi: e.memzero(POOL[:, qi * NQ:(qi + 1) * NQ])), writes=[b_pool0])
        else:
            S.add(en, (lambda e, qi=qi: e.memset(POOL[:, qi * NQ:(qi + 1) * NQ], 0.0)), writes=[b_pool0])
    S.add("dve", lambda e: e.memset(CP[:, 0:255], 0.0), writes=[b_pool0])
    S.barrier()
    cvt_rr = [0]

    def new_ps():
        return [S.buf(f"ps{i}") for i in range(8)]

    def load_weight(dst_ap_fn, src, kchunks, c_lo, ncols, wbuf, stage_bufs, stage_aps, CH=1024):
        srcv = src.rearrange("(k p) n -> p k n", p=128)
        i = 0
        for k in range(kchunks):
            for c0 in range(0, ncols, CH):
                cw = min(CH, ncols - c0)
                sb = stage_bufs[i % 2]
                sa = stage_aps[i % 2]
                q = "sp" if i % 2 == 0 else "pool"
                S.add(q, (lambda e, sa=sa, k=k, c0=c0, cw=cw: e.dma_start(out=sa[:, 0:cw], in_=srcv[:, k, c_lo + c0:c_lo + c0 + cw])),
                      writes=[sb], dma=True)
                ce = ("act", "dve")[cvt_rr[0] % 2]
                cvt_rr[0] += 1
                dst = dst_ap_fn(k, c0, cw)
                if ce == "act":
                    S.add("act", (lambda e, dst=dst, sa=sa, cw=cw: e.copy(out=dst, in_=sa[:, 0:cw])), reads=[sb], writes=[wbuf])
                else:
                    S.add("dve", (lambda e, dst=dst, sa=sa, cw=cw: e.tensor_copy(out=dst, in_=sa[:, 0:cw])), reads=[sb], writes=[wbuf])
                i += 1

    def load_bc(dst, src_row, n, buf, q="sp"):
        S.add(q, lambda e: e.dma_start(out=dst, in_=src_row[0:1, :].to_broadcast([128, n])), writes=[buf], dma=True)

    def rstd_ops(h, b_h, r, ssv, b_ssv, junk, b_junk, n):
        S.add("act", (lambda e: e.activation(out=junk[0:r, 0:n], in_=h[0:r, 0:n], func=AF.Square, accum_out=ssv[0:r, :])),
              reads=[b_h], writes=[b_junk, b_ssv])
        S.add("dve", (lambda e: e.tensor_scalar(out=ssv[0:r, :], in0=ssv[0:r, :], scalar1=1.0 / n, scalar2=EPS, op0=ALU.mult, op1=ALU.add)),
              reads=[b_ssv], writes=[b_ssv])
        S.add("pool", (lambda e: e.tensor_tensor(out=ssv[0:r, :], in0=ssv[0:r, :], in1=mhalf[0:r, :], op=ALU.pow)),
              reads=[b_ssv, b_mhalf], writes=[b_ssv])

    def norm_T(h, b_h, r, gbc, b_g, ssv, b_ssv, junk, b_junk, xn, b_xn, pt, b_pt, dstT, b_dst, c0, nk=8):
        n = nk * 128
        rstd_ops(h, b_h, r, ssv, b_ssv, junk, b_junk, n)
        S.add("dve", (lambda e: e.scalar_tensor_tensor(out=xn[0:r, 0:n], in0=h[0:r, 0:n], scalar=ssv[0:r, :], in1=gbc[0:r, 0:n], op0=ALU.mult, op1=ALU.mult)),
              reads=[b_h, b_ssv, b_g], writes=[b_xn])
        trans_T(xn, b_xn, r, pt, b_pt, dstT, b_dst, c0, nk)

    def trans_T(xn, b_xn, r, pt, b_pt, dstT, b_dst, c0, nk):
        for k0 in range(0, nk, 8):
            kk = min(8, nk - k0)
            ptv = pt[:].bitcast(BF16)[:, 0:1024].rearrange("p (k t) -> p k t", k=8)

            def tr(e, k0=k0, kk=kk, ptv=ptv):
                ins = None
                for k in range(kk):
                    ins = e.transpose(out=ptv[:, k, 0:r], in_=xn[0:r, (k0 + k) * 128:(k0 + k + 1) * 128], identity=ident[0:r, 0:r])
                return ins
            S.add("pe", tr, reads=[b_xn, b_ident], writes=[b_pt])
            S.add("act", (lambda e, k0=k0, kk=kk, ptv=ptv: e.copy(out=dstT[:, k0:k0 + kk, c0:c0 + r], in_=ptv[:, 0:kk, 0:r])),
                  reads=[b_pt], writes=[b_dst])

    def tiles_of(TT):
        return [(True, t0, TT) for t0 in range(0, T, TT)] + [(False, 0, NS)]

    def ffn_phase(src_p, src_s, gain_in, wg_d, wu_d, wd_d, post_init, post_sub, post_tile):
        TT = 256
        A = Alloc()
        Wg = A.bf(8 * DFF).rearrange("p (k n) -> p k n", k=8)
        Wu = A.bf(8 * DFF).rearrange("p (k n) -> p k n", k=8)
        Wd = A.bf(22 * D).rearrange("p (k n) -> p k n", k=22)
        st = A.f32(4096)
        xs_all = [[st[:, 0:1024], st[:, 1024:2048]], [st[:, 2048:3072], st[:, 3072:4096]]]
        stage = xs_all[0]
        hs_ = A.f32(2048)
        hs = [hs_[:, 0:1024], hs_[:, 1024:2048]]
        gbc = A.f32(1024)
        xn_ = A.bf(2048)
        xn = [xn_[:, 0:1024], xn_[:, 1024:2048]]
        uT_all = [A.bf(8 * TT).rearrange("p (k t) -> p k t", k=8) for _ in range(2)]
        sg_ = A.bf(2 * TT)
        sg = [sg_[:, 0:TT], sg_[:, TT:2 * TT]]
        actT = A.bf(22 * TT).rearrange("p (k t) -> p k t", k=22)
        junk = A.bf(1024)
        b_w = S.buf("w")
        b_xs_all = [[S.buf("st0"), S.buf("st1")], [S.buf("st2"), S.buf("st3")]]
        b_xs = b_xs_all[0]
        load_weight(lambda k, c0, cw: Wg[:, k, c0:c0 + cw], wg_d, 8, 0, DFF, b_w, b_xs, stage)
        load_weight(lambda k, c0, cw: Wu[:, k, c0:c0 + cw], wu_d, 8, 0, DFF, b_w, b_xs, stage)
        load_weight(lambda k, c0, cw: Wd[:, k, c0:c0 + cw], wd_d, 22, 0, D, b_w, b_xs, stage)
        b_g = S.buf("g")
        load_bc(gbc, gain_in, D, b_g)
        b_hs = [S.buf("h0"), S.buf("h1")]
        b_xn = [S.buf("xn0"), S.buf("xn1")]
        b_uT_all = [S.buf("uT0"), S.buf("uT1")]
        b_sg = [S.buf("sg0"), S.buf("sg1")]
        b_actT = S.buf("actT")
        b_ss = [S.buf("ss0"), S.buf("ss1")]
        b_ps = new_ps()
        ss = [CP[:, i:i + 1] for i in range(2)]
        b_junk = S.buf("junk")
        env = dict(A=A, PS=PS, b_ps=b_ps, ss=ss, b_ss=b_ss, junk=junk, b_junk=b_junk, TT=TT)
        post_init(env)
        for ti, (is_p, t0, tt) in enumerate(tiles_of(TT)):
            xs, b_xs = xs_all[ti % 2], b_xs_all[ti % 2]
            uT, b_uT = uT_all[ti % 2], b_uT_all[ti % 2]
            src = src_p if is_p else src_s
            nsub = (tt + 127) // 128
            row0 = t0 if is_p else T
            for s in range(nsub):
                r = min(128, tt - s * 128)
                xa = xs[s]
                S.add("sp", (lambda e, xa=xa, r=r, s=s, src=src, t0=t0: e.dma_start(out=xa[0:r, :], in_=src[t0 + s * 128:t0 + s * 128 + r, :])),
                      writes=[b_xs[s]], dma=True)
                norm_T(xa, b_xs[s], r, gbc, b_g, ss[s], b_ss[s], junk, b_junk, xn[s], b_xn[s], PS[6 + s], b_ps[6 + s], uT, b_uT, s * 128)
            for m in range(22):
                pg = PS[(m % 2) * 2]
                pu = PS[(m % 2) * 2 + 1]
                bg_, bu_ = b_ps[(m % 2) * 2], b_ps[(m % 2) * 2 + 1]

                def mm(e, m=m, pg=pg, pu=pu, tt=tt, uT=uT):
                    ins = None
                    for k in range(8):
                        ins = e.matmul(pg[:, 0:tt], lhsT=Wg[:, k, m * 128:(m + 1) * 128], rhs=uT[:, k, 0:tt], start=(k == 0), stop=(k == 7))
                    for k in range(8):
                        ins = e.matmul(pu[:, 0:tt], lhsT=Wu[:, k, m * 128:(m + 1) * 128], rhs=uT[:, k, 0:tt], start=(k == 0), stop=(k == 7))
                    return ins
                S.add("pe", mm, reads=[b_w, b_uT], writes=[bg_, bu_])
                sgt = sg[m % 2]
                S.add("act", (lambda e, pg=pg, sgt=sgt, tt=tt: e.activation(out=sgt[:, 0:tt], in_=pg[:, 0:tt], func=AF.Silu)),
                      reads=[bg_], writes=[b_sg[m % 2]])
                S.add("dve", (lambda e, pu=pu, sgt=sgt, tt=tt, m=m: e.tensor_tensor(out=actT[:, m, 0:tt], in0=pu[:, 0:tt], in1=sgt[:, 0:tt], op=ALU.mult)),
                      reads=[bu_, b_sg[m % 2]], writes=[b_actT])
            for s in range(nsub):
                r = min(128, tt - s * 128)
                for nh in range(2):
                    pd = PS[4 + nh]

                    def mmd(e, s=s, r=r, nh=nh, pd=pd):
                        ins = None
                        for k in range(22):
                            ins = e.matmul(pd[0:r, :], lhsT=actT[:, k, s * 128:s * 128 + r], rhs=Wd[:, k, nh * 512:(nh + 1) * 512], start=(k == 0), stop=(k == 21))
                        return ins
                    S.add("pe", mmd, reads=[b_w, b_actT], writes=[b_ps[4 + nh]])
                    S.add("dve", (lambda e, s=s, r=r, nh=nh, pd=pd, xs=xs: e.scalar_tensor_tensor(out=hs[s][0:r, nh * 512:(nh + 1) * 512], in0=pd[0:r, :], scalar=0.5, in1=xs[s][0:r, nh * 512:(nh + 1) * 512], op0=ALU.mult, op1=ALU.add)),
                          reads=[b_ps[4 + nh], b_xs[s]], writes=[b_hs[s]])
                post_sub(env, s, r, row0 + s * 128, hs[s], b_hs[s])
            post_tile(env, row0, tt)
        S.barrier()

    def f1_init(env):
        A = env["A"]
        env["gbc2"] = A.f32(1024)
        env["b_g2"] = S.buf("g2")
        load_bc(env["gbc2"], norm_mix, D, env["b_g2"])
        x2 = A.bf(2048)
        env["xn2"] = [x2[:, 0:1024], x2[:, 1024:2048]]
        env["b_xn2"] = [S.buf("xn2a"), S.buf("xn2b")]
        env["u2T"] = A.bf(8 * env["TT"]).rearrange("p (k t) -> p k t", k=8)
        env["b_u2T"] = S.buf("u2T")

    def f1_sub(env, s, r, row, h, b_h):
        S.add("pool", (lambda e: e.dma_start(out=H1[row:row + r, :], in_=h[0:r, :])), reads=[b_h], dma=True)
        norm_T(h, b_h, r, env["gbc2"], env["b_g2"], env["ss"][s], env["b_ss"][s], env["junk"], env["b_junk"],
               env["xn2"][s], env["b_xn2"][s], env["PS"][6 + s], env["b_ps"][6 + s], env["u2T"], env["b_u2T"], s * 128)

    def f1_tile(env, row0, tt):
        Uv = U.rearrange("(k p) t -> p k t", p=128)
        u2T = env["u2T"]
        S.add("sp", (lambda e: e.dma_start(out=Uv[:, :, row0:row0 + tt], in_=u2T[:, :, 0:tt])), reads=[env["b_u2T"]], dma=True)

    if "F1" in phases:
        ffn_phase(x_p, x_s, norm_ffn1, w1g, w1u, w1d, f1_init, f1_sub, f1_tile)

    def phase_a1():
        TT = 512
        A = Alloc()
        Wx = A.bf(8 * 3072).rearrange("p (k n) -> p k n", k=8)
        st = A.f32(2048)
        stage = [st[:, 0:1024], st[:, 1024:2048]]
        cw = A.f32(96).rearrange("p (m t) -> p m t", m=24)
        cb = A.f32(24)
        uT = A.bf(8 * TT).rearrange("p (k t) -> p k t", k=8)
        xraw = A.bf(24 * (TT + 4)).rearrange("p (m t) -> p m t", m=24)
        xc = A.bf(24 * TT).rearrange("p (m t) -> p m t", m=24)
        ac_ = A.f32(2 * TT)
        acc = [ac_[:, 0:TT], ac_[:, TT:2 * TT]]
        tok = A.f32(3072)
        prevT = A.f32(24 * 48).rearrange("p (m c) -> p m c", m=24)
        sc_tok = A.f32(3072)
        b_w = S.buf("w")
        b_st = [S.buf("st0"), S.buf("st1")]
        load_weight(lambda k, c0, c: Wx[:, k, c0:c0 + c], w_in, 8, 2048, 3072, b_w, b_st, stage)
        b_cw = S.buf("cw")
        S.add("sp", lambda e: e.dma_start(out=cw, in_=conv_wT[:, :, :]), writes=[b_cw], dma=True)
        S.add("sp", lambda e: e.dma_start(out=cb, in_=conv_bT[:, :]), writes=[b_cw], dma=True)
        b_uT = S.buf("uT")
        b_xraw = S.buf("xraw")
        b_xc = S.buf("xc")
        b_acc = [S.buf("acc0"), S.buf("acc1")]
        b_tok = S.buf("tok")
        b_ps = new_ps()
        Uv = U.rearrange("(k p) t -> p k t", p=128)
        XCv = XC.rearrange("(m p) t -> p m t", p=128)
        S.add("dve", lambda e: e.memset(xraw[:, :, 0:3], 0.0), writes=[b_xraw])

        def tokmajor(e_uT_cols, r, dst_tok, dst_dram):
            c_lo, c_hi = e_uT_cols
            for nb in range(6):
                pb = PS[4 + nb % 2]

                def mmt(e, nb=nb, pb=pb):
                    ins = None
                    for k in range(8):
                        ins = e.matmul(pb[0:r, :], lhsT=uT[:, k, c_lo:c_hi], rhs=Wx[:, k, nb * 512:(nb + 1) * 512], start=(k == 0), stop=(k == 7))
                    return ins
                S.add("pe", mmt, reads=[b_w, b_uT], writes=[b_ps[4 + nb % 2]])
                S.add("act", (lambda e, nb=nb, pb=pb: e.copy(out=dst_tok[0:r, nb * 512:(nb + 1) * 512], in_=pb[0:r, :])),
                      reads=[b_ps[4 + nb % 2]], writes=[b_tok])
            S.add("pool", (lambda e: e.dma_start(out=dst_dram, in_=dst_tok[0:r, :])), reads=[b_tok], dma=True)

        for (is_p, t0, tt) in tiles_of(TT):
            row0 = t0 if is_p else T
            S.add("sp", (lambda e, row0=row0, tt=tt: e.dma_start(out=uT[:, :, 0:tt], in_=Uv[:, :, row0:row0 + tt])), writes=[b_uT], dma=True)
            if not is_p:
                scv = state_conv
                S.add("sp", (lambda e: e.dma_start(out=sc_tok[0:48, :], in_=scv[:, :])), writes=[b_tok], dma=True)
                for g3 in range(3):
                    pb = PS[6]

                    def trp(e, g3=g3, pb=pb):
                        ins = None
                        for j in range(8):
                            m = g3 * 8 + j
                            ins = e.transpose(out=pb[:, j * 48:(j + 1) * 48], in_=sc_tok[0:48, m * 128:(m + 1) * 128], identity=ident_f[0:48, 0:48])
                        return ins
                    S.add("pe", trp, reads=[b_tok], writes=[b_ps[6]])
                    S.add("dve", (lambda e, g3=g3, pb=pb: e.tensor_copy(out=prevT[:, g3 * 8:(g3 + 1) * 8, :], in_=pb[:, 0:384].rearrange("p (j c) -> p j c", j=8))),
                          reads=[b_ps[6]], writes=[b_xraw])
            for m in range(24):
                pb = PS[m % 2]

                def mm(e, m=m, pb=pb, tt=tt):
                    ins = None
                    for k in range(8):
                        ins = e.matmul(pb[:, 0:tt], lhsT=Wx[:, k, m * 128:(m + 1) * 128], rhs=uT[:, k, 0:tt], start=(k == 0), stop=(k == 7))
                    return ins
                S.add("pe", mm, reads=[b_w, b_uT], writes=[b_ps[m % 2]])
                ac = acc[m % 2]
                b_ac = b_acc[m % 2]
                S.add("act", (lambda e, m=m, pb=pb, tt=tt, ac=ac: e.activation(out=ac[:, 0:tt], in_=pb[:, 0:tt], func=AF.Identity, scale=cw[:, m, 3:4], bias=cb[:, m:m + 1])),
                      reads=[b_ps[m % 2], b_cw], writes=[b_ac])
                if is_p:
                    S.add("act", (lambda e, m=m, pb=pb, tt=tt: e.copy(out=xraw[:, m, 3:3 + tt], in_=pb[:, 0:tt])),
                          reads=[b_ps[m % 2]], writes=[b_xraw])
                    for j in range(1, 4):
                        S.add("dve", (lambda e, m=m, j=j, tt=tt, ac=ac: e.scalar_tensor_tensor(out=ac[:, 0:tt], in0=xraw[:, m, 3 - j:3 - j + tt], scalar=cw[:, m, 3 - j:4 - j], in1=ac[:, 0:tt], op0=ALU.mult, op1=ALU.add)),
                              reads=[b_xraw, b_ac, b_cw], writes=[b_ac])
                else:
                    for j in range(3):
                        S.add("dve", (lambda e, m=m, j=j, tt=tt, ac=ac: e.scalar_tensor_tensor(out=ac[:, 0:tt], in0=prevT[:, m, j:48:3], scalar=cw[:, m, j:j + 1], in1=ac[:, 0:tt], op0=ALU.mult, op1=ALU.add)),
                              reads=[b_xraw, b_ac, b_cw], writes=[b_ac])
                S.add("act", (lambda e, m=m, tt=tt, ac=ac: e.activation(out=xc[:, m, 0:tt], in_=ac[:, 0:tt], func=AF.Silu)),
                      reads=[b_ac], writes=[b_xc])
            S.add("pool", (lambda e, row0=row0, tt=tt: e.dma_start(out=XCv[:, :, row0:row0 + tt], in_=xc[:, :, 0:tt])), reads=[b_xc], dma=True)
            if is_p:
                S.add("dve", (lambda e, tt=tt: e.tensor_copy(out=xraw[:, :, 0:3], in_=xraw[:, :, tt:tt + 3])), reads=[b_xraw], writes=[b_xraw])
            if is_p and t0 + tt == T:
                tokmajor((tt - 3, tt), 3, tok, conv_p[:, :])
            if not is_p:
                tokmajor((0, NS), NS, tok, conv_s[:, 2, :])
                scv3 = state_conv.rearrange("(b j) c -> b j c", j=3)
                S.add("sp", (lambda e: e.dma_start(out=conv_s[:, 0:2, :], in_=scv3[:, 1:3, :])), dma=True)
        S.barrier()

    if "A1" in phases:
        phase_a1()

    def phase_b():
        TT = 512
        A = Alloc()
        Wq = A.bf(8 * 4608).rearrange("p (k n) -> p k n", k=8)
        st = A.f32(2048)
        stage = [st[:, 0:1024], st[:, 1024:2048]]
        perm = A.bf(128)
        uT = A.bf(8 * TT).rearrange("p (k t) -> p k t", k=8)
        cosT = A.f32(TT)
        sinT = A.f32(TT)
        qb_ = A.bf(2 * TT)
        qb = [qb_[:, 0:TT], qb_[:, TT:2 * TT]]
        t1_ = A.f32(2 * TT)
        t1 = [t1_[:, 0:TT], t1_[:, TT:2 * TT]]
        qf_ = A.f32(2 * TT)
        qf = [qf_[:, 0:TT], qf_[:, TT:2 * TT]]
        b_qf = [S.buf("qf0"), S.buf("qf1")]
        t2_ = A.f32(2 * TT)
        t2 = [t2_[:, 0:TT], t2_[:, TT:2 * TT]]
        qk = A.bf(24 * TT).rearrange("p (m t) -> p m t", m=24)
        vtok_ = A.bf(2 * 1536)
        vtok = [vtok_[:, 0:1536], vtok_[:, 1536:3072]]
        stg_ = A.f32(2 * 512)
        stg = [stg_[:, 0:512], stg_[:, 512:1024]]
        b_w = S.buf("w")
        b_st = [S.buf("st0"), S.buf("st1")]
        load_weight(lambda k, c0, c: Wq[:, k, c0:c0 + c], w_in, 8, 5152, 4608, b_w, b_st, stage)
        b_perm = S.buf("perm")
        S.add("sp", lambda e: e.dma_start(out=stage[0][:, 0:128], in_=perm_d[:, :]), writes=[b_st[0]], dma=True)
        S.add("dve", lambda e: e.tensor_copy(out=perm, in_=stage[0][:, 0:128]), reads=[b_st[0]], writes=[b_perm])
        b_uT = S.buf("uT")
        b_cs = S.buf("cs")
        if os.environ.get('MSCOS'):
            S.add("dve", lambda e: e.memset(cosT, 0.5), writes=[b_cs])
            S.add("dve", lambda e: e.memset(sinT, 0.5), writes=[b_cs])
        b_qb = [S.buf("qb0"), S.buf("qb1")]
        b_t1 = [S.buf("t10"), S.buf("t11")]
        b_t2 = [S.buf("t20"), S.buf("t21")]
        b_qk = S.buf("qk")
        b_vt = [S.buf("vt0"), S.buf("vt1")]
        b_stg = [S.buf("sg0"), S.buf("sg1")]
        b_ps = new_ps()
        Uv = U.rearrange("(k p) t -> p k t", p=128)
        QTv = QT.rearrange("(m p) t -> p m t", p=128)
        KTv = KT.rearrange("(m p) t -> p m t", p=128)
        stg_i = [0]

        for (is_p, t0, tt) in tiles_of(TT):
            row0 = t0 if is_p else T
            nsub = (tt + 127) // 128
            S.add("sp", (lambda e, row0=row0, tt=tt: e.dma_start(out=uT[:, :, 0:tt], in_=Uv[:, :, row0:row0 + tt])), writes=[b_uT], dma=True)
            S.add("sp", (lambda e, row0=row0, tt=tt: e.dma_start(out=cosT[:, 0:tt], in_=cos_d[:, row0:row0 + tt])), writes=[b_cs], dma=True)
            S.add("sp", (lambda e, row0=row0, tt=tt: e.dma_start(out=sinT[:, 0:tt], in_=sin_d[:, row0:row0 + tt])), writes=[b_cs], dma=True)
            for m in range(0 if not os.environ.get('NOROT') else 24, 24):
                pa = PS[m % 2]
                pb = PS[2 + m % 2]
                i2 = m % 2

                def mm(e, m=m, pa=pa, tt=tt):
                    ins = None
                    for k in range(8):
                        ins = e.matmul(pa[:, 0:tt], lhsT=Wq[:, k, m * 128:(m + 1) * 128], rhs=uT[:, k, 0:tt], start=(k == 0), stop=(k == 7))
                    return ins
                S.add("pe", mm, reads=[b_w, b_uT], writes=[b_ps[m % 2]])
                S.add("act", (lambda e, pa=pa, i2=i2, tt=tt: e.copy(out=qb[i2][:, 0:tt], in_=pa[:, 0:tt])), reads=[b_ps[m % 2]], writes=[b_qb[i2]])
                S.add("pe", (lambda e, pb=pb, i2=i2, tt=tt: e.matmul(pb[:, 0:tt], lhsT=perm, rhs=qb[i2][:, 0:tt], start=True, stop=True)),
                      reads=[b_qb[i2], b_perm], writes=[b_ps[2 + m % 2]])
                S.add("act", (lambda e, pa=pa, i2=i2, tt=tt: e.copy(out=qf[i2][:, 0:tt], in_=pa[:, 0:tt])), reads=[b_ps[m % 2]], writes=[b_qf[i2]])
                S.add("pool", (lambda e, i2=i2, tt=tt: e.tensor_tensor(out=t1[i2][:, 0:tt], in0=qf[i2][:, 0:tt], in1=(cosT if not os.environ.get('NOCOS') else qf[i2])[:, 0:tt], op=ALU.mult)),
                      reads=[b_qf[i2], b_cs], writes=[b_t1[i2]])
                S.add("dve", (lambda e, pb=pb, i2=i2, tt=tt: e.tensor_tensor(out=t2[i2][:, 0:tt], in0=pb[:, 0:tt], in1=(sinT if not os.environ.get('NOCOS') else qf[i2])[:, 0:tt], op=ALU.mult)),
                      reads=[b_ps[2 + m % 2], b_cs], writes=[b_t2[i2]])
                S.add("dve", (lambda e, m=m, i2=i2, tt=tt: e.tensor_tensor(out=qk[:, m, 0:tt], in0=t1[i2][:, 0:tt], in1=t2[i2][:, 0:tt], op=ALU.add)),
                      reads=[b_t1[i2], b_t2[i2]], writes=[b_qk])
            S.add("pool", (lambda e, row0=row0, tt=tt: e.dma_start(out=QTv[:, :, row0:row0 + tt], in_=qk[:, 0:12, 0:tt])), reads=[b_qk], dma=True)
            S.add("pool", (lambda e, row0=row0, tt=tt: e.dma_start(out=KTv[:, :, row0:row0 + tt], in_=qk[:, 12:24, 0:tt])), reads=[b_qk], dma=True)
            if not is_p:
                for c6 in range(6):
                    pt = PS[6 + c6 % 2]
                    ptv = pt[:].bitcast(BF16)[:, 0:512]

                    def trq(e, c6=c6, ptv=ptv):
                        ins = None
                        for c in range(4):
                            ins = e.transpose(out=ptv[0:NS, c * 128:(c + 1) * 128], in_=qk[:, c6 * 4 + c, 0:NS], identity=ident[:, :])
                        return ins
                    S.add("pe", trq, reads=[b_qk, b_ident], writes=[b_ps[6 + c6 % 2]])
                    si = stg_i[0] % 2
                    stg_i[0] += 1
                    S.add("dve", (lambda e, ptv=ptv, si=si: e.tensor_copy(out=stg[si][0:NS, :], in_=ptv[0:NS, :])),
                          reads=[b_ps[6 + c6 % 2]], writes=[b_stg[si]])
                    S.add("sp", (lambda e, c6=c6, si=si: e.dma_start(out=QKS[:, c6 * 512:(c6 + 1) * 512], in_=stg[si][0:NS, :])), reads=[b_stg[si]], dma=True)
            for s in range(nsub if not os.environ.get('NOV') else 0):
                r = min(128, tt - s * 128)
                tok0 = t0 + s * 128
                vt = vtok[s % 2]
                for g in range(3):
                    pv = PS[4 + g % 2]

                    def mmv(e, g=g, pv=pv, s=s, r=r):
                        ins = None
                        for k in range(8):
                            ins = e.matmul(pv[0:r, :], lhsT=uT[:, k, s * 128:s * 128 + r], rhs=Wq[:, k, 3072 + g * 512:3072 + (g + 1) * 512], start=(k == 0), stop=(k == 7))
                        return ins
                    S.add("pe", mmv, reads=[b_w, b_uT], writes=[b_ps[4 + g % 2]])
                    S.add("act", (lambda e, g=g, pv=pv, r=r, vt=vt: e.copy(out=vt[0:r, g * 512:(g + 1) * 512], in_=pv[0:r, :])),
                          reads=[b_ps[4 + g % 2]], writes=[b_vt[s % 2]])
                    inwin = ((not is_p) or (tok0 >= T - W_G[g])) and not os.environ.get('NOKV')
                    if inwin:
                        if is_p:
                            dk = kv_p[g][tok0 - (T - W_G[g]):tok0 - (T - W_G[g]) + r, 0, :]
                            dv = kv_p[g][tok0 - (T - W_G[g]):tok0 - (T - W_G[g]) + r, 1, :]
                        else:
                            dk = kv_s[g][0:r, 0, :]
                            dv = kv_s[g][0:r, 1, :]
                        si = stg_i[0] % 2
                        stg_i[0] += 1
                        S.add("act", (lambda e, pv=pv, r=r, si=si: e.copy(out=stg[si][0:r, :], in_=pv[0:r, :])),
                              reads=[b_ps[4 + g % 2]], writes=[b_stg[si]])
                        S.add("sp", (lambda e, dv=dv, r=r, si=si: e.dma_start(out=dv, in_=stg[si][0:r, :])), reads=[b_stg[si]], dma=True)
                        pt = PS[6 + g % 2]
                        ptv = pt[:].bitcast(BF16)[:, 0:512]

                        def trk(e, g=g, s=s, r=r, ptv=ptv):
                            ins = None
                            for c in range(4):
                                ins = e.transpose(out=ptv[0:r, c * 128:(c + 1) * 128], in_=qk[:, 12 + g * 4 + c, s * 128:s * 128 + r], identity=ident[:, :])
                            return ins
                        S.add("pe", trk, reads=[b_qk, b_ident], writes=[b_ps[6 + g % 2]])
                        si = stg_i[0] % 2
                        stg_i[0] += 1
                        S.add("dve", (lambda e, ptv=ptv, r=r, si=si: e.tensor_copy(out=stg[si][0:r, :], in_=ptv[0:r, :])),
                              reads=[b_ps[6 + g % 2]], writes=[b_stg[si]])
                        S.add("sp", (lambda e, dk=dk, r=r, si=si: e.dma_start(out=dk, in_=stg[si][0:r, :])), reads=[b_stg[si]], dma=True)
                S.add("pool", (lambda e, vt=vt, r=r, row0=row0, s=s: e.dma_start(out=V[row0 + s * 128:row0 + s * 128 + r, :], in_=vt[0:r, :])),
                      reads=[b_vt[s % 2]], dma=True)
        S.barrier()

    if "B" in phases:
        phase_b()

    def phase_a2():
        A = Alloc()
        Wz = A.bf(8 * 2048).rearrange("p (k n) -> p k n", k=8)
        Wdt = A.bf(8 * 32).rearrange("p (k n) -> p k n", k=8)
        Wgs = A.bf(8 * D).rearrange("p (k n) -> p k n", k=8)
        Wo = A.bf(16 * D).rearrange("p (k n) -> p k n", k=16)
        cm = A.f32(640)
        triu, strict, ones = cm[:, 0:128], cm[:, 256:384], cm[:, 512:640]
        dtb = A.f32(32)
        abc = A.f32(32)
        dsk = A.f32(32)
        gss = A.f32(2048)
        A.off_after_consts = A.off
        st_off = A.off
        st = A.f32(2048)
        stage = [st[:, 0:1024], st[:, 1024:2048]]
        b_w = S.buf("w")
        b_st = [S.buf("st0"), S.buf("st1")]
        load_weight(lambda k, c0, c: Wz[:, k, c0:c0 + c], w_in, 8, 0, 2048, b_w, b_st, stage)
        load_weight(lambda k, c0, c: Wdt[:, k, c0:c0 + c], w_in, 8, 5120, 32, b_w, b_st, stage)
        load_weight(lambda k, c0, c: Wgs[:, k, c0:c0 + c], w_in, 8, 9760, 1024, b_w, b_st, stage)
        load_weight(lambda k, c0, c: Wo[:, k, c0:c0 + c], w_o_ssm, 16, 0, 1024, b_w, b_st, stage)
        b_c = S.buf("consts")
        S.add("sp", lambda e: e.dma_start(out=cm, in_=cmask_d[:, :]), writes=[b_c], dma=True)
        load_bc(dtb, dt_bias, 32, b_c)
        load_bc(abc, a_log, 32, b_c)
        load_bc(dsk, d_skip, 32, b_c)
        load_bc(gss, norm_ssm, 2048, b_c)
        S.add("act", lambda e: e.activation(out=abc, in_=abc, func=AF.Exp), reads=[b_c], writes=[b_c])
        S.add("dve", lambda e: e.tensor_scalar(out=abc, in0=abc, scalar1=-1.0, scalar2=None, op0=ALU.mult), reads=[b_c], writes=[b_c])
        S.barrier()
        b_w = S.buf("w")
        b_c = S.buf("consts")
        A.off = st_off
        A_mark = A.off_after_consts
        A.off = A.off_after_consts
        F = []
        for i in range(2):
            f = dict(uT=A.bf(1024).rearrange("p (k t) -> p k t", k=8), xcT=A.bf(24 * 128).rearrange("p (m t) -> p m t", m=24), xtok=A.bf(2560),
                     dt=A.f32(32), dta=A.f32(32), eacs=A.f32(32), dA=A.f32(32), zs=A.bf(2048), decT=A.bf(4096), cbm=A.bf(512))
            f["B"] = {n: S.buf(n + str(i)) for n in ("uT", "xcT", "xtok", "dt", "dta", "eacs", "dA", "zs", "decT", "cbm")}
            F.append(f)
        Rm = A.f32(2048)
        wts = A.bf(4096)
        xdt = A.bf(2048)
        xw = A.bf(2048)
        stT = A.f32(2048)
        stB = A.bf(2048)
        y = A.f32(2048)
        tmp = A.f32(512)
        tmp2 = A.f32(512)
        yn = A.bf(2048)
        ynT = A.bf(2048).rearrange("p (k t) -> p k t", k=16)
        sg = A.bf(128)
        MSc = A.bf(1024).rearrange("p (k t) -> p k t", k=8)
        nm = ["Rm", "wts", "xdt", "xw", "stT", "stB", "y", "tmp", "tmp2", "yn", "ynT", "sg", "MSc", "ss"]
        B = {n: S.buf(n) for n in nm}
        b_ps = new_ps()
        ssv = CP[:, 16:17]
        Uv = U.rearrange("(k p) t -> p k t", p=128)
        XCv = XC.rearrange("(m p) t -> p m t", p=128)
        MSv = MS.rearrange("(k p) t -> p k t", p=128)
        S.add("dve", lambda e: e.memset(stT, 0.0), writes=[B["stT"]])
        S.add("dve", lambda e: e.memset(stB, 0.0), writes=[B["stB"]])
        bc3 = lambda ap, n1, n2: ap.unsqueeze(2).to_broadcast([128, n1, n2])

        def front(c):
            f = F[c % 2]
            FB = f["B"]
            uT, xcT, xtok, dt, dta, eacs, dA, zs, decT, cbm = (f[k] for k in ("uT", "xcT", "xtok", "dt", "dta", "eacs", "dA", "zs", "decT", "cbm"))
            c0 = c * 128
            S.add("sp", (lambda e: e.dma_start(out=uT, in_=Uv[:, :, c0:c0 + 128])), writes=[FB["uT"]], dma=True)
            S.add("sp", (lambda e: e.dma_start(out=xcT, in_=XCv[:, :, c0:c0 + 128])), writes=[FB["xcT"]], dma=True)
            for g3 in range(3):
                nck = 8 if g3 < 2 else 4
                ptv = PS[3][:].bitcast(BF16)[:, 0:1024]

                def tr(e, g3=g3, nck=nck, ptv=ptv):
                    ins = None
                    for j in range(nck):
                        ins = e.transpose(out=ptv[:, j * 128:(j + 1) * 128], in_=xcT[:, g3 * 8 + j, :], identity=ident[:, :])
                    return ins
                S.add("pe", tr, reads=[FB["xcT"], b_ident], writes=[b_ps[3]])
                S.add("act", (lambda e, g3=g3, nck=nck, ptv=ptv: e.copy(out=xtok[:, g3 * 1024:g3 * 1024 + nck * 128], in_=ptv[:, 0:nck * 128])), reads=[b_ps[3]], writes=[FB["xtok"]])

            def mmdt(e):
                ins = None
                for k in range(8):
                    ins = e.matmul(PS[0][:, 0:32], lhsT=uT[:, k, :], rhs=Wdt[:, k, :], start=(k == 0), stop=(k == 7))
                return ins
            S.add("pe", mmdt, reads=[b_w, FB["uT"]], writes=[b_ps[0]])
            S.add("dve", lambda e: e.tensor_tensor(out=dt, in0=PS[0][:, 0:32], in1=dtb, op=ALU.add), reads=[b_ps[0], b_c], writes=[FB["dt"]])
            S.add("act", lambda e: e.activation(out=dt, in_=dt, func=AF.Exp), reads=[FB["dt"]], writes=[FB["dt"]])
            S.add("act", lambda e: e.activation(out=dt, in_=dt, func=AF.Ln, bias=1.0), reads=[FB["dt"]], writes=[FB["dt"]])
            S.add("dve", lambda e: e.tensor_tensor(out=dta, in0=dt, in1=abc, op=ALU.mult), reads=[FB["dt"], b_c], writes=[FB["dta"]])
            S.add("pe", lambda e: e.matmul(PS[0][:, 32:64], lhsT=triu, rhs=dta, start=True, stop=True), reads=[FB["dta"], b_c], writes=[b_ps[0]])
            S.add("act", lambda e: e.activation(out=eacs, in_=PS[0][:, 32:64], func=AF.Exp), reads=[b_ps[0]], writes=[FB["eacs"]])
            S.add("pe", lambda e: e.matmul(PS[0][:, 64:96], lhsT=ones, rhs=dta, start=True, stop=True), reads=[FB["dta"], b_c], writes=[b_ps[0]])
            S.add("act", lambda e: e.activation(out=dA, in_=PS[0][:, 64:96], func=AF.Exp), reads=[b_ps[0]], writes=[FB["dA"]])
            for h2 in range(2):
                S.add("dve", (lambda e, h2=h2: e.tensor_tensor(out=Rm.rearrange("p (h l) -> p h l", h=16), in0=bc3(dta[:, h2 * 16:(h2 + 1) * 16], 16, 128), in1=triu.unsqueeze(1).to_broadcast([128, 16, 128]), op=ALU.mult)),
                      reads=[FB["dta"], b_c], writes=[B["Rm"]])
                for hq in range(4):
                    hb = h2 * 4 + hq
                    pb = PS[2]
                    S.add("pe", (lambda e, hq=hq, pb=pb: e.matmul(pb[:, :], lhsT=strict, rhs=Rm[:, hq * 512:(hq + 1) * 512], start=True, stop=True)), reads=[B["Rm"], b_c], writes=[b_ps[2]])
                    S.add("act", (lambda e, hb=hb, pb=pb: e.activation(out=decT[:, hb * 512:(hb + 1) * 512], in_=pb[:, :], func=AF.Exp)), reads=[b_ps[2]], writes=[FB["decT"]])

            def mmcb(e):
                ins = None
                for g in range(4):
                    ins = e.matmul(PS[1][:, g * 128:(g + 1) * 128], lhsT=xcT[:, 16 + g, :], rhs=xcT[:, 20 + g, :], start=True, stop=True)
                return ins
            S.add("pe", mmcb, reads=[FB["xcT"]], writes=[b_ps[1]])
            S.add("dve", lambda e: e.tensor_tensor(out=cbm.rearrange("p (g l) -> p g l", g=4), in0=PS[1][:, :].rearrange("p (g l) -> p g l", g=4), in1=triu.unsqueeze(1).to_broadcast([128, 4, 128]), op=ALU.mult),
                  reads=[b_ps[1], b_c], writes=[FB["cbm"]])
            for nb in range(4):
                pz = PS[3]

                def mmz(e, nb=nb, pz=pz):
                    ins = None
                    for k in range(8):
                        ins = e.matmul(pz[:, :], lhsT=uT[:, k, :], rhs=Wz[:, k, nb * 512:(nb + 1) * 512], start=(k == 0), stop=(k == 7))
                    return ins
                S.add("pe", mmz, reads=[b_w, FB["uT"]], writes=[b_ps[3]])
                S.add("act", (lambda e, nb=nb, pz=pz: e.activation(out=zs[:, nb * 512:(nb + 1) * 512], in_=pz[:, :], func=AF.Silu)), reads=[b_ps[3]], writes=[FB["zs"]])

        def back(c):
            f = F[c % 2]
            FB = f["B"]
            uT, xcT, xtok, dt, dta, eacs, dA, zs, decT, cbm = (f[k] for k in ("uT", "xcT", "xtok", "dt", "dta", "eacs", "dA", "zs", "decT", "cbm"))
            c0 = c * 128
            S.add("dve", lambda e: e.tensor_tensor(out=wts.rearrange("p (g h l) -> p g h l", g=4, h=8), in0=decT.rearrange("p (g h l) -> p g h l", g=4, h=8),
                                                    in1=cbm.rearrange("p (g l) -> p g l", g=4).unsqueeze(2).to_broadcast([128, 4, 8, 128]), op=ALU.mult),
                  reads=[FB["decT"], FB["cbm"]], writes=[B["wts"]])
            S.add("dve", lambda e: e.tensor_tensor(out=xdt.rearrange("p (h e) -> p h e", h=32), in0=xtok[:, 0:2048].rearrange("p (h e) -> p h e", h=32), in1=bc3(dt, 32, 64), op=ALU.mult),
                  reads=[FB["xtok"], FB["dt"]], writes=[B["xdt"]])
            S.add("dve", lambda e: e.tensor_tensor(out=xw.rearrange("p (h e) -> p h e", h=32), in0=xdt.rearrange("p (h e) -> p h e", h=32),
                                                     in1=decT.rearrange("p (h l) -> p h l", h=32)[:, :, 127:128].to_broadcast([128, 32, 64]), op=ALU.mult),
                  reads=[B["xdt"], FB["decT"]], writes=[B["xw"]])
            for g in range(4):
                pa, pb = PS[4], PS[5]

                def mmy(e, g=g, pa=pa, pb=pb):
                    ins = None
                    for hh in range(8):
                        h = g * 8 + hh
                        ins = e.matmul(pa[:, hh * 64:(hh + 1) * 64], lhsT=wts[:, h * 128:(h + 1) * 128], rhs=xdt[:, h * 64:(h + 1) * 64], start=True, stop=True)
                    ins = e.matmul(pb[:, :], lhsT=xcT[:, 20 + g, :], rhs=stB[:, g * 512:(g + 1) * 512], start=True, stop=True)
                    return ins
                S.add("pe", mmy, reads=[B["wts"], B["xdt"], FB["xcT"], B["stB"]], writes=[b_ps[4], b_ps[5]])
                S.add("dve", (lambda e, g=g, pb=pb: e.tensor_tensor(out=tmp.rearrange("p (h e) -> p h e", h=8), in0=pb[:, :].rearrange("p (h e) -> p h e", h=8), in1=bc3(eacs[:, g * 8:(g + 1) * 8], 8, 64), op=ALU.mult)),
                      reads=[b_ps[5], FB["eacs"]], writes=[B["tmp"]])
                S.add("dve", (lambda e, g=g, pa=pa: e.tensor_tensor(out=y[:, g * 512:(g + 1) * 512], in0=pa[:, :], in1=tmp, op=ALU.add)), reads=[b_ps[4], B["tmp"]], writes=[B["y"]])
                S.add("pool", (lambda e, g=g: e.tensor_tensor(out=tmp2.rearrange("p (h e) -> p h e", h=8), in0=xtok[:, g * 512:(g + 1) * 512].rearrange("p (h e) -> p h e", h=8), in1=bc3(dsk[:, g * 8:(g + 1) * 8], 8, 64), op=ALU.mult)),
                      reads=[FB["xtok"], b_c], writes=[B["tmp2"]])
                S.add("pool", (lambda e, g=g: e.tensor_tensor(out=y[:, g * 512:(g + 1) * 512], in0=y[:, g * 512:(g + 1) * 512], in1=tmp2, op=ALU.add)), reads=[B["y"], B["tmp2"]], writes=[B["y"]])
            for g in range(4):
                pc = PS[7]
                S.add("pe", (lambda e, g=g, pc=pc: e.matmul(pc[:, :], lhsT=xtok[:, 2048 + g * 128:2048 + (g + 1) * 128], rhs=xw[:, g * 512:(g + 1) * 512], start=True, stop=True)),
                      reads=[FB["xtok"], B["xw"]], writes=[b_ps[7]])
                sv = stT[:, g * 512:(g + 1) * 512]
                S.add("dve", (lambda e, g=g, sv=sv: e.tensor_tensor(out=sv.rearrange("p (h e) -> p h e", h=8), in0=sv.rearrange("p (h e) -> p h e", h=8), in1=bc3(dA[:, g * 8:(g + 1) * 8], 8, 64), op=ALU.mult)),
                      reads=[B["stT"], FB["dA"]], writes=[B["stT"]])
                S.add("dve", (lambda e, sv=sv, pc=pc: e.tensor_tensor(out=sv, in0=pc[:, :], in1=sv, op=ALU.add)), reads=[b_ps[7], B["stT"]], writes=[B["stT"]])
                S.add("act", (lambda e, g=g, sv=sv: e.copy(out=stB[:, g * 512:(g + 1) * 512], in_=sv)), reads=[B["stT"]], writes=[B["stB"]])
            S.add("dve", lambda e: e.tensor_tensor(out=y, in0=y, in1=zs, op=ALU.mult), reads=[B["y"], FB["zs"]], writes=[B["y"]])
            norm_T(y, B["y"], 128, gss, b_c, ssv, B["ss"], yn, B["yn"], yn, B["yn"], PS[6], b_ps[6], ynT, B["ynT"], 0, nk=16)
            for m in range(8):
                pa, pb = PS[4], PS[5]

                def mmo(e, m=m, pa=pa, pb=pb):
                    ins = None
                    for k in range(16):
                        ins = e.matmul(pa[:, 0:128], lhsT=Wo[:, k, m * 128:(m + 1) * 128], rhs=ynT[:, k, :], start=(k == 0), stop=(k == 15))
                    for k in range(8):
                        ins = e.matmul(pb[:, 0:128], lhsT=Wgs[:, k, m * 128:(m + 1) * 128], rhs=uT[:, k, :], start=(k == 0), stop=(k == 7))
                    return ins
                S.add("pe", mmo, reads=[b_w, B["ynT"], FB["uT"]], writes=[b_ps[4], b_ps[5]])
                S.add("act", (lambda e, pb=pb: e.activation(out=sg, in_=pb[:, 0:128], func=AF.Sigmoid)), reads=[b_ps[5]], writes=[B["sg"]])
                S.add("dve", (lambda e, m=m, pa=pa: e.tensor_tensor(out=MSc[:, m, :], in0=pa[:, 0:128], in1=sg, op=ALU.mult)), reads=[b_ps[4], B["sg"]], writes=[B["MSc"]])
            S.add("pool", (lambda e: e.dma_start(out=MSv[:, :, c0:c0 + 128], in_=MSc)), reads=[B["MSc"]], dma=True)

        NCH = T // 128
        front(0)
        for c in range(NCH):
            if c + 1 < NCH:
                front(c + 1)
            back(c)
        for j in range(16):
            pt = PS[j % 2]
            S.add("pe", (lambda e, j=j, pt=pt: e.transpose(out=pt[:, 0:128], in_=stT[:, j * 128:(j + 1) * 128], identity=ident_f[:, :])), reads=[B["stT"]], writes=[b_ps[j % 2]])
            S.add("act", (lambda e, j=j, pt=pt: e.copy(out=Rm[:, j * 128:(j + 1) * 128], in_=pt[:, 0:128])), reads=[b_ps[j % 2]], writes=[B["Rm"]])
        S.add("sp", lambda e: e.dma_start(out=ssm_p.rearrange("(j p) n -> p j n", p=128), in_=Rm[:, 0:2048].rearrange("p (j n) -> p j n", j=16)), reads=[B["Rm"]], dma=True)
        S.barrier()

        A.off = A_mark
        uTs = A.bf(128).rearrange("p (k t) -> p k t", k=8)
        xcs = A.bf(24 * NS).rearrange("p (m t) -> p m t", m=24)
        xsf = A.f32(256).rearrange("p (j b) -> p j b", j=16)
        Ex = A.f32(2048)
        cols = A.f32(4)
        dtT = A.f32(16)
        decs = A.f32(16)
        dtx = A.f32(256).rearrange("p (j b) -> p j b", j=16)
        decx = A.f32(256).rearrange("p (j b) -> p j b", j=16)
        dskx = A.f32(16)
        xdtx = A.f32(256).rearrange("p (j b) -> p j b", j=16)
        bctok = A.f32(1024)
        bcbc = [A.f32(1024), A.f32(1024)]
        stt = [A.f32(2048).rearrange("p (j n) -> p j n", j=16) for _ in range(2)]
        tA = A.f32(512).rearrange("p (j n) -> p j n", j=4)
        tB = A.f32(512).rearrange("p (j n) -> p j n", j=4)
        ycol = A.f32(256).rearrange("p (j b) -> p j b", j=16)
        zsT = A.f32(256).rearrange("p (j b) -> p j b", j=16)
        yg = A.f32(256).rearrange("p (j b) -> p j b", j=16)
        sq = A.f32(256).rearrange("p (j b) -> p j b", j=16)
        red = A.f32(16)
        rstd = A.f32(16)
        gsT = A.f32(16)
        ynTs = A.bf(256).rearrange("p (k t) -> p k t", k=16)
        sgs = A.bf(16)
        MSs = A.bf(128).rearrange("p (k t) -> p k t", k=8)
        nm2 = ["uTs", "xcs", "xsf", "Ex", "cols", "dtT", "decs", "dtx", "decx", "dskx", "xdtx", "bctok", "bc0", "bc1", "st0", "st1", "tA", "tB", "ycol", "zsT", "yg", "sq", "red", "rstd", "gsT", "ynTs", "sgs", "MSs"]
        Bs = {n: S.buf(n) for n in nm2}
        b_w2 = S.buf("w2")
        b_ps = new_ps()
        S.add("sp", lambda e: e.dma_start(out=uTs, in_=Uv[:, :, T:T + NS]), writes=[Bs["uTs"]], dma=True)
        S.add("sp", lambda e: e.dma_start(out=xcs, in_=XCv[:, :, T:T + NS]), writes=[Bs["xcs"]], dma=True)
        S.add("sp", lambda e: e.dma_start(out=Ex[0:32, :], in_=expand_d[:, :]), writes=[Bs["Ex"]], dma=True)
        S.add("sp", lambda e: e.dma_start(out=cols[0:32, 0:1], in_=dt_biasT[:, :]), writes=[Bs["cols"]], dma=True)
        S.add("sp", lambda e: e.dma_start(out=cols[0:32, 1:2], in_=a_logT[:, :]), writes=[Bs["cols"]], dma=True)
        S.add("sp", lambda e: e.dma_start(out=cols[0:32, 2:3], in_=d_skipT[:, :]), writes=[Bs["cols"]], dma=True)
        S.add("sp", lambda e: e.dma_start(out=gsT, in_=norm_ssmT[:, :]), writes=[Bs["gsT"]], dma=True)
        S.add("act", lambda e: e.activation(out=cols[0:32, 1:2], in_=cols[0:32, 1:2], func=AF.Exp), reads=[Bs["cols"]], writes=[Bs["cols"]])
        S.add("dve", lambda e: e.tensor_scalar(out=cols[0:32, 1:2], in0=cols[0:32, 1:2], scalar1=-1.0, scalar2=None, op0=ALU.mult), reads=[Bs["cols"]], writes=[Bs["cols"]])

        def mmdts(e):
            ins = None
            for k in range(8):
                ins = e.matmul(PS[0][0:32, 0:NS], lhsT=Wdt[:, k, :], rhs=uTs[:, k, :], start=(k == 0), stop=(k == 7))
            return ins
        S.add("pe", mmdts, reads=[b_w2, Bs["uTs"]], writes=[b_ps[0]])
        S.add("act", lambda e: e.activation(out=dtT[0:32, :], in_=PS[0][0:32, 0:NS], func=AF.Exp, bias=cols[0:32, 0:1]), reads=[b_ps[0], Bs["cols"]], writes=[Bs["dtT"]])
        S.add("act", lambda e: e.activation(out=dtT[0:32, :], in_=dtT[0:32, :], func=AF.Ln, bias=1.0), reads=[Bs["dtT"]], writes=[Bs["dtT"]])
        S.add("act", lambda e: e.activation(out=decs[0:32, :], in_=dtT[0:32, :], func=AF.Exp, scale=cols[0:32, 1:2]), reads=[Bs["dtT"], Bs["cols"]], writes=[Bs["decs"]])

        def mmex(e):
            ins = None
            for j in range(16):
                ins = e.matmul(PS[1][:, j * NS:(j + 1) * NS], lhsT=Ex[0:32, j * 128:(j + 1) * 128], rhs=dtT[0:32, :], start=True, stop=True)
                ins = e.matmul(PS[1][:, 256 + j * NS:256 + (j + 1) * NS], lhsT=Ex[0:32, j * 128:(j + 1) * 128], rhs=decs[0:32, :], start=True, stop=True)
                ins = e.matmul(PS[2][:, j:j + 1], lhsT=Ex[0:32, j * 128:(j + 1) * 128], rhs=cols[0:32, 2:3], start=True, stop=True)
            return ins
        S.add("pe", mmex, reads=[Bs["Ex"], Bs["dtT"], Bs["decs"], Bs["cols"]], writes=[b_ps[1], b_ps[2]])
        S.add("dve", lambda e: e.tensor_copy(out=dtx, in_=PS[1][:, 0:256].rearrange("p (j b) -> p j b", j=16)), reads=[b_ps[1]], writes=[Bs["dtx"]])
        S.add("dve", lambda e: e.tensor_copy(out=decx, in_=PS[1][:, 256:512].rearrange("p (j b) -> p j b", j=16)), reads=[b_ps[1]], writes=[Bs["decx"]])
        S.add("dve", lambda e: e.tensor_copy(out=dskx, in_=PS[2][:, 0:16]), reads=[b_ps[2]], writes=[Bs["dskx"]])
        S.add("dve", lambda e: e.tensor_copy(out=xsf, in_=xcs[:, 0:16, :]), reads=[Bs["xcs"]], writes=[Bs["xsf"]])
        S.add("dve", lambda e: e.tensor_tensor(out=xdtx, in0=xsf, in1=dtx, op=ALU.mult), reads=[Bs["xsf"], Bs["dtx"]], writes=[Bs["xdtx"]])
        for half in range(2):
            ptv = PS[6 + half][:].bitcast(BF16)[:, 0:512]

            def trbc(e, half=half, ptv=ptv):
                ins = None
                for i in range(4):
                    ins = e.transpose(out=ptv[0:NS, i * 128:(i + 1) * 128], in_=xcs[:, 16 + half * 4 + i, :], identity=ident[:, :])
                return ins
            S.add("pe", trbc, reads=[Bs["xcs"], b_ident], writes=[b_ps[6 + half]])
            S.add("dve", (lambda e, half=half, ptv=ptv: e.tensor_copy(out=bctok[0:NS, half * 512:(half + 1) * 512], in_=ptv[0:NS, :])), reads=[b_ps[6 + half]], writes=[Bs["bctok"]])
        S.add("sp", lambda e: e.dma_start(out=BCS[:, :], in_=bctok[0:NS, :]), reads=[Bs["bctok"]], writes=[Bs["bctok"]], dma=True)
        ssm_sv = ssm_s.rearrange("b (j p) n -> b p j n", p=128)
        ssm_iv = state_ssm.rearrange("b (j p) n -> b p j n", p=128)
        for b in range(NS):
            i2 = b % 2
            bcb, stb = bcbc[i2], stt[i2]
            S.add("pool", (lambda e, b=b, bcb=bcb: e.dma_start(out=bcb, in_=BCS[b:b + 1, :].to_broadcast([128, 1024]))), reads=[Bs["bctok"]], writes=[Bs[f"bc{i2}"]], dma=True)
            S.add("sp", (lambda e, b=b, stb=stb: e.dma_start(out=stb, in_=ssm_iv[b])), writes=[Bs[f"st{i2}"]], dma=True)
            for g in range(4):
                js = slice(4 * g, 4 * g + 4)
                Bg = bcb[:, g * 128:(g + 1) * 128].unsqueeze(1).to_broadcast([128, 4, 128])
                Cg = bcb[:, 512 + g * 128:512 + (g + 1) * 128].unsqueeze(1).to_broadcast([128, 4, 128])
                S.add("dve", (lambda e, b=b, js=js, Bg=Bg: e.tensor_tensor(out=tA, in0=xdtx[:, js, b:b + 1].to_broadcast([128, 4, 128]), in1=Bg, op=ALU.mult)),
                      reads=[Bs["xdtx"], Bs[f"bc{i2}"]], writes=[Bs["tA"]])
                S.add("dve", (lambda e, b=b, js=js, stb=stb: e.tensor_tensor(out=stb[:, js, :], in0=stb[:, js, :], in1=decx[:, js, b:b + 1].to_broadcast([128, 4, 128]), op=ALU.mult)),
                      reads=[Bs[f"st{i2}"], Bs["decx"]], writes=[Bs[f"st{i2}"]])
                S.add("dve", (lambda e, js=js, stb=stb: e.tensor_tensor(out=stb[:, js, :], in0=stb[:, js, :], in1=tA, op=ALU.add)),
                      reads=[Bs[f"st{i2}"], Bs["tA"]], writes=[Bs[f"st{i2}"]])
                S.add("pool", (lambda e, js=js, stb=stb, Cg=Cg: e.tensor_tensor(out=tB, in0=stb[:, js, :], in1=Cg, op=ALU.mult)),
                      reads=[Bs[f"st{i2}"], Bs[f"bc{i2}"]], writes=[Bs["tB"]])
                S.add("dve", (lambda e, b=b, js=js: e.tensor_reduce(out=ycol[:, js, b], in_=tB, axis=AX.X, op=ALU.add)), reads=[Bs["tB"]], writes=[Bs["ycol"]])
            S.add("pool", (lambda e, b=b, stb=stb: e.dma_start(out=ssm_sv[b], in_=stb)), reads=[Bs[f"st{i2}"]], dma=True)
        S.add("dve", lambda e: e.tensor_tensor(out=yg, in0=xsf, in1=dskx.unsqueeze(2).to_broadcast([128, 16, NS]), op=ALU.mult), reads=[Bs["xsf"], Bs["dskx"]], writes=[Bs["yg"]])
        S.add("dve", lambda e: e.tensor_tensor(out=yg, in0=yg, in1=ycol, op=ALU.add), reads=[Bs["yg"], Bs["ycol"]], writes=[Bs["yg"]])
        for j in range(16):
            def mmzs(e, j=j):
                ins = None
                for k in range(8):
                    ins = e.matmul(PS[3][:, j * NS:(j + 1) * NS], lhsT=Wz[:, k, j * 128:(j + 1) * 128], rhs=uTs[:, k, :], start=(k == 0), stop=(k == 7))
                return ins
            S.add("pe", mmzs, reads=[b_w2, Bs["uTs"]], writes=[b_ps[3]])
        S.add("act", lambda e: e.activation(out=zsT, in_=PS[3][:, 0:256].rearrange("p (j b) -> p j b", j=16), func=AF.Silu), reads=[b_ps[3]], writes=[Bs["zsT"]])
        S.add("dve", lambda e: e.tensor_tensor(out=yg, in0=yg, in1=zsT, op=ALU.mult), reads=[Bs["yg"], Bs["zsT"]], writes=[Bs["yg"]])
        S.add("dve", lambda e: e.tensor_tensor(out=sq, in0=yg, in1=yg, op=ALU.mult), reads=[Bs["yg"]], writes=[Bs["sq"]])
        S.add("dve", lambda e: e.tensor_reduce(out=red, in_=sq.rearrange("p j b -> p b j"), axis=AX.X, op=ALU.add), reads=[Bs["sq"]], writes=[Bs["red"]])
        S.add("pe", lambda e: e.matmul(PS[4][:, 0:NS], lhsT=ones, rhs=red, start=True, stop=True), reads=[Bs["red"]], writes=[b_ps[4]])
        S.add("dve", lambda e: e.tensor_scalar(out=rstd, in0=PS[4][:, 0:NS], scalar1=1.0 / 2048, scalar2=EPS, op0=ALU.mult, op1=ALU.add), reads=[b_ps[4]], writes=[Bs["rstd"]])
        S.add("pool", lambda e: e.tensor_tensor(out=rstd, in0=rstd, in1=mhalf.to_broadcast([128, NS]), op=ALU.pow), reads=[Bs["rstd"]], writes=[Bs["rstd"]])
        S.add("dve", lambda e: e.tensor_tensor(out=yg, in0=yg, in1=rstd.unsqueeze(1).to_broadcast([128, 16, NS]), op=ALU.mult), reads=[Bs["yg"], Bs["rstd"]], writes=[Bs["yg"]])
        S.add("dve", lambda e: e.tensor_tensor(out=ynTs, in0=yg, in1=gsT.unsqueeze(2).to_broadcast([128, 16, NS]), op=ALU.mult), reads=[Bs["yg"], Bs["gsT"]], writes=[Bs["ynTs"]])
        for m in range(8):
            pa, pb = PS[m % 2], PS[2 + m % 2]

            def mmos(e, m=m, pa=pa, pb=pb):
                ins = None
                for k in range(16):
                    ins = e.matmul(pa[:, 0:NS], lhsT=Wo[:, k, m * 128:(m + 1) * 128], rhs=ynTs[:, k, :], start=(k == 0), stop=(k == 15))
                for k in range(8):
                    ins = e.matmul(pb[:, 0:NS], lhsT=Wgs[:, k, m * 128:(m + 1) * 128], rhs=uTs[:, k, :], start=(k == 0), stop=(k == 7))
                return ins
            S.add("pe", mmos, reads=[b_w2, Bs["ynTs"], Bs["uTs"]], writes=[b_ps[m % 2], b_ps[2 + m % 2]])
            S.add("act", (lambda e, pb=pb: e.activation(out=sgs, in_=pb[:, 0:NS], func=AF.Sigmoid)), reads=[b_ps[2 + m % 2]], writes=[Bs["sgs"]])
            S.add("dve", (lambda e, m=m, pa=pa: e.tensor_tensor(out=MSs[:, m, :], in0=pa[:, 0:NS], in1=sgs, op=ALU.mult)), reads=[b_ps[m % 2], Bs["sgs"]], writes=[Bs["MSs"]])
        S.add("pool", lambda e: e.dma_start(out=MSv[:, :, T:T + NS], in_=MSs), reads=[Bs["MSs"]], dma=True)
        S.barrier()

    if "A2" in phases:
        phase_a2()

    def phase_c():
        A = Alloc()
        WMAX = 2048
        QTs = A.bf(4 * WMAX)
        KTs = [A.bf(4 * WMAX), A.bf(4 * WMAX)]
        Vb = [A.bf(16 * 520), A.bf(16 * 520)]
        Pt = [A.bf(2048), A.bf(2048)]
        maskb = A.bf(2048)
        mstage = A.f32(256)
        Osb = [A.f32(520), A.f32(520)]
        b_q = S.buf("q")
        b_k = [S.buf("k0"), S.buf("k1")]
        b_v = [S.buf("v0"), S.buf("v1")]
        b_pt = [S.buf("pt0"), S.buf("pt1")]
        b_mask = S.buf("mask")
        b_ms = S.buf("mstage")
        b_o = [S.buf("o0"), S.buf("o1")]
        b_ps = new_ps()
        S.add("sp", lambda e: e.dma_start(out=mstage[:, 0:128], in_=cmask_d[:, 128:256]), writes=[b_ms], dma=True)
        S.add("sp", lambda e: e.dma_start(out=mstage[:, 128:256], in_=cmask_d[:, 0:128]), writes=[b_ms], dma=True)
        mv = maskb.rearrange("p (h c) -> p h c", h=8)
        for h in range(8):
            S.add("dve", (lambda e, h=h: e.tensor_copy(out=mv[:, h, :], in_=mstage[:, :])), reads=[b_ms], writes=[b_mask])
        for i in range(2):
            vv = Vb[i].rearrange("p (r h e) -> p r h e", r=16, h=8)
            S.add("dve", (lambda e, vv=vv: e.memset(vv[:, :, :, 64:65], 1.0)), writes=[b_v[i]])
        QTv = QT.rearrange("(m p) t -> p m t", p=128)
        KTv = KT.rearrange("(m p) t -> p m t", p=128)
        blk = [0]
        for g in [int(c) for c in os.environ.get('CG', '012')]:
            W = W_G[g]
            d = DIL[g]
            qs = QTs[:, 0:4 * W].rearrange("p (c t) -> p c t", c=4)
            ks = [KTs[i][:, 0:4 * W].rearrange("p (c t) -> p c t", c=4) for i in range(2)]
            for nb in range(T // W):
                cur = nb % 2
                prv = 1 - cur
                t0 = nb * W
                S.add("sp", (lambda e, qs=qs, g=g, t0=t0, W=W: e.dma_start(out=qs, in_=QTv[:, 4 * g:4 * g + 4, t0:t0 + W])), writes=[b_q], dma=True)
                S.add("sp", (lambda e, kc=ks[cur], g=g, t0=t0, W=W: e.dma_start(out=kc, in_=KTv[:, 4 * g:4 * g + 4, t0:t0 + W])), writes=[b_k[cur]], dma=True)
                vcur = Vb[cur].rearrange("p (r h e) -> p r h e", r=16, h=8)
                vprv = Vb[prv].rearrange("p (r h e) -> p r h e", r=16, h=8)
                for r in range(d):
                    src = V[t0 + r:t0 + W:d, g * 512:(g + 1) * 512].rearrange("m (h e) -> m h e", h=8)
                    S.add("pool", (lambda e, vcur=vcur, r=r, src=src: e.dma_start(out=vcur[:, r, :, 0:64], in_=src)), writes=[b_v[cur]], dma=True)
                for r in range(d):
                    bi = blk[0] % 2
                    blk[0] += 1
                    ptv = Pt[bi].rearrange("p (h c) -> p h c", h=8)
                    halves = (0, 1) if nb > 0 else (1,)

                    def scores(e, qs=qs, kcur=ks[cur], kprv=ks[prv], r=r, d=d, W=W, halves=halves):
                        ins = None
                        for h in range(8):
                            c, pr = h // 2, (h % 2) * 64
                            for hv in halves:
                                kk = kcur if hv == 1 else kprv
                                bkk = (h % 2) * 2 + h // 4
                                oo = ((h // 2) % 2) * 256 + hv * 128
                                ins = e.matmul(PS[bkk][:, oo:oo + 128],
                                               lhsT=kk[pr:pr + 64, c, r:W:d], rhs=qs[pr:pr + 64, c, r:W:d], start=True, stop=True)
                        return ins
                    S.add("pe", scores, reads=[b_q, b_k[0], b_k[1]], writes=[b_ps[0], b_ps[1], b_ps[2], b_ps[3]])
                    if os.environ.get('CSTOP') == '0':
                        continue
                    for bk in range(4):
                        if nb > 0:
                            S.add("act", (lambda e, bk=bk, bi=bi: e.activation(out=Pt[bi][:, bk * 512:(bk + 1) * 512], in_=PS[bk][:, :], func=AF.Exp, scale=0.125)),
                                  reads=[b_ps[bk]], writes=[b_pt[bi]])
                        else:
                            pin = PS[bk][:, :].rearrange("p (h c) -> p h c", h=2)[:, :, 128:256]
                            pout = Pt[bi][:, bk * 512:(bk + 1) * 512].rearrange("p (h c) -> p h c", h=2)[:, :, 128:256]
                            S.add("act", (lambda e, pin=pin, pout=pout: e.activation(out=pout, in_=pin, func=AF.Exp, scale=0.125)),
                                  reads=[b_ps[bk]], writes=[b_pt[bi]])
                    S.add("dve", (lambda e, bi=bi: e.tensor_tensor(out=Pt[bi], in0=Pt[bi], in1=maskb, op=ALU.mult)),
                          reads=[b_pt[bi], b_mask], writes=[b_pt[bi]])
                    ob = 4 + 2 * bi
                    if os.environ.get('CSTOP') == '1':
                        continue

                    def pv(e, ptv=ptv, vcur=vcur, vprv=vprv, r=r, ob=ob, halves=halves):
                        ins = None
                        for h in range(8):
                            dst = PS[ob + h // 4][:, (h % 4) * 65:(h % 4) * 65 + 65]
                            for i, hv in enumerate(halves):
                                vv = vcur if hv == 1 else vprv
                                sl = ((h % 2) * 2 + h // 4) * 2 + (h // 2) % 2
                                ins = e.matmul(dst, lhsT=ptv[:, sl, hv * 128:hv * 128 + 128], rhs=vv[:, r, h, :], start=(i == 0), stop=(i == len(halves) - 1))
                        return ins
                    S.add("pe", pv, reads=[b_pt[bi], b_v[0], b_v[1]], writes=[b_ps[ob], b_ps[ob + 1]])
                    S.add("act", (lambda e, bi=bi, ob=ob: e.copy(out=Osb[bi][:, 0:260], in_=PS[ob][:, 0:260])), reads=[b_ps[ob]], writes=[b_o[bi]])
                    S.add("dve", (lambda e, bi=bi, ob=ob: e.tensor_copy(out=Osb[bi][:, 260:520], in_=PS[ob + 1][:, 0:260])), reads=[b_ps[ob + 1]], writes=[b_o[bi]])
                    dst = AO[g][t0 + r:t0 + W:d, :]
                    S.add("sp", (lambda e, bi=bi, dst=dst: e.dma_start(out=dst, in_=Osb[bi][:, :])), reads=[b_o[bi]], dma=True)
        S.barrier()

    def phase_cs():
        A = Alloc()
        kvt = [A.f32(1024), A.f32(1024)]
        qbc = [A.f32(512), A.f32(512)]
        prod = A.f32(512)
        pvt = A.f32(520)
        sc = A.f32(8)
        selfk = A.f32(512)
        selfv = A.bf(512)
        sprod = A.f32(512)
        spv = A.f32(520)
        ssc = A.f32(8)
        ones = A.f32(1)
        osb = A.f32(520)
        b_kv = [S.buf("kv0"), S.buf("kv1")]
        b_qb = [S.buf("qb0"), S.buf("qb1")]
        b_prod = S.buf("prod")
        b_pv = S.buf("pv")
        b_sc = S.buf("sc")
        b_self = S.buf("self")
        b_sp = S.buf("sprod")
        b_spv = S.buf("spv")
        b_ssc = S.buf("ssc")
        b_ones = S.buf("ones")
        b_osb = S.buf("osb")
        b_ps = new_ps()
        S.add("dve", lambda e: e.memset(ones, 1.0), writes=[b_ones])
        it = 0
        for b in range(NS):
            for g in range(3):
                W, d = W_G[g], DIL[g]
                i2 = it % 2
                it += 1
                S.add("sp", (lambda e, i2=i2, g=g, b=b, W=W, d=d: e.dma_start(out=kvt[i2], in_=cache_kv[g][b, 0:W:d, :])), writes=[b_kv[i2]], dma=True)
                S.add("pool", (lambda e, i2=i2, g=g, b=b: e.dma_start(out=qbc[i2], in_=QKS[b:b + 1, g * 512:(g + 1) * 512].to_broadcast([128, 512]))), writes=[b_qb[i2]], dma=True)
                S.add("sp", (lambda e, g=g, b=b: e.dma_start(out=selfk[0:1, :], in_=QKS[b:b + 1, 1536 + g * 512:1536 + (g + 1) * 512])), writes=[b_self], dma=True)
                S.add("sp", (lambda e, g=g, b=b: e.dma_start(out=selfv[0:1, :], in_=V[T + b:T + b + 1, g * 512:(g + 1) * 512])), writes=[b_self], dma=True)
                S.add("dve", (lambda e, i2=i2: e.tensor_tensor(out=prod, in0=kvt[i2][:, 0:512], in1=qbc[i2], op=ALU.mult)), reads=[b_kv[i2], b_qb[i2]], writes=[b_prod])
                S.add("dve", (lambda e: e.tensor_reduce(out=sc, in_=prod.rearrange("p (h e) -> p h e", h=8), axis=AX.X, op=ALU.add)), reads=[b_prod], writes=[b_sc])
                S.add("act", (lambda e: e.activation(out=pvt[:, 512:520], in_=sc, func=AF.Exp, scale=0.125)), reads=[b_sc], writes=[b_pv])
                S.add("dve", (lambda e, i2=i2: e.tensor_tensor(out=pvt[:, 0:512].rearrange("p (h e) -> p h e", h=8), in0=kvt[i2][:, 512:1024].rearrange("p (h e) -> p h e", h=8),
                                                                in1=pvt[:, 512:520].unsqueeze(2).to_broadcast([128, 8, 64]), op=ALU.mult)), reads=[b_kv[i2], b_pv], writes=[b_pv])
                S.add("dve", (lambda e, i2=i2: e.tensor_tensor(out=sprod[0:1, :], in0=selfk[0:1, :], in1=qbc[i2][0:1, :], op=ALU.mult)), reads=[b_self, b_qb[i2]], writes=[b_sp])
                S.add("dve", (lambda e: e.tensor_reduce(out=ssc[0:1, :], in_=sprod[0:1, :].rearrange("p (h e) -> p h e", h=8), axis=AX.X, op=ALU.add)), reads=[b_sp], writes=[b_ssc])
                S.add("act", (lambda e: e.activation(out=spv[0:1, 512:520], in_=ssc[0:1, :], func=AF.Exp, scale=0.125)), reads=[b_ssc], writes=[b_spv])
                S.add("dve", (lambda e: e.tensor_tensor(out=spv[0:1, 0:512].rearrange("p (h e) -> p h e", h=8), in0=selfv[0:1, :].rearrange("p (h e) -> p h e", h=8),
                                                        in1=spv[0:1, 512:520].unsqueeze(2).to_broadcast([1, 8, 64]), op=ALU.mult)), reads=[b_self, b_spv], writes=[b_spv])

                def red(e, g=g):
                    e.matmul(PS[0][0:1, 0:512], lhsT=ones[:, 0:1], rhs=pvt[:, 0:512], start=(g == 0), stop=False)
                    e.matmul(PS[1][0:1, 0:8], lhsT=ones[:, 0:1], rhs=pvt[:, 512:520], start=(g == 0), stop=False)
                    e.matmul(PS[0][0:1, 0:512], lhsT=ones[0:1, 0:1], rhs=spv[0:1, 0:512], start=False, stop=(g == 2))
                    return e.matmul(PS[1][0:1, 0:8], lhsT=ones[0:1, 0:1], rhs=spv[0:1, 512:520], start=False, stop=(g == 2))
                S.add("pe", red, reads=[b_pv, b_spv, b_ones], writes=[b_ps[0], b_ps[1]])
            ov = osb[0:1, :].rearrange("p (h e) -> p h e", e=65)
            S.add("act", (lambda e, ov=ov: e.copy(out=ov[:, :, 0:64], in_=PS[0][0:1, 0:512].rearrange("p (h e) -> p h e", e=64))), reads=[b_ps[0]], writes=[b_osb])
            S.add("act", (lambda e, ov=ov: e.copy(out=ov[:, :, 64:65], in_=PS[1][0:1, 0:8].unsqueeze(2))), reads=[b_ps[1]], writes=[b_osb])
            S.add("pool", (lambda e, b=b: e.dma_start(out=AOS[b:b + 1, :], in_=osb[0:1, :])), reads=[b_osb], dma=True)
        S.barrier()

    if "C" in phases:
        if not os.environ.get('NOC'):
            phase_c()
        if not os.environ.get('NOCS'):
            phase_cs()

    def phase_d1():
        TT = 512
        A = Alloc()
        Wga = A.bf(8 * D).rearrange("p (k n) -> p k n", k=8)
        Woa = A.bf(4 * D).rearrange("p (k n) -> p k n", k=4)
        Wout = A.bf(8 * D).rearrange("p (k n) -> p k n", k=8)
        st = A.f32(2048)
        stage = [st[:, 0:1024], st[:, 1024:2048]]
        uT = A.bf(8 * TT).rearrange("p (k t) -> p k t", k=8)
        MSt = A.bf(8 * TT).rearrange("p (k t) -> p k t", k=8)
        attnT = A.bf(4 * TT).rearrange("p (k t) -> p k t", k=4)
        merged = A.bf(8 * TT).rearrange("p (k t) -> p k t", k=8)
        ao = [A.f32(520) for _ in range(3)]
        asum = A.f32(520)
        rl = A.f32(8)
        attn_b = A.bf(512)
        sgm = [A.bf(TT), A.bf(TT)]
        tmp = [A.f32(TT), A.f32(TT)]
        h1 = [A.f32(1024), A.f32(1024)]
        h2 = [A.f32(1024), A.f32(1024)]
        b_w = S.buf("w")
        b_st = [S.buf("st0"), S.buf("st1")]
        load_weight(lambda k, c0, c: Wga[:, k, c0:c0 + c], w_in, 8, 10784, 1024, b_w, b_st, stage)
        load_weight(lambda k, c0, c: Woa[:, k, c0:c0 + c], w_o_attn, 4, 0, 1024, b_w, b_st, stage)
        load_weight(lambda k, c0, c: Wout[:, k, c0:c0 + c], w_out, 8, 0, 1024, b_w, b_st, stage)
        b_uT, b_ms, b_at, b_mg = S.buf("uT"), S.buf("ms"), S.buf("attnT"), S.buf("merged")
        b_ao = [S.buf(f"ao{i}") for i in range(3)]
        b_as, b_rl, b_ab = S.buf("asum"), S.buf("rl"), S.buf("attn_b")
        b_sg = [S.buf("sg0"), S.buf("sg1")]
        b_tmp = [S.buf("tmp0"), S.buf("tmp1")]
        b_h1 = [S.buf("h10"), S.buf("h11")]
        b_h2 = [S.buf("h20"), S.buf("h21")]
        b_ps = new_ps()
        Uv = U.rearrange("(k p) t -> p k t", p=128)
        MSv = MS.rearrange("(k p) t -> p k t", p=128)
        for (is_p, t0, tt) in tiles_of(TT):
            row0 = t0 if is_p else T
            nsub = (tt + 127) // 128
            S.add("sp", (lambda e, row0=row0, tt=tt: e.dma_start(out=uT[:, :, 0:tt], in_=Uv[:, :, row0:row0 + tt])), writes=[b_uT], dma=True)
            S.add("sp", (lambda e, row0=row0, tt=tt: e.dma_start(out=MSt[:, :, 0:tt], in_=MSv[:, :, row0:row0 + tt])), writes=[b_ms], dma=True)
            for s in range(nsub):
                r = min(128, tt - s * 128)
                tok0 = t0 + s * 128
                if is_p:
                    for g in range(3):
                        S.add("pool", (lambda e, g=g, tok0=tok0, r=r: e.dma_start(out=ao[g][0:r, :], in_=AO[g][tok0:tok0 + r, :])), writes=[b_ao[g]], dma=True)
                    S.add("pool", (lambda e, r=r: e.tensor_tensor(out=asum[0:r, :], in0=ao[0][0:r, :], in1=ao[1][0:r, :], op=ALU.add)), reads=[b_ao[0], b_ao[1]], writes=[b_as])
                    S.add("pool", (lambda e, r=r: e.tensor_tensor(out=asum[0:r, :], in0=asum[0:r, :], in1=ao[2][0:r, :], op=ALU.add)), reads=[b_as, b_ao[2]], writes=[b_as])
                else:
                    S.add("pool", (lambda e, r=r: e.dma_start(out=asum[0:r, :], in_=AOS[0:r, :])), writes=[b_as], dma=True)
                av = asum.rearrange("p (h e) -> p h e", e=65)
                S.add("dve", (lambda e, r=r, av=av: e.reciprocal(out=rl[0:r, :], in_=av[0:r, :, 64])), reads=[b_as], writes=[b_rl])
                S.add("dve", (lambda e, r=r, av=av: e.tensor_tensor(out=attn_b[0:r, :].rearrange("p (h e) -> p h e", h=8), in0=av[0:r, :, 0:64],
                                                                    in1=rl[0:r, :].unsqueeze(2).to_broadcast([r, 8, 64]), op=ALU.mult)), reads=[b_as, b_rl], writes=[b_ab])
                trans_T(attn_b, b_ab, r, PS[6 + s % 2], b_ps[6 + s % 2], attnT, b_at, s * 128, 4)
            for m in range(8):
                pa, pb = PS[m % 2], PS[2 + m % 2]
                i2 = m % 2

                def mm(e, m=m, pa=pa, pb=pb, tt=tt):
                    ins = None
                    for k in range(4):
                        ins = e.matmul(pa[:, 0:tt], lhsT=Woa[:, k, m * 128:(m + 1) * 128], rhs=attnT[:, k, 0:tt], start=(k == 0), stop=(k == 3))
                    for k in range(8):
                        ins = e.matmul(pb[:, 0:tt], lhsT=Wga[:, k, m * 128:(m + 1) * 128], rhs=uT[:, k, 0:tt], start=(k == 0), stop=(k == 7))
                    return ins
                S.add("pe", mm, reads=[b_w, b_at, b_uT], writes=[b_ps[m % 2], b_ps[2 + m % 2]])
                S.add("act", (lambda e, pb=pb, i2=i2, tt=tt: e.activation(out=sgm[i2][:, 0:tt], in_=pb[:, 0:tt], func=AF.Sigmoid)), reads=[b_ps[2 + m % 2]], writes=[b_sg[i2]])
                S.add("dve", (lambda e, pa=pa, i2=i2, tt=tt: e.tensor_tensor(out=tmp[i2][:, 0:tt], in0=pa[:, 0:tt], in1=sgm[i2][:, 0:tt], op=ALU.mult)),
                      reads=[b_ps[m % 2], b_sg[i2]], writes=[b_tmp[i2]])
                S.add("pool", (lambda e, m=m, i2=i2, tt=tt: e.tensor_tensor(out=merged[:, m, 0:tt], in0=tmp[i2][:, 0:tt], in1=MSt[:, m, 0:tt], op=ALU.add)),
                      reads=[b_tmp[i2], b_ms], writes=[b_mg])
            for s in range(nsub):
                r = min(128, tt - s * 128)
                row = row0 + s * 128
                i2 = s % 2
                S.add("sp", (lambda e, i2=i2, row=row, r=r: e.dma_start(out=h1[i2][0:r, :], in_=H1[row:row + r, :])), writes=[b_h1[i2]], dma=True)
                for nh in range(2):
                    pd = PS[4 + nh]

                    def mmo(e, s=s, r=r, nh=nh, pd=pd):
                        ins = None
                        for k in range(8):
                            ins = e.matmul(pd[0:r, :], lhsT=merged[:, k, s * 128:s * 128 + r], rhs=Wout[:, k, nh * 512:(nh + 1) * 512], start=(k == 0), stop=(k == 7))
                        return ins
                    S.add("pe", mmo, reads=[b_w, b_mg], writes=[b_ps[4 + nh]])
                    S.add("dve", (lambda e, i2=i2, r=r, nh=nh, pd=pd: e.tensor_tensor(out=h2[i2][0:r, nh * 512:(nh + 1) * 512], in0=pd[0:r, :], in1=h1[i2][0:r, nh * 512:(nh + 1) * 512], op=ALU.add)),
                          reads=[b_ps[4 + nh], b_h1[i2]], writes=[b_h2[i2]])
                S.add("pool", (lambda e, i2=i2, row=row, r=r: e.dma_start(out=H2[row:row + r, :], in_=h2[i2][0:r, :])), reads=[b_h2[i2]], dma=True)
        S.barrier()

    if "D1" in phases:
        phase_d1()

    def d2_sub(env, s, r, row, h, b_h):
        S.add("pool", (lambda e: e.dma_start(out=H3[row:row + r, :], in_=h[0:r, :])), reads=[b_h], dma=True)

    if "D2" in phases:
        ffn_phase(H2[0:T, :], H2[T:TA, :], norm_ffn2, w2g, w2u, w2d, lambda env: None, d2_sub, lambda env, row0, tt: None)

    def phase_d3():
        A = Alloc()
        Wpg = A.bf(8 * D).rearrange("p (k n) -> p k n", k=8)
        Wpp = A.bf(2 * D).rearrange("p (k n) -> p k n", k=2)
        st = A.f32(2048)
        stage = [st[:, 0:1024], st[:, 1024:2048]]
        gple = A.f32(1024)
        gfin = A.f32(1024)
        h3 = [A.f32(1024), A.f32(1024)]
        h4 = [A.f32(1024), A.f32(1024)]
        yo = [A.f32(1024), A.f32(1024)]
        pf = [A.f32(256), A.f32(256)]
        pb16 = [A.bf(256), A.bf(256)]
        xn = [A.bf(1024), A.bf(1024)]
        u3T = [A.bf(1024).rearrange("p (k t) -> p k t", k=8) for _ in range(2)]
        pT = [A.bf(256).rearrange("p (k t) -> p k t", k=2) for _ in range(2)]
        sg = [A.f32(512), A.f32(512)]
        tm = [A.f32(512), A.f32(512)]
        junk = A.bf(1024)
        b_w = S.buf("w")
        b_st = [S.buf("st0"), S.buf("st1")]
        load_weight(lambda k, c0, c: Wpg[:, k, c0:c0 + c], w_ple_gate, 8, 0, 1024, b_w, b_st, stage)
        load_weight(lambda k, c0, c: Wpp[:, k, c0:c0 + c], w_ple_proj, 2, 0, 1024, b_w, b_st, stage)
        b_g = S.buf("g")
        load_bc(gple, norm_ple, D, b_g)
        load_bc(gfin, norm_final, D, b_g)
        b_h3 = [S.buf("h30"), S.buf("h31")]
        b_h4 = [S.buf("h40"), S.buf("h41")]
        b_yo = [S.buf("yo0"), S.buf("yo1")]
        b_pf = [S.buf("pf0"), S.buf("pf1")]
        b_pb = [S.buf("pb0"), S.buf("pb1")]
        b_xn = [S.buf("xn0"), S.buf("xn1")]
        b_u3 = [S.buf("u30"), S.buf("u31")]
        b_pT = [S.buf("pT0"), S.buf("pT1")]
        b_sg = [S.buf("sg0"), S.buf("sg1")]
        b_tm = [S.buf("tm0"), S.buf("tm1")]
        b_ss = [S.buf("ss0"), S.buf("ss1")]
        b_ss2 = [S.buf("ss20"), S.buf("ss21")]
        b_junk = S.buf("junk")
        b_ps = new_ps()
        subs = [(True, t0, 128) for t0 in range(0, T, 128)] + [(False, 0, NS)]
        for it, (is_p, t0, r) in enumerate(subs):
            i2 = it % 2
            row = t0 if is_p else T
            ssv = CP[:, 8 + i2:9 + i2]
            ssv2 = CP[:, 12 + i2:13 + i2]
            psrc = p_p[t0:t0 + r, :] if is_p else p_s[0:r, :]
            ydst = y_p[t0:t0 + r, :] if is_p else y_s[0:r, :]
            S.add("sp", (lambda e, i2=i2, row=row, r=r: e.dma_start(out=h3[i2][0:r, :], in_=H3[row:row + r, :])), writes=[b_h3[i2]], dma=True)
            S.add("sp", (lambda e, i2=i2, psrc=psrc, r=r: e.dma_start(out=pf[i2][0:r, :], in_=psrc)), writes=[b_pf[i2]], dma=True)
            norm_T(h3[i2], b_h3[i2], r, gple, b_g, ssv, b_ss[i2], junk, b_junk, xn[i2], b_xn[i2], PS[6], b_ps[6], u3T[i2], b_u3[i2], 0)
            S.add("dve", (lambda e, i2=i2, r=r: e.tensor_copy(out=pb16[i2][0:r, :], in_=pf[i2][0:r, :])), reads=[b_pf[i2]], writes=[b_pb[i2]])
            trans_T(pb16[i2], b_pb[i2], r, PS[7], b_ps[7], pT[i2], b_pT[i2], 0, 2)
            for nh in range(2):
                pg, pp = PS[nh], PS[2 + nh]

                def mm(e, i2=i2, r=r, nh=nh, pg=pg, pp=pp):
                    ins = None
                    for k in range(8):
                        ins = e.matmul(pg[0:r, :], lhsT=u3T[i2][:, k, 0:r], rhs=Wpg[:, k, nh * 512:(nh + 1) * 512], start=(k == 0), stop=(k == 7))
                    for k in range(2):
                        ins = e.matmul(pp[0:r, :], lhsT=pT[i2][:, k, 0:r], rhs=Wpp[:, k, nh * 512:(nh + 1) * 512], start=(k == 0), stop=(k == 1))
                    return ins
                S.add("pe", mm, reads=[b_w, b_u3[i2], b_pT[i2]], writes=[b_ps[nh], b_ps[2 + nh]])
                S.add("act", (lambda e, nh=nh, r=r, pg=pg: e.activation(out=sg[nh][0:r, :], in_=pg[0:r, :], func=AF.Sigmoid)), reads=[b_ps[nh]], writes=[b_sg[nh]])
                S.add("dve", (lambda e, nh=nh, r=r, pp=pp: e.tensor_tensor(out=tm[nh][0:r, :], in0=pp[0:r, :], in1=sg[nh][0:r, :], op=ALU.mult)), reads=[b_ps[2 + nh], b_sg[nh]], writes=[b_tm[nh]])
                S.add("pool", (lambda e, i2=i2, nh=nh, r=r: e.tensor_tensor(out=h4[i2][0:r, nh * 512:(nh + 1) * 512], in0=tm[nh][0:r, :], in1=h3[i2][0:r, nh * 512:(nh + 1) * 512], op=ALU.add)),
                      reads=[b_tm[nh], b_h3[i2]], writes=[b_h4[i2]])
            rstd_ops(h4[i2], b_h4[i2], r, ssv2, b_ss2[i2], junk, b_junk, D)
            S.add("dve", (lambda e, i2=i2, r=r, ssv2=ssv2: e.scalar_tensor_tensor(out=yo[i2][0:r, :], in0=h4[i2][0:r, :], scalar=ssv2[0:r, :], in1=gfin[0:r, :], op0=ALU.mult, op1=ALU.mult)),
                  reads=[b_h4[i2], b_ss2[i2], b_g], writes=[b_yo[i2]])
            S.add("pool", (lambda e, i2=i2, r=r, ydst=ydst: e.dma_start(out=ydst, in_=yo[i2][0:r, :])), reads=[b_yo[i2]], dma=True)
        S.barrier()

    if "D3" in phases:
        phase_d3()

    S.barrier()
    S.finalize()
    sems = {e: es.enter_context(nc.semaphore(f"c_{e}")) for e in ENGS}
    dsems = {e: [es.enter_context(nc.semaphore(f"d_{e}{i}")) for i in range(NDSEM)] for e in ("sp", "pool", "act")}
    with nc.Block() as block:
        @block.sync
        def _(eng):
            S.emit("sp", eng, sems, dsems)

        @block.tensor
        def _(eng):
            S.emit("pe", eng, sems, dsems)

        @block.scalar
        def _(eng):
            S.emit("act", eng, sems, dsems)

        @block.vector
        def _(eng):
            S.emit("dve", eng, sems, dsems)

        @block.gpsimd
        def _(eng):
            S.emit("pool", eng, sems, dsems)
    es.close()
    return P


def make_consts():
    c = {}
    c["c_ident"] = np.eye(128, dtype=np.float32)
    perm = np.zeros((128, 128), np.float32)
    for m in range(128):
        k = (m // 64) * 64 + ((m % 64) + 32) % 64
        perm[k, m] = 1.0
    c["c_perm"] = perm
    pos = np.concatenate([np.arange(T), np.full(NS, T)]).astype(np.float32)
    half = 32
    inv_freq = (np.float32(10000.0) ** (-np.arange(half, dtype=np.float32) / np.float32(half))).astype(np.float32)
    ang = pos[:, None] * inv_freq[None, :]
    cos = np.cos(ang).astype(np.float32)
    sin = np.sin(ang).astype(np.float32)
    cosT = np.zeros((128, TA), np.float32)
    sinT = np.zeros((128, TA), np.float32)
    for p in range(128):
        dh = p % 64
        cosT[p] = cos[:, dh % 32]
        sinT[p] = -sin[:, dh % 32] if dh < 32 else sin[:, dh % 32]
    c["c_cos"] = cosT
    c["c_sin"] = sinT
    i = np.arange(128)
    triu = (i[:, None] <= i[None, :]).astype(np.float32)
    tril = (i[:, None] >= i[None, :]).astype(np.float32)
    strict = (i[:, None] > i[None, :]).astype(np.float32)
    ones = np.ones((128, 128), np.float32)
    c["c_masks"] = np.concatenate([triu, tril, strict, triu, ones], axis=1)
    ex = np.zeros((32, 2048), np.float32)
    for h in range(32):
        ex[h, h * 64:(h + 1) * 64] = 1.0
    c["c_expand"] = ex
    return c


_CACHE = {}


def core_inputs(inputs, c, consts):
    f = lambda a: np.ascontiguousarray(a, dtype=np.float32)
    sl = slice(c * NS, (c + 1) * NS)
    m = {
        "x_prompt": f(inputs["x_prompt"][c]),
        "x_sample": f(inputs["x_sample"][sl, 0]),
        "p_prompt": f(inputs["p_prompt"][0, c]),
        "p_sample": f(inputs["p_sample"][0, sl, 0]),
        "conv_wT": f(inputs["conv_w"][0].reshape(4, 24, 128).transpose(2, 1, 0)),
        "conv_bT": f(inputs["conv_b"][0].reshape(24, 128).T),
        "state_conv": f(inputs["state_conv"][0, sl].reshape(NS * 3, 3072)),
        "state_ssm": f(inputs["state_ssm"][0, sl].reshape(NS, 2048, 128)),
        "cache_kv0": f(inputs["cache_kv_w128"][0, sl].reshape(NS, 128, 1024)),
        "cache_kv1": f(inputs["cache_kv_w512"][0, sl].reshape(NS, 512, 1024)),
        "cache_kv2": f(inputs["cache_kv_w2048"][0, sl].reshape(NS, 2048, 1024)),
        "norm_final": f(inputs["norm_final"].reshape(1, D)),
    }
    m["dt_biasT"] = f(inputs["dt_bias"].reshape(32, 1))
    m["a_logT"] = f(inputs["a_log"].reshape(32, 1))
    m["d_skipT"] = f(inputs["d_skip"].reshape(32, 1))
    m["norm_ssmT"] = f(inputs["norm_ssm"].reshape(16, 128).T)
    for k in ("norm_ffn1", "norm_mix", "dt_bias", "a_log", "d_skip", "norm_ssm", "norm_ffn2", "norm_ple"):
        m[k] = f(inputs[k].reshape(1, -1))
    for k in ("w_ffn1_gate", "w_ffn1_up", "w_ffn1_down", "w_in", "w_o_ssm", "w_o_attn", "w_out",
              "w_ffn2_gate", "w_ffn2_up", "w_ffn2_down", "w_ple_gate", "w_ple_proj"):
        m[k] = f(inputs[k][0])
    m.update(consts)
    return m


def kernel(**inputs):
    if "P" not in _CACHE:
        _CACHE["P"] = build()
    P = _CACHE["P"]
    consts = make_consts()
    in_maps = []
    for c in range(NCORES):
        m = core_inputs(inputs, c, consts)
        in_maps.append({k: m[k] for k in P.ins})
    res = run_bass_kernel_spmd(P.nc, in_maps, core_ids=list(range(NCORES)))
    R = res.results
    g = lambda name: [np.asarray(R[c][name], dtype=np.float32) for c in range(NCORES)]
    y_prompt = np.stack(g("y_prompt"), 0)
    y_sample = np.concatenate(g("y_sample"), 0).reshape(NCORES * NS, 1, D)
    outs = [y_prompt, y_sample]
    for gi in range(3):
        outs.append(np.stack(g(f"kv{gi}_p"), 0).reshape(1, NCORES, W_G[gi], 2, 8, 64))
    outs.append(np.stack(g("conv_p"), 0).reshape(1, NCORES, 3, 3072))
    outs.append(np.stack(g("ssm_p"), 0).reshape(1, NCORES, 32, 64, 128))
    for gi in range(3):
        outs.append(np.concatenate(g(f"kv{gi}_s"), 0).reshape(1, NCORES * NS, 1, 2, 8, 64))
    outs.append(np.concatenate(g("conv_s"), 0).reshape(1, NCORES * NS, 3, 3072))
    outs.append(np.concatenate(g("ssm_s"), 0).reshape(1, NCORES * NS, 32, 64, 128))
    return tuple(outs)
```

```python
import numpy as np
from contextlib import ExitStack
import concourse.bass as bass
import concourse.mybir as mybir
from concourse.bass_utils import run_bass_kernel_spmd

F32 = mybir.dt.float32
BF16 = mybir.dt.bfloat16
ALU = mybir.AluOpType
AF = mybir.ActivationFunctionType
AX = mybir.AxisListType

D = 1024
DFF = 2816
import os
T = int(os.environ.get('KT_T', 8192))
NS = 16
NCORES = 8
NIN = 11808
EPS = 1e-6


class Buf:
    __slots__ = ("name", "w", "r", "wd", "rd")

    def __init__(self, name):
        self.name = name
        self.w = {}
        self.r = {}
        self.wd = []
        self.rd = []


class Op:
    __slots__ = ("eng", "fn", "deps", "needs_inc", "inc_idx", "is_dma", "dsem", "dval", "dprev")

    def __init__(self, eng, fn, is_dma=False):
        self.eng = eng
        self.fn = fn
        self.deps = []
        self.needs_inc = False
        self.inc_idx = None
        self.is_dma = is_dma
        self.dsem = None
        self.dval = None
        self.dprev = None


ENGS = ("pe", "act", "dve", "pool", "sp")
NDSEM = 8


class Sched:
    def __init__(self):
        self.ops = {e: [] for e in ENGS}
        self.ndma = {e: 0 for e in ENGS}
        self.all_bufs = []

    def buf(self, name):
        b = Buf(name)
        self.all_bufs.append(b)
        return b

    def add(self, eng, fn, reads=(), writes=(), dma=False):
        op = Op(eng, fn, dma)
        deps = []
        raw = []
        for b in reads:
            raw.extend(b.w.values())
            raw.extend(b.wd)
        deps.extend(raw)
        for b in writes:
            deps.extend(b.w.values())
            deps.extend(b.wd)
            deps.extend(b.r.values())
            deps.extend(b.rd)
        seen = set()
        raw_ids = set(id(d) for d in raw)
        for d in deps:
            if d is op or id(d) in seen:
                continue
            seen.add(id(d))
            if d.eng == eng and not d.is_dma and not dma and id(d) not in raw_ids and not os.environ.get('KSTRICT'):
                continue
            op.deps.append(d)
        for b in reads:
            if dma:
                b.rd.append(op)
                if len(b.rd) > 24:
                    b.rd.pop(0)
            else:
                b.r[eng] = op
        for b in writes:
            if dma:
                b.wd.append(op)
                if len(b.wd) > 24:
                    b.wd.pop(0)
            else:
                b.w[eng] = op
        if dma:
            i = self.ndma[eng]
            self.ndma[eng] += 1
            op.dsem = i % NDSEM
            op.dval = 16 * (i // NDSEM + 1)
        self.ops[eng].append(op)
        return op

    def barrier(self):
        last = []
        for e in ENGS:
            ops = self.ops[e]
            for o in reversed(ops):
                if o.fn is not None and not o.is_dma:
                    last.append(o)
                    break
            cnt = 0
            for o in reversed(ops):
                if o.is_dma:
                    last.append(o)
                    cnt += 1
                    if cnt >= NDSEM:
                        break
        for e in ENGS:
            op = Op(e, None)
            op.deps = [d for d in last]
            self.ops[e].append(op)
        for b in self.all_bufs:
            b.w = {}
            b.r = {}
            b.wd = []
            b.rd = []
        self.all_bufs = []

    def finalize(self):
        for e in ENGS:
            for op in self.ops[e]:
                for d in op.deps:
                    if not d.is_dma:
                        d.needs_inc = True
        for e in ENGS:
            c = 0
            for op in self.ops[e]:
                if op.needs_inc and not op.is_dma:
                    c += 1
                    op.inc_idx = c

    def emit(self, eng, handle, sems, dsems):
        waited = {}
        for op in self.ops[eng]:
            need = {}
            for d in op.deps:
                if d.is_dma:
                    key = ("d", d.eng, d.dsem)
                    val = d.dval
                else:
                    if d.eng == eng and d.inc_idx is None:
                        continue
                    key = ("c", d.eng)
                    val = d.inc_idx
                if need.get(key, 0) < val:
                    need[key] = val
            if op.is_dma:
                if op.dval > 16:
                    key = ("d", eng, op.dsem)
                    if need.get(key, 0) < op.dval - 16:
                        need[key] = op.dval - 16
            for key, val in need.items():
                if waited.get(key, 0) >= val:
                    continue
                waited[key] = val
                if key[0] == "c":
                    handle.wait_ge(sems[key[1]], val)
                else:
                    handle.wait_ge(dsems[key[1]][key[2]], val)
            if op.fn is None:
                continue
            inst = op.fn(handle)
            if op.is_dma:
                inst.then_inc(dsems[eng][op.dsem], 16)
            elif op.needs_inc:
                inst.then_inc(sems[eng], 1)


class Prog:
    def __init__(self, debug_outs=()):
        self.nc = bass.Bass("TRN2", target_bir_lowering=False)
        self.S = Sched()
        self.debug_outs = set(debug_outs)
        self.ins = {}
        self.outs = {}
        self.scr = {}

    def din(self, name, shape, dt=F32):
        t = self.nc.dram_tensor(name, list(shape), dt, kind="ExternalInput").ap()
        self.ins[name] = t
        return t

    def dout(self, name, shape, dt=F32):
        t = self.nc.dram_tensor(name, list(shape), dt, kind="ExternalOutput").ap()
        self.outs[name] = t
        return t

    def dscr(self, name, shape, dt):
        kind = "ExternalOutput" if name in self.debug_outs else "Internal"
        t = self.nc.dram_tensor(name, list(shape), dt, kind=kind).ap()
        self.scr[name] = t
        return t


import os
NPOOL = 104448
TA = T + NS
W_G = (128, 512, 2048)
DIL = (1, 4, 16)


def build(debug_outs=(), phases=("F1", "A1", "B", "A2", "C", "D1", "D2", "D3")):
    P = Prog(debug_outs)
    nc = P.nc
    S = P.S
    es = ExitStack()

    x_p = P.din("x_prompt", [T, D])
    x_s = P.din("x_sample", [NS, D])
    p_p = P.din("p_prompt", [T, 256])
    p_s = P.din("p_sample", [NS, 256])
    norm_ffn1 = P.din("norm_ffn1", [1, D])
    w1g = P.din("w_ffn1_gate", [D, DFF])
    w1u = P.din("w_ffn1_up", [D, DFF])
    w1d = P.din("w_ffn1_down", [DFF, D])
    norm_mix = P.din("norm_mix", [1, D])
    w_in = P.din("w_in", [D, NIN])
    conv_wT = P.din("conv_wT", [128, 24, 4])
    conv_bT = P.din("conv_bT", [128, 24])
    dt_bias = P.din("dt_bias", [1, 32])
    a_log = P.din("a_log", [1, 32])
    d_skip = P.din("d_skip", [1, 32])
    norm_ssm = P.din("norm_ssm", [1, 2048])
    w_o_ssm = P.din("w_o_ssm", [2048, D])
    w_o_attn = P.din("w_o_attn", [512, D])
    w_out = P.din("w_out", [D, D])
    norm_ffn2 = P.din("norm_ffn2", [1, D])
    w2g = P.din("w_ffn2_gate", [D, DFF])
    w2u = P.din("w_ffn2_up", [D, DFF])
    w2d = P.din("w_ffn2_down", [DFF, D])
    norm_ple = P.din("norm_ple", [1, D])
    w_ple_gate = P.din("w_ple_gate", [D, D])
    w_ple_proj = P.din("w_ple_proj", [256, D])
    norm_final = P.din("norm_final", [1, D])
    state_conv = P.din("state_conv", [NS * 3, 3072])
    state_ssm = P.din("state_ssm", [NS, 2048, 128])
    cache_kv = [P.din(f"cache_kv{g}", [NS, W_G[g], 1024]) for g in range(3)]
    ident_d = P.din("c_ident", [128, 128])
    perm_d = P.din("c_perm", [128, 128])
    cos_d = P.din("c_cos", [128, TA])
    sin_d = P.din("c_sin", [128, TA])
    cmask_d = P.din("c_masks", [128, 5 * 128])
    expand_d = P.din("c_expand", [32, 2048])
    dt_biasT = P.din("dt_biasT", [32, 1])
    a_logT = P.din("a_logT", [32, 1])
    d_skipT = P.din("d_skipT", [32, 1])
    norm_ssmT = P.din("norm_ssmT", [128, 16])

    H1 = P.dscr("H1", [TA, D], F32)
    U = P.dscr("U", [D, TA], BF16)
    XC = P.dscr("XC", [3072, TA], BF16)
    QT = P.dscr("QT", [1536, TA], BF16)
    KT = P.dscr("KT", [1536, TA], BF16)
    V = P.dscr("V", [TA, 1536], BF16)
    MS = P.dscr("MS", [D, TA], BF16)
    AO = [P.dscr(f"AO{g}", [T, 520], F32) for g in range(3)]
    AOS = P.dscr("AOS", [NS, 520], F32)
    QKS = P.dscr("QKS", [NS, 3072], F32)
    BCS = P.dscr("BCS", [NS, 1024], F32)
    H2 = P.dscr("H2", [TA, D], F32)
    H3 = P.dscr("H3", [TA, D], F32)

    y_p = P.dout("y_prompt", [T, D])
    y_s = P.dout("y_sample", [NS, D])
    kv_p = [P.dout(f"kv{g}_p", [W_G[g], 2, 512]) for g in range(3)]
    conv_p = P.dout("conv_p", [3, 3072])
    ssm_p = P.dout("ssm_p", [2048, 128])
    kv_s = [P.dout(f"kv{g}_s", [NS, 2, 512]) for g in range(3)]
    conv_s = P.dout("conv_s", [NS, 3, 3072])
    ssm_s = P.dout("ssm_s", [NS, 2048, 128])

    POOL = es.enter_context(nc.sbuf_tensor("pool", [128, NPOOL], BF16))
    CP = es.enter_context(nc.sbuf_tensor("cpool", [128, 256], F32))
    ident_f = es.enter_context(nc.sbuf_tensor("ident_f", [128, 128], F32))
    ident = es.enter_context(nc.sbuf_tensor("ident_b", [128, 128], BF16))
    PS = [es.enter_context(nc.psum_tensor(f"ps{i}", [128, 512], F32)) for i in range(8)]

    class Alloc:
        def __init__(self):
            self.off = 0

        def bf(self, n):
            n2 = (n + 1) // 2 * 2
            v = POOL[:, self.off:self.off + n]
            self.off += n2
            assert self.off <= NPOOL, self.off
            return v

        def f32(self, n):
            v = POOL[:, self.off:self.off + 2 * n].bitcast(F32)
            self.off += 2 * n
            assert self.off <= NPOOL, self.off
            return v

    b_ident = S.buf("ident")
    bi0 = S.buf("identf")
    S.add("sp", lambda e: e.dma_start(out=ident_f[:], in_=ident_d[:, :]), writes=[bi0], dma=True)
    S.add("dve", lambda e: e.tensor_copy(out=ident[:], in_=ident_f[:]), reads=[bi0], writes=[b_ident])
    mhalf = CP[:, 255:256]
    b_mhalf = S.buf("mhalf")
    S.add("pool", lambda e: e.memset(mhalf, -0.5), writes=[b_mhalf])
    b_pool0 = S.buf("pool0")
    NQ = NPOOL // 4
    for qi, en in enumerate(("dve", "pool", "act", "dve")):
        if en == "act":
            S.add(en, (lambda e, qi=qi: e.memzero(POOL[:, qi * NQ:(qi + 1) * NQ])), writes=[b_pool0])
        else:
            S.add(en, (lambda e, qi=qi: e.memset(POOL[:, qi * NQ:(qi + 1) * NQ], 0.0)), writes=[b_pool0])
    S.add("dve", lambda e: e.memset(CP[:, 0:255], 0.0), writes=[b_pool0])
    S.barrier()
    cvt_rr = [0]

    def new_ps():
        return [S.buf(f"ps{i}") for i in range(8)]

    def load_weight(dst_ap_fn, src, kchunks, c_lo, ncols, wbuf, stage_bufs, stage_aps, CH=1024):
        srcv = src.rearrange("(k p) n -> p k n", p=128)
        i = 0
        for k in range(kchunks):
            for c0 in range(0, ncols, CH):
                cw = min(CH, ncols - c0)
                sb = stage_bufs[i % 2]
                sa = stage_aps[i % 2]
                q = "sp" if i % 2 == 0 else "pool"
                S.add(q, (lambda e, sa=sa, k=k, c0=c0, cw=cw: e.dma_start(out=sa[:, 0:cw], in_=srcv[:, k, c_lo + c0:c_lo + c0 + cw])),
                      writes=[sb], dma=True)
                ce = ("act", "dve")[cvt_rr[0] % 2]
                cvt_rr[0] += 1
                dst = dst_ap_fn(k, c0, cw)
                if ce == "act":
                    S.add("act", (lambda e, dst=dst, sa=sa, cw=cw: e.copy(out=dst, in_=sa[:, 0:cw])), reads=[sb], writes=[wbuf])
                else:
                    S.add("dve", (lambda e, dst=dst, sa=sa, cw=cw: e.tensor_copy(out=dst, in_=sa[:, 0:cw])), reads=[sb], writes=[wbuf])
                i += 1

    def load_bc(dst, src_row, n, buf, q="sp"):
        S.add(q, lambda e: e.dma_start(out=dst, in_=src_row[0:1, :].to_broadcast([128, n])), writes=[buf], dma=True)

    def rstd_ops(h, b_h, r, ssv, b_ssv, junk, b_junk, n):
        S.add("act", (lambda e: e.activation(out=junk[0:r, 0:n], in_=h[0:r, 0:n], func=AF.Square, accum_out=ssv[0:r, :])),
              reads=[b_h], writes=[b_junk, b_ssv])
        S.add("dve", (lambda e: e.tensor_scalar(out=ssv[0:r, :], in0=ssv[0:r, :], scalar1=1.0 / n, scalar2=EPS, op0=ALU.mult, op1=ALU.add)),
              reads=[b_ssv], writes=[b_ssv])
        S.add("pool", (lambda e: e.tensor_tensor(out=ssv[0:r, :], in0=ssv[0:r, :], in1=mhalf[0:r, :], op=ALU.pow)),
              reads=[b_ssv, b_mhalf], writes=[b_ssv])

    def norm_T(h, b_h, r, gbc, b_g, ssv, b_ssv, junk, b_junk, xn, b_xn, pt, b_pt, dstT, b_dst, c0, nk=8):
        n = nk * 128
        rstd_ops(h, b_h, r, ssv, b_ssv, junk, b_junk, n)
        S.add("dve", (lambda e: e.scalar_tensor_tensor(out=xn[0:r, 0:n], in0=h[0:r, 0:n], scalar=ssv[0:r, :], in1=gbc[0:r, 0:n], op0=ALU.mult, op1=ALU.mult)),
              reads=[b_h, b_ssv, b_g], writes=[b_xn])
        trans_T(xn, b_xn, r, pt, b_pt, dstT, b_dst, c0, nk)

    def trans_T(xn, b_xn, r, pt, b_pt, dstT, b_dst, c0, nk):
        for k0 in range(0, nk, 8):
            kk = min(8, nk - k0)
            ptv = pt[:].bitcast(BF16)[:, 0:1024].rearrange("p (k t) -> p k t", k=8)

            def tr(e, k0=k0, kk=kk, ptv=ptv):
                ins = None
                for k in range(kk):
                    ins = e.transpose(out=ptv[:, k, 0:r], in_=xn[0:r, (k0 + k) * 128:(k0 + k + 1) * 128], identity=ident[0:r, 0:r])
                return ins
            S.add("pe", tr, reads=[b_xn, b_ident], writes=[b_pt])
            S.add("act", (lambda e, k0=k0, kk=kk, ptv=ptv: e.copy(out=dstT[:, k0:k0 + kk, c0:c0 + r], in_=ptv[:, 0:kk, 0:r])),
                  reads=[b_pt], writes=[b_dst])

    def tiles_of(TT):
        return [(True, t0, TT) for t0 in range(0, T, TT)] + [(False, 0, NS)]

    def ffn_phase(src_p, src_s, gain_in, wg_d, wu_d, wd_d, post_init, post_sub, post_tile):
        TT = 256
        A = Alloc()
        Wg = A.bf(8 * DFF).rearrange("p (k n) -> p k n", k=8)
        Wu = A.bf(8 * DFF).rearrange("p (k n) -> p k n", k=8)
        Wd = A.bf(22 * D).rearrange("p (k n) -> p k n", k=22)
        st = A.f32(4096)
        xs_all = [[st[:, 0:1024], st[:, 1024:2048]], [st[:, 2048:3072], st[:, 3072:4096]]]
        stage = xs_all[0]
        hs_ = A.f32(2048)
        hs = [hs_[:, 0:1024], hs_[:, 1024:2048]]
        gbc = A.f32(1024)
        xn_ = A.bf(2048)
        xn = [xn_[:, 0:1024], xn_[:, 1024:2048]]
        uT_all = [A.bf(8 * TT).rearrange("p (k t) -> p k t", k=8) for _ in range(2)]
        sg_ = A.bf(2 * TT)
        sg = [sg_[:, 0:TT], sg_[:, TT:2 * TT]]
        actT = A.bf(22 * TT).rearrange("p (k t) -> p k t", k=22)
        junk = A.bf(1024)
        b_w = S.buf("w")
        b_xs_all = [[S.buf("st0"), S.buf("st1")], [S.buf("st2"), S.buf("st3")]]
        b_xs = b_xs_all[0]
        load_weight(lambda k, c0, cw: Wg[:, k, c0:c0 + cw], wg_d, 8, 0, DFF, b_w, b_xs, stage)
        load_weight(lambda k, c0, cw: Wu[:, k, c0:c0 + cw], wu_d, 8, 0, DFF, b_w, b_xs, stage)
        load_weight(lambda k, c0, cw: Wd[:, k, c0:c0 + cw], wd_d, 22, 0, D, b_w, b_xs, stage)
        b_g = S.buf("g")
        load_bc(gbc, gain_in, D, b_g)
        b_hs = [S.buf("h0"), S.buf("h1")]
        b_xn = [S.buf("xn0"), S.buf("xn1")]
        b_uT_all = [S.buf("uT0"), S.buf("uT1")]
        b_sg = [S.buf("sg0"), S.buf("sg1")]
        b_actT = S.buf("actT")
        b_ss = [S.buf("ss0"), S.buf("ss1")]
        b_ps = new_ps()
        ss = [CP[:, i:i + 1] for i in range(2)]
        b_junk = S.buf("junk")
        env = dict(A=A, PS=PS, b_ps=b_ps, ss=ss, b_ss=b_ss, junk=junk, b_junk=b_junk, TT=TT)
        post_init(env)
        for ti, (is_p, t0, tt) in enumerate(tiles_of(TT)):
            xs, b_xs = xs_all[ti % 2], b_xs_all[ti % 2]
            uT, b_uT = uT_all[ti % 2], b_uT_all[ti % 2]
            src = src_p if is_p else src_s
            nsub = (tt + 127) // 128
            row0 = t0 if is_p else T
            for s in range(nsub):
                r = min(128, tt - s * 128)
                xa = xs[s]
                S.add("sp", (lambda e, xa=xa, r=r, s=s, src=src, t0=t0: e.dma_start(out=xa[0:r, :], in_=src[t0 + s * 128:t0 + s * 128 + r, :])),
                      writes=[b_xs[s]], dma=True)
                norm_T(xa, b_xs[s], r, gbc, b_g, ss[s], b_ss[s], junk, b_junk, xn[s], b_xn[s], PS[6 + s], b_ps[6 + s], uT, b_uT, s * 128)
            for m in range(22):
                pg = PS[(m % 2) * 2]
                pu = PS[(m % 2) * 2 + 1]
                bg_, bu_ = b_ps[(m % 2) * 2], b_ps[(m % 2) * 2 + 1]

                def mm(e, m=m, pg=pg, pu=pu, tt=tt, uT=uT):
                    ins = None
                    for k in range(8):
                        ins = e.matmul(pg[:, 0:tt], lhsT=Wg[:, k, m * 128:(m + 1) * 128], rhs=uT[:, k, 0:tt], start=(k == 0), stop=(k == 7))
                    for k in range(8):
                        ins = e.matmul(pu[:, 0:tt], lhsT=Wu[:, k, m * 128:(m + 1) * 128], rhs=uT[:, k, 0:tt], start=(k == 0), stop=(k == 7))
                    return ins
                S.add("pe", mm, reads=[b_w, b_uT], writes=[bg_, bu_])
                sgt = sg[m % 2]
                S.add("act", (lambda e, pg=pg, sgt=sgt, tt=tt: e.activation(out=sgt[:, 0:tt], in_=pg[:, 0:tt], func=AF.Silu)),
                      reads=[bg_], writes=[b_sg[m % 2]])
                S.add("dve", (lambda e, pu=pu, sgt=sgt, tt=tt, m=m: e.tensor_tensor(out=actT[:, m, 0:tt], in0=pu[:, 0:tt], in1=sgt[:, 0:tt], op=ALU.mult)),
                      reads=[bu_, b_sg[m % 2]], writes=[b_actT])
            for s in range(nsub):
                r = min(128, tt - s * 128)
                for nh in range(2):
                    pd = PS[4 + nh]

                    def mmd(e, s=s, r=r, nh=nh, pd=pd):
                        ins = None
                        for k in range(22):
                            ins = e.matmul(pd[0:r, :], lhsT=actT[:, k, s * 128:s * 128 + r], rhs=Wd[:, k, nh * 512:(nh + 1) * 512], start=(k == 0), stop=(k == 21))
                        return ins
                    S.add("pe", mmd, reads=[b_w, b_actT], writes=[b_ps[4 + nh]])
                    S.add("dve", (lambda e, s=s, r=r, nh=nh, pd=pd, xs=xs: e.scalar_tensor_tensor(out=hs[s][0:r, nh * 512:(nh + 1) * 512], in0=pd[0:r, :], scalar=0.5, in1=xs[s][0:r, nh * 512:(nh + 1) * 512], op0=ALU.mult, op1=ALU.add)),
                          reads=[b_ps[4 + nh], b_xs[s]], writes=[b_hs[s]])
                post_sub(env, s, r, row0 + s * 128, hs[s], b_hs[s])
            post_tile(env, row0, tt)
        S.barrier()

    def f1_init(env):
        A = env["A"]
        env["gbc2"] = A.f32(1024)
        env["b_g2"] = S.buf("g2")
        load_bc(env["gbc2"], norm_mix, D, env["b_g2"])
        x2 = A.bf(2048)
        env["xn2"] = [x2[:, 0:1024], x2[:, 1024:2048]]
        env["b_xn2"] = [S.buf("xn2a"), S.buf("xn2b")]
        env["u2T"] = A.bf(8 * env["TT"]).rearrange("p (k t) -> p k t", k=8)
        env["b_u2T"] = S.buf("u2T")

    def f1_sub(env, s, r, row, h, b_h):
        S.add("pool", (lambda e: e.dma_start(out=H1[row:row + r, :], in_=h[0:r, :])), reads=[b_h], dma=True)
        norm_T(h, b_h, r, env["gbc2"], env["b_g2"], env["ss"][s], env["b_ss"][s], env["junk"], env["b_junk"],
               env["xn2"][s], env["b_xn2"][s], env["PS"][6 + s], env["b_ps"][6 + s], env["u2T"], env["b_u2T"], s * 128)

    def f1_tile(env, row0, tt):
        Uv = U.rearrange("(k p) t -> p k t", p=128)
        u2T = env["u2T"]
        S.add("sp", (lambda e: e.dma_start(out=Uv[:, :, row0:row0 + tt], in_=u2T[:, :, 0:tt])), reads=[env["b_u2T"]], dma=True)

    if "F1" in phases:
        ffn_phase(x_p, x_s, norm_ffn1, w1g, w1u, w1d, f1_init, f1_sub, f1_tile)

    def phase_a1():
        TT = 512
        A = Alloc()
        Wx = A.bf(8 * 3072).rearrange("p (k n) -> p k n", k=8)
        st = A.f32(2048)
        stage = [st[:, 0:1024], st[:, 1024:2048]]
        cw = A.f32(96).rearrange("p (m t) -> p m t", m=24)
        cb = A.f32(24)
        uT = A.bf(8 * TT).rearrange("p (k t) -> p k t", k=8)
        xraw = A.bf(24 * (TT + 4)).rearrange("p (m t) -> p m t", m=24)
        xc = A.bf(24 * TT).rearrange("p (m t) -> p m t", m=24)
        ac_ = A.f32(2 * TT)
        acc = [ac_[:, 0:TT], ac_[:, TT:2 * TT]]
        tok = A.f32(3072)
        prevT = A.f32(24 * 48).rearrange("p (m c) -> p m c", m=24)
        sc_tok = A.f32(3072)
        b_w = S.buf("w")
        b_st = [S.buf("st0"), S.buf("st1")]
        load_weight(lambda k, c0, c: Wx[:, k, c0:c0 + c], w_in, 8, 2048, 3072, b_w, b_st, stage)
        b_cw = S.buf("cw")
        S.add("sp", lambda e: e.dma_start(out=cw, in_=conv_wT[:, :, :]), writes=[b_cw], dma=True)
        S.add("sp", lambda e: e.dma_start(out=cb, in_=conv_bT[:, :]), writes=[b_cw], dma=True)
        b_uT = S.buf("uT")
        b_xraw = S.buf("xraw")
        b_xc = S.buf("xc")
        b_acc = [S.buf("acc0"), S.buf("acc1")]
        b_tok = S.buf("tok")
        b_ps = new_ps()
        Uv = U.rearrange("(k p) t -> p k t", p=128)
        XCv = XC.rearrange("(m p) t -> p m t", p=128)
        S.add("dve", lambda e: e.memset(xraw[:, :, 0:3], 0.0), writes=[b_xraw])

        def tokmajor(e_uT_cols, r, dst_tok, dst_dram):
            c_lo, c_hi = e_uT_cols
            for nb in range(6):
                pb = PS[4 + nb % 2]

                def mmt(e, nb=nb, pb=pb):
                    ins = None
                    for k in range(8):
                        ins = e.matmul(pb[0:r, :], lhsT=uT[:, k, c_lo:c_hi], rhs=Wx[:, k, nb * 512:(nb + 1) * 512], start=(k == 0), stop=(k == 7))
                    return ins
                S.add("pe", mmt, reads=[b_w, b_uT], writes=[b_ps[4 + nb % 2]])
                S.add("act", (lambda e, nb=nb, pb=pb: e.copy(out=dst_tok[0:r, nb * 512:(nb + 1) * 512], in_=pb[0:r, :])),
                      reads=[b_ps[4 + nb % 2]], writes=[b_tok])
            S.add("pool", (lambda e: e.dma_start(out=dst_dram, in_=dst_tok[0:r, :])), reads=[b_tok], dma=True)

        for (is_p, t0, tt) in tiles_of(TT):
            row0 = t0 if is_p else T
            S.add("sp", (lambda e, row0=row0, tt=tt: e.dma_start(out=uT[:, :, 0:tt], in_=Uv[:, :, row0:row0 + tt])), writes=[b_uT], dma=True)
            if not is_p:
                scv = state_conv
                S.add("sp", (lambda e: e.dma_start(out=sc_tok[0:48, :], in_=scv[:, :])), writes=[b_tok], dma=True)
                for g3 in range(3):
                    pb = PS[6]

                    def trp(e, g3=g3, pb=pb):
                        ins = None
                        for j in range(8):
                            m = g3 * 8 + j
                            ins = e.transpose(out=pb[:, j * 48:(j + 1) * 48], in_=sc_tok[0:48, m * 128:(m + 1) * 128], identity=ident_f[0:48, 0:48])
                        return ins
                    S.add("pe", trp, reads=[b_tok], writes=[b_ps[6]])
                    S.add("dve", (lambda e, g3=g3, pb=pb: e.tensor_copy(out=prevT[:, g3 * 8:(g3 + 1) * 8, :], in_=pb[:, 0:384].rearrange("p (j c) -> p j c", j=8))),
                          reads=[b_ps[6]], writes=[b_xraw])
            for m in range(24):
                pb = PS[m % 2]

                def mm(e, m=m, pb=pb, tt=tt):
                    ins = None
                    for k in range(8):
                        ins = e.matmul(pb[:, 0:tt], lhsT=Wx[:, k, m * 128:(m + 1) * 128], rhs=uT[:, k, 0:tt], start=(k == 0), stop=(k == 7))
                    return ins
                S.add("pe", mm, reads=[b_w, b_uT], writes=[b_ps[m % 2]])
                ac = acc[m % 2]
                b_ac = b_acc[m % 2]
                S.add("act", (lambda e, m=m, pb=pb, tt=tt, ac=ac: e.activation(out=ac[:, 0:tt], in_=pb[:, 0:tt], func=AF.Identity, scale=cw[:, m, 3:4], bias=cb[:, m:m + 1])),
                      reads=[b_ps[m % 2], b_cw], writes=[b_ac])
                if is_p:
                    S.add("act", (lambda e, m=m, pb=pb, tt=tt: e.copy(out=xraw[:, m, 3:3 + tt], in_=pb[:, 0:tt])),
                          reads=[b_ps[m % 2]], writes=[b_xraw])
                    for j in range(1, 4):
                        S.add("dve", (lambda e, m=m, j=j, tt=tt, ac=ac: e.scalar_tensor_tensor(out=ac[:, 0:tt], in0=xraw[:, m, 3 - j:3 - j + tt], scalar=cw[:, m, 3 - j:4 - j], in1=ac[:, 0:tt], op0=ALU.mult, op1=ALU.add)),
                              reads=[b_xraw, b_ac, b_cw], writes=[b_ac])
                else:
                    for j in range(3):
                        S.add("dve", (lambda e, m=m, j=j, tt=tt, ac=ac: e.scalar_tensor_tensor(out=ac[:, 0:tt], in0=prevT[:, m, j:48:3], scalar=cw[:, m, j:j + 1], in1=ac[:, 0:tt], op0=ALU.mult, op1=ALU.add)),
                              reads=[b_xraw, b_ac, b_cw], writes=[b_ac])
                S.add("act", (lambda e, m=m, tt=tt, ac=ac: e.activation(out=xc[:, m, 0:tt], in_=ac[:, 0:tt], func=AF.Silu)),
                      reads=[b_ac], writes=[b_xc])
            S.add("pool", (lambda e, row0=row0, tt=tt: e.dma_start(out=XCv[:, :, row0:row0 + tt], in_=xc[:, :, 0:tt])), reads=[b_xc], dma=True)
            if is_p:
                S.add("dve", (lambda e, tt=tt: e.tensor_copy(out=xraw[:, :, 0:3], in_=xraw[:, :, tt:tt + 3])), reads=[b_xraw], writes=[b_xraw])
            if is_p and t0 + tt == T:
                tokmajor((tt - 3, tt), 3, tok, conv_p[:, :])
            if not is_p:
                tokmajor((0, NS), NS, tok, conv_s[:, 2, :])
                scv3 = state_conv.rearrange("(b j) c -> b j c", j=3)
                S.add("sp", (lambda e: e.dma_start(out=conv_s[:, 0:2, :], in_=scv3[:, 1:3, :])), dma=True)
        S.barrier()

    if "A1" in phases:
        phase_a1()

    def phase_b():
        TT = 512
        A = Alloc()
        Wq = A.bf(8 * 4608).rearrange("p (k n) -> p k n", k=8)
        st = A.f32(2048)
        stage = [st[:, 0:1024], st[:, 1024:2048]]
        perm = A.bf(128)
        uT = A.bf(8 * TT).rearrange("p (k t) -> p k t", k=8)
        cosT = A.f32(TT)
        sinT = A.f32(TT)
        qb_ = A.bf(2 * TT)
        qb = [qb_[:, 0:TT], qb_[:, TT:2 * TT]]
        t1_ = A.f32(2 * TT)
        t1 = [t1_[:, 0:TT], t1_[:, TT:2 * TT]]
        qf_ = A.f32(2 * TT)
        qf = [qf_[:, 0:TT], qf_[:, TT:2 * TT]]
        b_qf = [S.buf("qf0"), S.buf("qf1")]
        t2_ = A.f32(2 * TT)
        t2 = [t2_[:, 0:TT], t2_[:, TT:2 * TT]]
        qk = A.bf(24 * TT).rearrange("p (m t) -> p m t", m=24)
        vtok_ = A.bf(2 * 1536)
        vtok = [vtok_[:, 0:1536], vtok_[:, 1536:3072]]
        stg_ = A.f32(2 * 512)
        stg = [stg_[:, 0:512], stg_[:, 512:1024]]
        b_w = S.buf("w")
        b_st = [S.buf("st0"), S.buf("st1")]
        load_weight(lambda k, c0, c: Wq[:, k, c0:c0 + c], w_in, 8, 5152, 4608, b_w, b_st, stage)
        b_perm = S.buf("perm")
        S.add("sp", lambda e: e.dma_start(out=stage[0][:, 0:128], in_=perm_d[:, :]), writes=[b_st[0]], dma=True)
        S.add("dve", lambda e: e.tensor_copy(out=perm, in_=stage[0][:, 0:128]), reads=[b_st[0]], writes=[b_perm])
        b_uT = S.buf("uT")
        b_cs = S.buf("cs")
        if os.environ.get('MSCOS'):
            S.add("dve", lambda e: e.memset(cosT, 0.5), writes=[b_cs])
            S.add("dve", lambda e: e.memset(sinT, 0.5), writes=[b_cs])
        b_qb = [S.buf("qb0"), S.buf("qb1")]
        b_t1 = [S.buf("t10"), S.buf("t11")]
        b_t2 = [S.buf("t20"), S.buf("t21")]
        b_qk = S.buf("qk")
        b_vt = [S.buf("vt0"), S.buf("vt1")]
        b_stg = [S.buf("sg0"), S.buf("sg1")]
        b_ps = new_ps()
        Uv = U.rearrange("(k p) t -> p k t", p=128)
        QTv = QT.rearrange("(m p) t -> p m t", p=128)
        KTv = KT.rearrange("(m p) t -> p m t", p=128)
        stg_i = [0]

        for (is_p, t0, tt) in tiles_of(TT):
            row0 = t0 if is_p else T
            nsub = (tt + 127) // 128
            S.add("sp", (lambda e, row0=row0, tt=tt: e.dma_start(out=uT[:, :, 0:tt], in_=Uv[:, :, row0:row0 + tt])), writes=[b_uT], dma=True)
            S.add("sp", (lambda e, row0=row0, tt=tt: e.dma_start(out=cosT[:, 0:tt], in_=cos_d[:, row0:row0 + tt])), writes=[b_cs], dma=True)
            S.add("sp", (lambda e, row0=row0, tt=tt: e.dma_start(out=sinT[:, 0:tt], in_=sin_d[:, row0:row0 + tt])), writes=[b_cs], dma=True)
            for m in range(0 if not os.environ.get('NOROT') else 24, 24):
                pa = PS[m % 2]
                pb = PS[2 + m % 2]
                i2 = m % 2

                def mm(e, m=m, pa=pa, tt=tt):
                    ins = None
                    for k in range(8):
                        ins = e.matmul(pa[:, 0:tt], lhsT=Wq[:, k, m * 128:(m + 1) * 128], rhs=uT[:, k, 0:tt], start=(k == 0), stop=(k == 7))
                    return ins
                S.add("pe", mm, reads=[b_w, b_uT], writes=[b_ps[m % 2]])
                S.add("act", (lambda e, pa=pa, i2=i2, tt=tt: e.copy(out=qb[i2][:, 0:tt], in_=pa[:, 0:tt])), reads=[b_ps[m % 2]], writes=[b_qb[i2]])
                S.add("pe", (lambda e, pb=pb, i2=i2, tt=tt: e.matmul(pb[:, 0:tt], lhsT=perm, rhs=qb[i2][:, 0:tt], start=True, stop=True)),
                      reads=[b_qb[i2], b_perm], writes=[b_ps[2 + m % 2]])
                S.add("act", (lambda e, pa=pa, i2=i2, tt=tt: e.copy(out=qf[i2][:, 0:tt], in_=pa[:, 0:tt])), reads=[b_ps[m % 2]], writes=[b_qf[i2]])
                S.add("pool", (lambda e, i2=i2, tt=tt: e.tensor_tensor(out=t1[i2][:, 0:tt], in0=qf[i2][:, 0:tt], in1=(cosT if not os.environ.get('NOCOS') else qf[i2])[:, 0:tt], op=ALU.mult)),
                      reads=[b_qf[i2], b_cs], writes=[b_t1[i2]])
                S.add("dve", (lambda e, pb=pb, i2=i2, tt=tt: e.tensor_tensor(out=t2[i2][:, 0:tt], in0=pb[:, 0:tt], in1=(sinT if not os.environ.get('NOCOS') else qf[i2])[:, 0:tt], op=ALU.mult)),
                      reads=[b_ps[2 + m % 2], b_cs], writes=[b_t2[i2]])
                S.add("dve", (lambda e, m=m, i2=i2, tt=tt: e.tensor_tensor(out=qk[:, m, 0:tt], in0=t1[i2][:, 0:tt], in1=t2[i2][:, 0:tt], op=ALU.add)),
                      reads=[b_t1[i2], b_t2[i2]], writes=[b_qk])
            S.add("pool", (lambda e, row0=row0, tt=tt: e.dma_start(out=QTv[:, :, row0:row0 + tt], in_=qk[:, 0:12, 0:tt])), reads=[b_qk], dma=True)
            S.add("pool", (lambda e, row0=row0, tt=tt: e.dma_start(out=KTv[:, :, row0:row0 + tt], in_=qk[:, 12:24, 0:tt])), reads=[b_qk], dma=True)
            if not is_p:
                for c6 in range(6):
                    pt = PS[6 + c6 % 2]
                    ptv = pt[:].bitcast(BF16)[:, 0:512]

                    def trq(e, c6=c6, ptv=ptv):
                        ins = None
                        for c in range(4):
                            ins = e.transpose(out=ptv[0:NS, c * 128:(c + 1) * 128], in_=qk[:, c6 * 4 + c, 0:NS], identity=ident[:, :])
                        return ins
                    S.add("pe", trq, reads=[b_qk, b_ident], writes=[b_ps[6 + c6 % 2]])
                    si = stg_i[0] % 2
                    stg_i[0] += 1
                    S.add("dve", (lambda e, ptv=ptv, si=si: e.tensor_copy(out=stg[si][0:NS, :], in_=ptv[0:NS, :])),
                          reads=[b_ps[6 + c6 % 2]], writes=[b_stg[si]])
                    S.add("sp", (lambda e, c6=c6, si=si: e.dma_start(out=QKS[:, c6 * 512:(c6 + 1) * 512], in_=stg[si][0:NS, :])), reads=[b_stg[si]], dma=True)
            for s in range(nsub if not os.environ.get('NOV') else 0):
                r = min(128, tt - s * 128)
                tok0 = t0 + s * 128
                vt = vtok[s % 2]
                for g in range(3):
                    pv = PS[4 + g % 2]

                    def mmv(e, g=g, pv=pv, s=s, r=r):
                        ins = None
                        for k in range(8):
                            ins = e.matmul(pv[0:r, :], lhsT=uT[:, k, s * 128:s * 128 + r], rhs=Wq[:, k, 3072 + g * 512:3072 + (g + 1) * 512], start=(k == 0), stop=(k == 7))
                        return ins
                    S.add("pe", mmv, reads=[b_w, b_uT], writes=[b_ps[4 + g % 2]])
                    S.add("act", (lambda e, g=g, pv=pv, r=r, vt=vt: e.copy(out=vt[0:r, g * 512:(g + 1) * 512], in_=pv[0:r, :])),
                          reads=[b_ps[4 + g % 2]], writes=[b_vt[s % 2]])
                    inwin = ((not is_p) or (tok0 >= T - W_G[g])) and not os.environ.get('NOKV')
                    if inwin:
                        if is_p:
                            dk = kv_p[g][tok0 - (T - W_G[g]):tok0 - (T - W_G[g]) + r, 0, :]
                            dv = kv_p[g][tok0 - (T - W_G[g]):tok0 - (T - W_G[g]) + r, 1, :]
                        else:
                            dk = kv_s[g][0:r, 0, :]
                            dv = kv_s[g][0:r, 1, :]
                        si = stg_i[0] % 2
                        stg_i[0] += 1
                        S.add("act", (lambda e, pv=pv, r=r, si=si: e.copy(out=stg[si][0:r, :], in_=pv[0:r, :])),
                              reads=[b_ps[4 + g % 2]], writes=[b_stg[si]])
                        S.add("sp", (lambda e, dv=dv, r=r, si=si: e.dma_start(out=dv, in_=stg[si][0:r, :])), reads=[b_stg[si]], dma=True)
                        pt = PS[6 + g % 2]
                        ptv = pt[:].bitcast(BF16)[:, 0:512]

                        def trk(e, g=g, s=s, r=r, ptv=ptv):
                            ins = None
                            for c in range(4):
                                ins = e.transpose(out=ptv[0:r, c * 128:(c + 1) * 128], in_=qk[:, 12 + g * 4 + c, s * 128:s * 128 + r], identity=ident[:, :])
                            return ins
                        S.add("pe", trk, reads=[b_qk, b_ident], writes=[b_ps[6 + g % 2]])
                        si = stg_i[0] % 2
                        stg_i[0] += 1
                        S.add("dve", (lambda e, ptv=ptv, r=r, si=si: e.tensor_copy(out=stg[si][0:r, :], in_=ptv[0:r, :])),
                              reads=[b_ps[6 + g % 2]], writes=[b_stg[si]])
                        S.add("sp", (lambda e, dk=dk, r=r, si=si: e.dma_start(out=dk, in_=stg[si][0:r, :])), reads=[b_stg[si]], dma=True)
                S.add("pool", (lambda e, vt=vt, r=r, row0=row0, s=s: e.dma_start(out=V[row0 + s * 128:row0 + s * 128 + r, :], in_=vt[0:r, :])),
                      reads=[b_vt[s % 2]], dma=True)
        S.barrier()

    if "B" in phases:
        phase_b()

    def phase_a2():
        A = Alloc()
        Wz = A.bf(8 * 2048).rearrange("p (k n) -> p k n", k=8)
        Wdt = A.bf(8 * 32).rearrange("p (k n) -> p k n", k=8)
        Wgs = A.bf(8 * D).rearrange("p (k n) -> p k n", k=8)
        Wo = A.bf(16 * D).rearrange("p (k n) -> p k n", k=16)
        st = A.f32(2048)
        stage = [st[:, 0:1024], st[:, 1024:2048]]
        cm = A.f32(640)
        triu, strict, ones = cm[:, 0:128], cm[:, 256:384], cm[:, 512:640]
        dtb = A.f32(32)
        abc = A.f32(32)
        dsk = A.f32(32)
        gss = A.f32(2048)
        A_mark = A.off
        uT = A.bf(1024).rearrange("p (k t) -> p k t", k=8)
        xcT = A.bf(24 * 128).rearrange("p (m t) -> p m t", m=24)
        xtok = A.bf(2560)
        dt = A.f32(32)
        dta = A.f32(32)
        eacs = A.f32(32)
        dA = A.f32(32)
        zs = A.bf(2048)
        Rm = A.f32(4096)
        decT = A.bf(4096)
        wts = A.bf(4096)
        cbm = A.bf(512)
        xdt = A.bf(2048)
        xw = A.bf(2048)
        stT = A.f32(2048)
        stB = A.bf(2048)
        y = A.f32(2048)
        tmp = A.f32(512)
        tmp2 = A.f32(512)
        yn = A.bf(2048)
        ynT = A.bf(2048).rearrange("p (k t) -> p k t", k=16)
        sg = A.bf(128)
        MSc = A.bf(1024).rearrange("p (k t) -> p k t", k=8)
        junk = A.bf(2048)
        b_w = S.buf("w")
        b_st = [S.buf("st0"), S.buf("st1")]
        load_weight(lambda k, c0, c: Wz[:, k, c0:c0 + c], w_in, 8, 0, 2048, b_w, b_st, stage)
        load_weight(lambda k, c0, c: Wdt[:, k, c0:c0 + c], w_in, 8, 5120, 32, b_w, b_st, stage)
        load_weight(lambda k, c0, c: Wgs[:, k, c0:c0 + c], w_in, 8, 9760, 1024, b_w, b_st, stage)
        load_weight(lambda k, c0, c: Wo[:, k, c0:c0 + c], w_o_ssm, 16, 0, 1024, b_w, b_st, stage)
        b_c = S.buf("consts")
        S.add("sp", lambda e: e.dma_start(out=cm, in_=cmask_d[:, :]), writes=[b_c], dma=True)
        load_bc(dtb, dt_bias, 32, b_c)
        load_bc(abc, a_log, 32, b_c)
        load_bc(dsk, d_skip, 32, b_c)
        load_bc(gss, norm_ssm, 2048, b_c)
        S.add("act", lambda e: e.activation(out=abc, in_=abc, func=AF.Exp), reads=[b_c], writes=[b_c])
        S.add("dve", lambda e: e.tensor_scalar(out=abc, in0=abc, scalar1=-1.0, scalar2=None, op0=ALU.mult), reads=[b_c], writes=[b_c])
        nm = ["uT", "xcT", "xtok", "dt", "dta", "eacs", "dA", "zs", "Rm", "decT", "wts", "cbm", "xdt", "xw", "stT", "stB", "y", "tmp", "tmp2", "yn", "ynT", "sg", "MSc", "junk", "ss"]
        B = {n: S.buf(n) for n in nm}
        b_ps = new_ps()
        ssv = CP[:, 16:17]
        Uv = U.rearrange("(k p) t -> p k t", p=128)
        XCv = XC.rearrange("(m p) t -> p m t", p=128)
        MSv = MS.rearrange("(k p) t -> p k t", p=128)
        S.add("dve", lambda e: e.memset(stT, 0.0), writes=[B["stT"]])
        S.add("dve", lambda e: e.memset(stB, 0.0), writes=[B["stB"]])
        bc3 = lambda ap, n1, n2: ap.unsqueeze(2).to_broadcast([128, n1, n2])

        for c in range(T // 128):
            c0 = c * 128
            S.add("sp", (lambda e, c0=c0: e.dma_start(out=uT, in_=Uv[:, :, c0:c0 + 128])), writes=[B["uT"]], dma=True)
            S.add("sp", (lambda e, c0=c0: e.dma_start(out=xcT, in_=XCv[:, :, c0:c0 + 128])), writes=[B["xcT"]], dma=True)
            for g3 in range(3):
                nck = 8 if g3 < 2 else 4
                ptv = PS[6][:].bitcast(BF16)[:, 0:1024]

                def tr(e, g3=g3, nck=nck, ptv=ptv):
                    ins = None
                    for j in range(nck):
                        ins = e.transpose(out=ptv[:, j * 128:(j + 1) * 128], in_=xcT[:, g3 * 8 + j, :], identity=ident[:, :])
                    return ins
                S.add("pe", tr, reads=[B["xcT"], b_ident], writes=[b_ps[6]])
                S.add("act", (lambda e, g3=g3, nck=nck, ptv=ptv: e.copy(out=xtok[:, g3 * 1024:g3 * 1024 + nck * 128], in_=ptv[:, 0:nck * 128])), reads=[b_ps[6]], writes=[B["xtok"]])
            def mmdt(e):
                ins = None
                for k in range(8):
                    ins = e.matmul(PS[0][:, 0:32], lhsT=uT[:, k, :], rhs=Wdt[:, k, :], start=(k == 0), stop=(k == 7))
                return ins
            S.add("pe", mmdt, reads=[b_w, B["uT"]], writes=[b_ps[0]])
            S.add("dve", lambda e: e.tensor_tensor(out=dt, in0=PS[0][:, 0:32], in1=dtb, op=ALU.add), reads=[b_ps[0], b_c], writes=[B["dt"]])
            S.add("act", lambda e: e.activation(out=dt, in_=dt, func=AF.Exp), reads=[B["dt"]], writes=[B["dt"]])
            S.add("act", lambda e: e.activation(out=dt, in_=dt, func=AF.Ln, bias=1.0), reads=[B["dt"]], writes=[B["dt"]])
            S.add("dve", lambda e: e.tensor_tensor(out=dta, in0=dt, in1=abc, op=ALU.mult), reads=[B["dt"], b_c], writes=[B["dta"]])
            S.add("pe", lambda e: e.matmul(PS[0][:, 32:64], lhsT=triu, rhs=dta, start=True, stop=True), reads=[B["dta"], b_c], writes=[b_ps[0]])
            S.add("act", lambda e: e.activation(out=eacs, in_=PS[0][:, 32:64], func=AF.Exp), reads=[b_ps[0]], writes=[B["eacs"]])
            S.add("pe", lambda e: e.matmul(PS[0][:, 64:96], lhsT=ones, rhs=dta, start=True, stop=True), reads=[B["dta"], b_c], writes=[b_ps[0]])
            S.add("act", lambda e: e.activation(out=dA, in_=PS[0][:, 64:96], func=AF.Exp), reads=[b_ps[0]], writes=[B["dA"]])
            S.add("dve", lambda e: e.tensor_tensor(out=Rm.rearrange("p (h l) -> p h l", h=32), in0=bc3(dta, 32, 128), in1=triu.unsqueeze(1).to_broadcast([128, 32, 128]), op=ALU.mult),
                  reads=[B["dta"], b_c], writes=[B["Rm"]])
            for hb in range(8):
                pb = PS[2 + hb % 2]
                S.add("pe", (lambda e, hb=hb, pb=pb: e.matmul(pb[:, :], lhsT=strict, rhs=Rm[:, hb * 512:(hb + 1) * 512], start=True, stop=True)), reads=[B["Rm"], b_c], writes=[b_ps[2 + hb % 2]])
                S.add("act", (lambda e, hb=hb, pb=pb: e.activation(out=decT[:, hb * 512:(hb + 1) * 512], in_=pb[:, :], func=AF.Exp)), reads=[b_ps[2 + hb % 2]], writes=[B["decT"]])
            def mmcb(e):
                ins = None
                for g in range(4):
                    ins = e.matmul(PS[1][:, g * 128:(g + 1) * 128], lhsT=xcT[:, 16 + g, :], rhs=xcT[:, 20 + g, :], start=True, stop=True)
                return ins
            S.add("pe", mmcb, reads=[B["xcT"]], writes=[b_ps[1]])
            S.add("dve", lambda e: e.tensor_tensor(out=cbm.rearrange("p (g l) -> p g l", g=4), in0=PS[1][:, :].rearrange("p (g l) -> p g l", g=4), in1=triu.unsqueeze(1).to_broadcast([128, 4, 128]), op=ALU.mult),
                  reads=[b_ps[1], b_c], writes=[B["cbm"]])
            S.add("dve", lambda e: e.tensor_tensor(out=wts.rearrange("p (g h l) -> p g h l", g=4, h=8), in0=decT.rearrange("p (g h l) -> p g h l", g=4, h=8),
                                                    in1=cbm.rearrange("p (g l) -> p g l", g=4).unsqueeze(2).to_broadcast([128, 4, 8, 128]), op=ALU.mult),
                  reads=[B["decT"], B["cbm"]], writes=[B["wts"]])
            S.add("dve", lambda e: e.tensor_tensor(out=xdt.rearrange("p (h e) -> p h e", h=32), in0=xtok[:, 0:2048].rearrange("p (h e) -> p h e", h=32), in1=bc3(dt, 32, 64), op=ALU.mult),
                  reads=[B["xtok"], B["dt"]], writes=[B["xdt"]])
            S.add("dve", lambda e: e.tensor_tensor(out=xw.rearrange("p (h e) -> p h e", h=32), in0=xdt.rearrange("p (h e) -> p h e", h=32),
                                                    in1=decT.rearrange("p (h l) -> p h l", h=32)[:, :, 127:128].to_broadcast([128, 32, 64]), op=ALU.mult),
                  reads=[B["xdt"], B["decT"]], writes=[B["xw"]])
            for nb in range(4):
                pz = PS[4 + nb % 2]

                def mmz(e, nb=nb, pz=pz):
                    ins = None
                    for k in range(8):
                        ins = e.matmul(pz[:, :], lhsT=uT[:, k, :], rhs=Wz[:, k, nb * 512:(nb + 1) * 512], start=(k == 0), stop=(k == 7))
                    return ins
                S.add("pe", mmz, reads=[b_w, B["uT"]], writes=[b_ps[4 + nb % 2]])
                S.add("act", (lambda e, nb=nb, pz=pz: e.activation(out=zs[:, nb * 512:(nb + 1) * 512], in_=pz[:, :], func=AF.Silu)), reads=[b_ps[4 + nb % 2]], writes=[B["zs"]])
            for g in range(4):
                pa, pb = PS[4], PS[5]

                def mmy(e, g=g, pa=pa, pb=pb):
                    ins = None
                    for hh in range(8):
                        h = g * 8 + hh
                        ins = e.matmul(pa[:, hh * 64:(hh + 1) * 64], lhsT=wts[:, h * 128:(h + 1) * 128], rhs=xdt[:, h * 64:(h + 1) * 64], start=True, stop=True)
                    ins = e.matmul(pb[:, :], lhsT=xcT[:, 20 + g, :], rhs=stB[:, g * 512:(g + 1) * 512], start=True, stop=True)
                    return ins
                S.add("pe", mmy, reads=[B["wts"], B["xdt"], B["xcT"], B["stB"]], writes=[b_ps[4], b_ps[5]])
                S.add("dve", (lambda e, g=g, pb=pb: e.tensor_tensor(out=tmp.rearrange("p (h e) -> p h e", h=8), in0=pb[:, :].rearrange("p (h e) -> p h e", h=8), in1=bc3(eacs[:, g * 8:(g + 1) * 8], 8, 64), op=ALU.mult)),
                      reads=[b_ps[5], B["eacs"]], writes=[B["tmp"]])
                S.add("dve", (lambda e, g=g, pa=pa: e.tensor_tensor(out=y[:, g * 512:(g + 1) * 512], in0=pa[:, :], in1=tmp, op=ALU.add)), reads=[b_ps[4], B["tmp"]], writes=[B["y"]])
                S.add("pool", (lambda e, g=g: e.tensor_tensor(out=tmp2.rearrange("p (h e) -> p h e", h=8), in0=xtok[:, g * 512:(g + 1) * 512].rearrange("p (h e) -> p h e", h=8), in1=bc3(dsk[:, g * 8:(g + 1) * 8], 8, 64), op=ALU.mult)),
                      reads=[B["xtok"], b_c], writes=[B["tmp2"]])
                S.add("pool", (lambda e, g=g: e.tensor_tensor(out=y[:, g * 512:(g + 1) * 512], in0=y[:, g * 512:(g + 1) * 512], in1=tmp2, op=ALU.add)), reads=[B["y"], B["tmp2"]], writes=[B["y"]])
            for g in range(4):
                pc = PS[7]
                S.add("pe", (lambda e, g=g, pc=pc: e.matmul(pc[:, :], lhsT=xtok[:, 2048 + g * 128:2048 + (g + 1) * 128], rhs=xw[:, g * 512:(g + 1) * 512], start=True, stop=True)),
                      reads=[B["xtok"], B["xw"]], writes=[b_ps[7]])
                sv = stT[:, g * 512:(g + 1) * 512]
                S.add("dve", (lambda e, g=g, sv=sv: e.tensor_tensor(out=sv.rearrange("p (h e) -> p h e", h=8), in0=sv.rearrange("p (h e) -> p h e", h=8), in1=bc3(dA[:, g * 8:(g + 1) * 8], 8, 64), op=ALU.mult)),
                      reads=[B["stT"], B["dA"]], writes=[B["stT"]])
                S.add("dve", (lambda e, sv=sv, pc=pc: e.tensor_tensor(out=sv, in0=pc[:, :], in1=sv, op=ALU.add)), reads=[b_ps[7], B["stT"]], writes=[B["stT"]])
                S.add("act", (lambda e, g=g, sv=sv: e.copy(out=stB[:, g * 512:(g + 1) * 512], in_=sv)), reads=[B["stT"]], writes=[B["stB"]])
            S.add("dve", lambda e: e.tensor_tensor(out=y, in0=y, in1=zs, op=ALU.mult), reads=[B["y"], B["zs"]], writes=[B["y"]])
            norm_T(y, B["y"], 128, gss, b_c, ssv, B["ss"], junk, B["junk"], yn, B["yn"], PS[6], b_ps[6], ynT, B["ynT"], 0, nk=16)
            for m in range(8):
                pa, pb = PS[2 + m % 2], PS[m % 2]

                def mmo(e, m=m, pa=pa, pb=pb):
                    ins = None
                    for k in range(16):
                        ins = e.matmul(pa[:, 0:128], lhsT=Wo[:, k, m * 128:(m + 1) * 128], rhs=ynT[:, k, :], start=(k == 0), stop=(k == 15))
                    for k in range(8):
                        ins = e.matmul(pb[:, 0:128], lhsT=Wgs[:, k, m * 128:(m + 1) * 128], rhs=uT[:, k, :], start=(k == 0), stop=(k == 7))
                    return ins
                S.add("pe", mmo, reads=[b_w, B["ynT"], B["uT"]], writes=[b_ps[2 + m % 2], b_ps[m % 2]])
                S.add("act", (lambda e, pb=pb: e.activation(out=sg, in_=pb[:, 0:128], func=AF.Sigmoid)), reads=[b_ps[m % 2]], writes=[B["sg"]])
                S.add("dve", (lambda e, m=m, pa=pa: e.tensor_tensor(out=MSc[:, m, :], in0=pa[:, 0:128], in1=sg, op=ALU.mult)), reads=[b_ps[2 + m % 2], B["sg"]], writes=[B["MSc"]])
            S.add("pool", (lambda e, c0=c0: e.dma_start(out=MSv[:, :, c0:c0 + 128], in_=MSc)), reads=[B["MSc"]], dma=True)
        for j in range(16):
            pt = PS[j % 2]
            S.add("pe", (lambda e, j=j, pt=pt: e.transpose(out=pt[:, 0:128], in_=stT[:, j * 128:(j + 1) * 128], identity=ident_f[:, :])), reads=[B["stT"]], writes=[b_ps[j % 2]])
            S.add("act", (lambda e, j=j, pt=pt: e.copy(out=Rm[:, j * 128:(j + 1) * 128], in_=pt[:, 0:128])), reads=[b_ps[j % 2]], writes=[B["Rm"]])
        S.add("sp", lambda e: e.dma_start(out=ssm_p.rearrange("(j p) n -> p j n", p=128), in_=Rm[:, 0:2048].rearrange("p (j n) -> p j n", j=16)), reads=[B["Rm"]], dma=True)
        S.barrier()

        A.off = A_mark
        uTs = A.bf(128).rearrange("p (k t) -> p k t", k=8)
        xcs = A.bf(24 * NS).rearrange("p (m t) -> p m t", m=24)
        xsf = A.f32(256).rearrange("p (j b) -> p j b", j=16)
        Ex = A.f32(2048)
        cols = A.f32(4)
        dtT = A.f32(16)
        decs = A.f32(16)
        dtx = A.f32(256).rearrange("p (j b) -> p j b", j=16)
        decx = A.f32(256).rearrange("p (j b) -> p j b", j=16)
        dskx = A.f32(16)
        xdtx = A.f32(256).rearrange("p (j b) -> p j b", j=16)
        bctok = A.f32(1024)
        bcbc = [A.f32(1024), A.f32(1024)]
        stt = [A.f32(2048).rearrange("p (j n) -> p j n", j=16) for _ in range(2)]
        tA = A.f32(512).rearrange("p (j n) -> p j n", j=4)
        tB = A.f32(512).rearrange("p (j n) -> p j n", j=4)
        ycol = A.f32(256).rearrange("p (j b) -> p j b", j=16)
        zsT = A.f32(256).rearrange("p (j b) -> p j b", j=16)
        yg = A.f32(256).rearrange("p (j b) -> p j b", j=16)
        sq = A.f32(256).rearrange("p (j b) -> p j b", j=16)
        red = A.f32(16)
        rstd = A.f32(16)
        gsT = A.f32(16)
        ynTs = A.bf(256).rearrange("p (k t) -> p k t", k=16)
        sgs = A.bf(16)
        MSs = A.bf(128).rearrange("p (k t) -> p k t", k=8)
        nm2 = ["uTs", "xcs", "xsf", "Ex", "cols", "dtT", "decs", "dtx", "decx", "dskx", "xdtx", "bctok", "bc0", "bc1", "st0", "st1", "tA", "tB", "ycol", "zsT", "yg", "sq", "red", "rstd", "gsT", "ynTs", "sgs", "MSs"]
        Bs = {n: S.buf(n) for n in nm2}
        b_w2 = S.buf("w2")
        b_ps = new_ps()
        S.add("sp", lambda e: e.dma_start(out=uTs, in_=Uv[:, :, T:T + NS]), writes=[Bs["uTs"]], dma=True)
        S.add("sp", lambda e: e.dma_start(out=xcs, in_=XCv[:, :, T:T + NS]), writes=[Bs["xcs"]], dma=True)
        S.add("sp", lambda e: e.dma_start(out=Ex[0:32, :], in_=expand_d[:, :]), writes=[Bs["Ex"]], dma=True)
        S.add("sp", lambda e: e.dma_start(out=cols[0:32, 0:1], in_=dt_biasT[:, :]), writes=[Bs["cols"]], dma=True)
        S.add("sp", lambda e: e.dma_start(out=cols[0:32, 1:2], in_=a_logT[:, :]), writes=[Bs["cols"]], dma=True)
        S.add("sp", lambda e: e.dma_start(out=cols[0:32, 2:3], in_=d_skipT[:, :]), writes=[Bs["cols"]], dma=True)
        S.add("sp", lambda e: e.dma_start(out=gsT, in_=norm_ssmT[:, :]), writes=[Bs["gsT"]], dma=True)
        S.add("act", lambda e: e.activation(out=cols[0:32, 1:2], in_=cols[0:32, 1:2], func=AF.Exp), reads=[Bs["cols"]], writes=[Bs["cols"]])
        S.add("dve", lambda e: e.tensor_scalar(out=cols[0:32, 1:2], in0=cols[0:32, 1:2], scalar1=-1.0, scalar2=None, op0=ALU.mult), reads=[Bs["cols"]], writes=[Bs["cols"]])

        def mmdts(e):
            ins = None
            for k in range(8):
                ins = e.matmul(PS[0][0:32, 0:NS], lhsT=Wdt[:, k, :], rhs=uTs[:, k, :], start=(k == 0), stop=(k == 7))
            return ins
        S.add("pe", mmdts, reads=[b_w2, Bs["uTs"]], writes=[b_ps[0]])
        S.add("act", lambda e: e.activation(out=dtT[0:32, :], in_=PS[0][0:32, 0:NS], func=AF.Exp, bias=cols[0:32, 0:1]), reads=[b_ps[0], Bs["cols"]], writes=[Bs["dtT"]])
        S.add("act", lambda e: e.activation(out=dtT[0:32, :], in_=dtT[0:32, :], func=AF.Ln, bias=1.0), reads=[Bs["dtT"]], writes=[Bs["dtT"]])
        S.add("act", lambda e: e.activation(out=decs[0:32, :], in_=dtT[0:32, :], func=AF.Exp, scale=cols[0:32, 1:2]), reads=[Bs["dtT"], Bs["cols"]], writes=[Bs["decs"]])

        def mmex(e):
            ins = None
            for j in range(16):
                ins = e.matmul(PS[1][:, j * NS:(j + 1) * NS], lhsT=Ex[0:32, j * 128:(j + 1) * 128], rhs=dtT[0:32, :], start=True, stop=True)
                ins = e.matmul(PS[1][:, 256 + j * NS:256 + (j + 1) * NS], lhsT=Ex[0:32, j * 128:(j + 1) * 128], rhs=decs[0:32, :], start=True, stop=True)
                ins = e.matmul(PS[2][:, j:j + 1], lhsT=Ex[0:32, j * 128:(j + 1) * 128], rhs=cols[0:32, 2:3], start=True, stop=True)
            return ins
        S.add("pe", mmex, reads=[Bs["Ex"], Bs["dtT"], Bs["decs"], Bs["cols"]], writes=[b_ps[1], b_ps[2]])
        S.add("dve", lambda e: e.tensor_copy(out=dtx, in_=PS[1][:, 0:256].rearrange("p (j b) -> p j b", j=16)), reads=[b_ps[1]], writes=[Bs["dtx"]])
        S.add("dve", lambda e: e.tensor_copy(out=decx, in_=PS[1][:, 256:512].rearrange("p (j b) -> p j b", j=16)), reads=[b_ps[1]], writes=[Bs["decx"]])
        S.add("dve", lambda e: e.tensor_copy(out=dskx, in_=PS[2][:, 0:16]), reads=[b_ps[2]], writes=[Bs["dskx"]])
        S.add("dve", lambda e: e.tensor_copy(out=xsf, in_=xcs[:, 0:16, :]), reads=[Bs["xcs"]], writes=[Bs["xsf"]])
        S.add("dve", lambda e: e.tensor_tensor(out=xdtx, in0=xsf, in1=dtx, op=ALU.mult), reads=[Bs["xsf"], Bs["dtx"]], writes=[Bs["xdtx"]])
        for half in range(2):
            ptv = PS[6 + half][:].bitcast(BF16)[:, 0:512]

            def trbc(e, half=half, ptv=ptv):
                ins = None
                for i in range(4):
                    ins = e.transpose(out=ptv[0:NS, i * 128:(i + 1) * 128], in_=xcs[:, 16 + half * 4 + i, :], identity=ident[:, :])
                return ins
            S.add("pe", trbc, reads=[Bs["xcs"], b_ident], writes=[b_ps[6 + half]])
            S.add("dve", (lambda e, half=half, ptv=ptv: e.tensor_copy(out=bctok[0:NS, half * 512:(half + 1) * 512], in_=ptv[0:NS, :])), reads=[b_ps[6 + half]], writes=[Bs["bctok"]])
        S.add("sp", lambda e: e.dma_start(out=BCS[:, :], in_=bctok[0:NS, :]), reads=[Bs["bctok"]], writes=[Bs["bctok"]], dma=True)
        ssm_sv = ssm_s.rearrange("b (j p) n -> b p j n", p=128)
        ssm_iv = state_ssm.rearrange("b (j p) n -> b p j n", p=128)
        for b in range(NS):
            i2 = b % 2
            bcb, stb = bcbc[i2], stt[i2]
            S.add("pool", (lambda e, b=b, bcb=bcb: e.dma_start(out=bcb, in_=BCS[b:b + 1, :].to_broadcast([128, 1024]))), reads=[Bs["bctok"]], writes=[Bs[f"bc{i2}"]], dma=True)
            S.add("sp", (lambda e, b=b, stb=stb: e.dma_start(out=stb, in_=ssm_iv[b])), writes=[Bs[f"st{i2}"]], dma=True)
            for g in range(4):
                js = slice(4 * g, 4 * g + 4)
                Bg = bcb[:, g * 128:(g + 1) * 128].unsqueeze(1).to_broadcast([128, 4, 128])
                Cg = bcb[:, 512 + g * 128:512 + (g + 1) * 128].unsqueeze(1).to_broadcast([128, 4, 128])
                S.add("dve", (lambda e, b=b, js=js, Bg=Bg: e.tensor_tensor(out=tA, in0=xdtx[:, js, b:b + 1].to_broadcast([128, 4, 128]), in1=Bg, op=ALU.mult)),
                      reads=[Bs["xdtx"], Bs[f"bc{i2}"]], writes=[Bs["tA"]])
                S.add("dve", (lambda e, b=b, js=js, stb=stb: e.tensor_tensor(out=stb[:, js, :], in0=stb[:, js, :], in1=decx[:, js, b:b + 1].to_broadcast([128, 4, 128]), op=ALU.mult)),
                      reads=[Bs[f"st{i2}"], Bs["decx"]], writes=[Bs[f"st{i2}"]])
                S.add("dve", (lambda e, js=js, stb=stb: e.tensor_tensor(out=stb[:, js, :], in0=stb[:, js, :], in1=tA, op=ALU.add)),
                      reads=[Bs[f"st{i2}"], Bs["tA"]], writes=[Bs[f"st{i2}"]])
                S.add("pool", (lambda e, js=js, stb=stb, Cg=Cg: e.tensor_tensor(out=tB, in0=stb[:, js, :], in1=Cg, op=ALU.mult)),
                      reads=[Bs[f"st{i2}"], Bs[f"bc{i2}"]], writes=[Bs["tB"]])
                S.add("dve", (lambda e, b=b, js=js: e.tensor_reduce(out=ycol[:, js, b], in_=tB, axis=AX.X, op=ALU.add)), reads=[Bs["tB"]], writes=[Bs["ycol"]])
            S.add("pool", (lambda e, b=b, stb=stb: e.dma_start(out=ssm_sv[b], in_=stb)), reads=[Bs[f"st{i2}"]], dma=True)
        S.add("dve", lambda e: e.tensor_tensor(out=yg, in0=xsf, in1=dskx.unsqueeze(2).to_broadcast([128, 16, NS]), op=ALU.mult), reads=[Bs["xsf"], Bs["dskx"]], writes=[Bs["yg"]])
        S.add("dve", lambda e: e.tensor_tensor(out=yg, in0=yg, in1=ycol, op=ALU.add), reads=[Bs["yg"], Bs["ycol"]], writes=[Bs["yg"]])
        for j in range(16):
            def mmzs(e, j=j):
                ins = None
                for k in range(8):
                    ins = e.matmul(PS[3][:, j * NS:(j + 1) * NS], lhsT=Wz[:, k, j * 128:(j + 1) * 128], rhs=uTs[:, k, :], start=(k == 0), stop=(k == 7))
                return ins
            S.add("pe", mmzs, reads=[b_w2, Bs["uTs"]], writes=[b_ps[3]])
        S.add("act", lambda e: e.activation(out=zsT, in_=PS[3][:, 0:256].rearrange("p (j b) -> p j b", j=16), func=AF.Silu), reads=[b_ps[3]], writes=[Bs["zsT"]])
        S.add("dve", lambda e: e.tensor_tensor(out=yg, in0=yg, in1=zsT, op=ALU.mult), reads=[Bs["yg"], Bs["zsT"]], writes=[Bs["yg"]])
        S.add("dve", lambda e: e.tensor_tensor(out=sq, in0=yg, in1=yg, op=ALU.mult), reads=[Bs["yg"]], writes=[Bs["sq"]])
        S.add("dve", lambda e: e.tensor_reduce(out=red, in_=sq.rearrange("p j b -> p b j"), axis=AX.X, op=ALU.add), reads=[Bs["sq"]], writes=[Bs["red"]])
        S.add("pe", lambda e: e.matmul(PS[4][:, 0:NS], lhsT=ones, rhs=red, start=True, stop=True), reads=[Bs["red"]], writes=[b_ps[4]])
        S.add("dve", lambda e: e.tensor_scalar(out=rstd, in0=PS[4][:, 0:NS], scalar1=1.0 / 2048, scalar2=EPS, op0=ALU.mult, op1=ALU.add), reads=[b_ps[4]], writes=[Bs["rstd"]])
        S.add("pool", lambda e: e.tensor_tensor(out=rstd, in0=rstd, in1=mhalf.to_broadcast([128, NS]), op=ALU.pow), reads=[Bs["rstd"]], writes=[Bs["rstd"]])
        S.add("dve", lambda e: e.tensor_tensor(out=yg, in0=yg, in1=rstd.unsqueeze(1).to_broadcast([128, 16, NS]), op=ALU.mult), reads=[Bs["yg"], Bs["rstd"]], writes=[Bs["yg"]])
        S.add("dve", lambda e: e.tensor_tensor(out=ynTs, in0=yg, in1=gsT.unsqueeze(2).to_broadcast([128, 16, NS]), op=ALU.mult), reads=[Bs["yg"], Bs["gsT"]], writes=[Bs["ynTs"]])
        for m in range(8):
            pa, pb = PS[m % 2], PS[2 + m % 2]

            def mmos(e, m=m, pa=pa, pb=pb):
                ins = None
                for k in range(16):
                    ins = e.matmul(pa[:, 0:NS], lhsT=Wo[:, k, m * 128:(m + 1) * 128], rhs=ynTs[:, k, :], start=(k == 0), stop=(k == 15))
                for k in range(8):
                    ins = e.matmul(pb[:, 0:NS], lhsT=Wgs[:, k, m * 128:(m + 1) * 128], rhs=uTs[:, k, :], start=(k == 0), stop=(k == 7))
                return ins
            S.add("pe", mmos, reads=[b_w2, Bs["ynTs"], Bs["uTs"]], writes=[b_ps[m % 2], b_ps[2 + m % 2]])
            S.add("act", (lambda e, pb=pb: e.activation(out=sgs, in_=pb[:, 0:NS], func=AF.Sigmoid)), reads=[b_ps[2 + m % 2]], writes=[Bs["sgs"]])
            S.add("dve", (lambda e, m=m, pa=pa: e.tensor_tensor(out=MSs[:, m, :], in0=pa[:, 0:NS], in1=sgs, op=ALU.mult)), reads=[b_ps[m % 2], Bs["sgs"]], writes=[Bs["MSs"]])
        S.add("pool", lambda e: e.dma_start(out=MSv[:, :, T:T + NS], in_=MSs), reads=[Bs["MSs"]], dma=True)
        S.barrier()

    if "A2" in phases:
        phase_a2()

    def phase_c():
        A = Alloc()
        WMAX = 2048
        QTs = A.bf(4 * WMAX)
        KTs = [A.bf(4 * WMAX), A.bf(4 * WMAX)]
        Vb = [A.bf(16 * 520), A.bf(16 * 520)]
        Pt = [A.bf(2048), A.bf(2048)]
        maskb = A.bf(2048)
        mstage = A.f32(256)
        Osb = [A.f32(520), A.f32(520)]
        b_q = S.buf("q")
        b_k = [S.buf("k0"), S.buf("k1")]
        b_v = [S.buf("v0"), S.buf("v1")]
        b_pt = [S.buf("pt0"), S.buf("pt1")]
        b_mask = S.buf("mask")
        b_ms = S.buf("mstage")
        b_o = [S.buf("o0"), S.buf("o1")]
        b_ps = new_ps()
        S.add("sp", lambda e: e.dma_start(out=mstage[:, 0:128], in_=cmask_d[:, 128:256]), writes=[b_ms], dma=True)
        S.add("sp", lambda e: e.dma_start(out=mstage[:, 128:256], in_=cmask_d[:, 0:128]), writes=[b_ms], dma=True)
        mv = maskb.rearrange("p (h c) -> p h c", h=8)
        for h in range(8):
            S.add("dve", (lambda e, h=h: e.tensor_copy(out=mv[:, h, :], in_=mstage[:, :])), reads=[b_ms], writes=[b_mask])
        for i in range(2):
            vv = Vb[i].rearrange("p (r h e) -> p r h e", r=16, h=8)
            S.add("dve", (lambda e, vv=vv: e.memset(vv[:, :, :, 64:65], 1.0)), writes=[b_v[i]])
        QTv = QT.rearrange("(m p) t -> p m t", p=128)
        KTv = KT.rearrange("(m p) t -> p m t", p=128)
        blk = [0]
        for g in [int(c) for c in os.environ.get('CG', '012')]:
            W = W_G[g]
            d = DIL[g]
            qs = QTs[:, 0:4 * W].rearrange("p (c t) -> p c t", c=4)
            ks = [KTs[i][:, 0:4 * W].rearrange("p (c t) -> p c t", c=4) for i in range(2)]
            for nb in range(T // W):
                cur = nb % 2
                prv = 1 - cur
                t0 = nb * W
                S.add("sp", (lambda e, qs=qs, g=g, t0=t0, W=W: e.dma_start(out=qs, in_=QTv[:, 4 * g:4 * g + 4, t0:t0 + W])), writes=[b_q], dma=True)
                S.add("sp", (lambda e, kc=ks[cur], g=g, t0=t0, W=W: e.dma_start(out=kc, in_=KTv[:, 4 * g:4 * g + 4, t0:t0 + W])), writes=[b_k[cur]], dma=True)
                vcur = Vb[cur].rearrange("p (r h e) -> p r h e", r=16, h=8)
                vprv = Vb[prv].rearrange("p (r h e) -> p r h e", r=16, h=8)
                for r in range(d):
                    src = V[t0 + r:t0 + W:d, g * 512:(g + 1) * 512].rearrange("m (h e) -> m h e", h=8)
                    S.add("pool", (lambda e, vcur=vcur, r=r, src=src: e.dma_start(out=vcur[:, r, :, 0:64], in_=src)), writes=[b_v[cur]], dma=True)
                for r in range(d):
                    bi = blk[0] % 2
                    blk[0] += 1
                    ptv = Pt[bi].rearrange("p (h c) -> p h c", h=8)
                    halves = (0, 1) if nb > 0 else (1,)

                    def scores(e, qs=qs, kcur=ks[cur], kprv=ks[prv], r=r, d=d, W=W, halves=halves):
                        ins = None
                        for h in range(8):
                            c, pr = h // 2, (h % 2) * 64
                            for hv in halves:
                                kk = kcur if hv == 1 else kprv
                                bkk = (h % 2) * 2 + h // 4
                                oo = ((h // 2) % 2) * 256 + hv * 128
                                ins = e.matmul(PS[bkk][:, oo:oo + 128],
                                               lhsT=kk[pr:pr + 64, c, r:W:d], rhs=qs[pr:pr + 64, c, r:W:d], start=True, stop=True)
                        return ins
                    S.add("pe", scores, reads=[b_q, b_k[0], b_k[1]], writes=[b_ps[0], b_ps[1], b_ps[2], b_ps[3]])
                    if os.environ.get('CSTOP') == '0':
                        continue
                    for bk in range(4):
                        if nb > 0:
                            S.add("act", (lambda e, bk=bk, bi=bi: e.activation(out=Pt[bi][:, bk * 512:(bk + 1) * 512], in_=PS[bk][:, :], func=AF.Exp, scale=0.125)),
                                  reads=[b_ps[bk]], writes=[b_pt[bi]])
                        else:
                            pin = PS[bk][:, :].rearrange("p (h c) -> p h c", h=2)[:, :, 128:256]
                            pout = Pt[bi][:, bk * 512:(bk + 1) * 512].rearrange("p (h c) -> p h c", h=2)[:, :, 128:256]
                            S.add("act", (lambda e, pin=pin, pout=pout: e.activation(out=pout, in_=pin, func=AF.Exp, scale=0.125)),
                                  reads=[b_ps[bk]], writes=[b_pt[bi]])
                    S.add("dve", (lambda e, bi=bi: e.tensor_tensor(out=Pt[bi], in0=Pt[bi], in1=maskb, op=ALU.mult)),
                          reads=[b_pt[bi], b_mask], writes=[b_pt[bi]])
                    ob = 4 + 2 * bi
                    if os.environ.get('CSTOP') == '1':
                        continue

                    def pv(e, ptv=ptv, vcur=vcur, vprv=vprv, r=r, ob=ob, halves=halves):
                        ins = None
                        for h in range(8):
                            dst = PS[ob + h // 4][:, (h % 4) * 65:(h % 4) * 65 + 65]
                            for i, hv in enumerate(halves):
                                vv = vcur if hv == 1 else vprv
                                sl = ((h % 2) * 2 + h // 4) * 2 + (h // 2) % 2
                                ins = e.matmul(dst, lhsT=ptv[:, sl, hv * 128:hv * 128 + 128], rhs=vv[:, r, h, :], start=(i == 0), stop=(i == len(halves) - 1))
                        return ins
                    S.add("pe", pv, reads=[b_pt[bi], b_v[0], b_v[1]], writes=[b_ps[ob], b_ps[ob + 1]])
                    S.add("act", (lambda e, bi=bi, ob=ob: e.copy(out=Osb[bi][:, 0:260], in_=PS[ob][:, 0:260])), reads=[b_ps[ob]], writes=[b_o[bi]])
                    S.add("dve", (lambda e, bi=bi, ob=ob: e.tensor_copy(out=Osb[bi][:, 260:520], in_=PS[ob + 1][:, 0:260])), reads=[b_ps[ob + 1]], writes=[b_o[bi]])
                    dst = AO[g][t0 + r:t0 + W:d, :]
                    S.add("sp", (lambda e, bi=bi, dst=dst: e.dma_start(out=dst, in_=Osb[bi][:, :])), reads=[b_o[bi]], dma=True)
        S.barrier()

    def phase_cs():
        A = Alloc()
        kvt = [A.f32(1024), A.f32(1024)]
        qbc = [A.f32(512), A.f32(512)]
        prod = A.f32(512)
        pvt = A.f32(520)
        sc = A.f32(8)
        selfk = A.f32(512)
        selfv = A.bf(512)
        sprod = A.f32(512)
        spv = A.f32(520)
        ssc = A.f32(8)
        ones = A.f32(1)
        osb = A.f32(520)
        b_kv = [S.buf("kv0"), S.buf("kv1")]
        b_qb = [S.buf("qb0"), S.buf("qb1")]
        b_prod = S.buf("prod")
        b_pv = S.buf("pv")
        b_sc = S.buf("sc")
        b_self = S.buf("self")
        b_sp = S.buf("sprod")
        b_spv = S.buf("spv")
        b_ssc = S.buf("ssc")
        b_ones = S.buf("ones")
        b_osb = S.buf("osb")
        b_ps = new_ps()
        S.add("dve", lambda e: e.memset(ones, 1.0), writes=[b_ones])
        it = 0
        for b in range(NS):
            for g in range(3):
                W, d = W_G[g], DIL[g]
                i2 = it % 2
                it += 1
                S.add("sp", (lambda e, i2=i2, g=g, b=b, W=W, d=d: e.dma_start(out=kvt[i2], in_=cache_kv[g][b, 0:W:d, :])), writes=[b_kv[i2]], dma=True)
                S.add("pool", (lambda e, i2=i2, g=g, b=b: e.dma_start(out=qbc[i2], in_=QKS[b:b + 1, g * 512:(g + 1) * 512].to_broadcast([128, 512]))), writes=[b_qb[i2]], dma=True)
                S.add("sp", (lambda e, g=g, b=b: e.dma_start(out=selfk[0:1, :], in_=QKS[b:b + 1, 1536 + g * 512:1536 + (g + 1) * 512])), writes=[b_self], dma=True)
                S.add("sp", (lambda e, g=g, b=b: e.dma_start(out=selfv[0:1, :], in_=V[T + b:T + b + 1, g * 512:(g + 1) * 512])), writes=[b_self], dma=True)
                S.add("dve", (lambda e, i2=i2: e.tensor_tensor(out=prod, in0=kvt[i2][:, 0:512], in1=qbc[i2], op=ALU.mult)), reads=[b_kv[i2], b_qb[i2]], writes=[b_prod])
                S.add("dve", (lambda e: e.tensor_reduce(out=sc, in_=prod.rearrange("p (h e) -> p h e", h=8), axis=AX.X, op=ALU.add)), reads=[b_prod], writes=[b_sc])
                S.add("act", (lambda e: e.activation(out=pvt[:, 512:520], in_=sc, func=AF.Exp, scale=0.125)), reads=[b_sc], writes=[b_pv])
                S.add("dve", (lambda e, i2=i2: e.tensor_tensor(out=pvt[:, 0:512].rearrange("p (h e) -> p h e", h=8), in0=kvt[i2][:, 512:1024].rearrange("p (h e) -> p h e", h=8),
                                                                in1=pvt[:, 512:520].unsqueeze(2).to_broadcast([128, 8, 64]), op=ALU.mult)), reads=[b_kv[i2], b_pv], writes=[b_pv])
                S.add("dve", (lambda e, i2=i2: e.tensor_tensor(out=sprod[0:1, :], in0=selfk[0:1, :], in1=qbc[i2][0:1, :], op=ALU.mult)), reads=[b_self, b_qb[i2]], writes=[b_sp])
                S.add("dve", (lambda e: e.tensor_reduce(out=ssc[0:1, :], in_=sprod[0:1, :].rearrange("p (h e) -> p h e", h=8), axis=AX.X, op=ALU.add)), reads=[b_sp], writes=[b_ssc])
                S.add("act", (lambda e: e.activation(out=spv[0:1, 512:520], in_=ssc[0:1, :], func=AF.Exp, scale=0.125)), reads=[b_ssc], writes=[b_spv])
                S.add("dve", (lambda e: e.tensor_tensor(out=spv[0:1, 0:512].rearrange("p (h e) -> p h e", h=8), in0=selfv[0:1, :].rearrange("p (h e) -> p h e", h=8),
                                                        in1=spv[0:1, 512:520].unsqueeze(2).to_broadcast([1, 8, 64]), op=ALU.mult)), reads=[b_self, b_spv], writes=[b_spv])

                def red(e, g=g):
                    e.matmul(PS[0][0:1, 0:512], lhsT=ones[:, 0:1], rhs=pvt[:, 0:512], start=(g == 0), stop=False)
                    e.matmul(PS[1][0:1, 0:8], lhsT=ones[:, 0:1], rhs=pvt[:, 512:520], start=(g == 0), stop=False)
                    e.matmul(PS[0][0:1, 0:512], lhsT=ones[0:1, 0:1], rhs=spv[0:1, 0:512], start=False, stop=(g == 2))
                    return e.matmul(PS[1][0:1, 0:8], lhsT=ones[0:1, 0:1], rhs=spv[0:1, 512:520], start=False, stop=(g == 2))
                S.add("pe", red, reads=[b_pv, b_spv, b_ones], writes=[b_ps[0], b_ps[1]])
            ov = osb[0:1, :].rearrange("p (h e) -> p h e", e=65)
            S.add("act", (lambda e, ov=ov: e.copy(out=ov[:, :, 0:64], in_=PS[0][0:1, 0:512].rearrange("p (h e) -> p h e", e=64))), reads=[b_ps[0]], writes=[b_osb])
            S.add("act", (lambda e, ov=ov: e.copy(out=ov[:, :, 64:65], in_=PS[1][0:1, 0:8].unsqueeze(2))), reads=[b_ps[1]], writes=[b_osb])
            S.add("pool", (lambda e, b=b: e.dma_start(out=AOS[b:b + 1, :], in_=osb[0:1, :])), reads=[b_osb], dma=True)
        S.barrier()

    if "C" in phases:
        if not os.environ.get('NOC'):
            phase_c()
        if not os.environ.get('NOCS'):
            phase_cs()

    def phase_d1():
        TT = 512
        A = Alloc()
        Wga = A.bf(8 * D).rearrange("p (k n) -> p k n", k=8)
        Woa = A.bf(4 * D).rearrange("p (k n) -> p k n", k=4)
        Wout = A.bf(8 * D).rearrange("p (k n) -> p k n", k=8)
        st = A.f32(2048)
        stage = [st[:, 0:1024], st[:, 1024:2048]]
        uT = A.bf(8 * TT).rearrange("p (k t) -> p k t", k=8)
        MSt = A.bf(8 * TT).rearrange("p (k t) -> p k t", k=8)
        attnT = A.bf(4 * TT).rearrange("p (k t) -> p k t", k=4)
        merged = A.bf(8 * TT).rearrange("p (k t) -> p k t", k=8)
        ao = [A.f32(520) for _ in range(3)]
        asum = A.f32(520)
        rl = A.f32(8)
        attn_b = A.bf(512)
        sgm = [A.bf(TT), A.bf(TT)]
        tmp = [A.f32(TT), A.f32(TT)]
        h1 = [A.f32(1024), A.f32(1024)]
        h2 = [A.f32(1024), A.f32(1024)]
        b_w = S.buf("w")
        b_st = [S.buf("st0"), S.buf("st1")]
        load_weight(lambda k, c0, c: Wga[:, k, c0:c0 + c], w_in, 8, 10784, 1024, b_w, b_st, stage)
        load_weight(lambda k, c0, c: Woa[:, k, c0:c0 + c], w_o_attn, 4, 0, 1024, b_w, b_st, stage)
        load_weight(lambda k, c0, c: Wout[:, k, c0:c0 + c], w_out, 8, 0, 1024, b_w, b_st, stage)
        b_uT, b_ms, b_at, b_mg = S.buf("uT"), S.buf("ms"), S.buf("attnT"), S.buf("merged")
        b_ao = [S.buf(f"ao{i}") for i in range(3)]
        b_as, b_rl, b_ab = S.buf("asum"), S.buf("rl"), S.buf("attn_b")
        b_sg = [S.buf("sg0"), S.buf("sg1")]
        b_tmp = [S.buf("tmp0"), S.buf("tmp1")]
        b_h1 = [S.buf("h10"), S.buf("h11")]
        b_h2 = [S.buf("h20"), S.buf("h21")]
        b_ps = new_ps()
        Uv = U.rearrange("(k p) t -> p k t", p=128)
        MSv = MS.rearrange("(k p) t -> p k t", p=128)
        for (is_p, t0, tt) in tiles_of(TT):
            row0 = t0 if is_p else T
            nsub = (tt + 127) // 128
            S.add("sp", (lambda e, row0=row0, tt=tt: e.dma_start(out=uT[:, :, 0:tt], in_=Uv[:, :, row0:row0 + tt])), writes=[b_uT], dma=True)
            S.add("sp", (lambda e, row0=row0, tt=tt: e.dma_start(out=MSt[:, :, 0:tt], in_=MSv[:, :, row0:row0 + tt])), writes=[b_ms], dma=True)
            for s in range(nsub):
                r = min(128, tt - s * 128)
                tok0 = t0 + s * 128
                if is_p:
                    for g in range(3):
                        S.add("pool", (lambda e, g=g, tok0=tok0, r=r: e.dma_start(out=ao[g][0:r, :], in_=AO[g][tok0:tok0 + r, :])), writes=[b_ao[g]], dma=True)
                    S.add("pool", (lambda e, r=r: e.tensor_tensor(out=asum[0:r, :], in0=ao[0][0:r, :], in1=ao[1][0:r, :], op=ALU.add)), reads=[b_ao[0], b_ao[1]], writes=[b_as])
                    S.add("pool", (lambda e, r=r: e.tensor_tensor(out=asum[0:r, :], in0=asum[0:r, :], in1=ao[2][0:r, :], op=ALU.add)), reads=[b_as, b_ao[2]], writes=[b_as])
                else:
                    S.add("pool", (lambda e, r=r: e.dma_start(out=asum[0:r, :], in_=AOS[0:r, :])), writes=[b_as], dma=True)
                av = asum.rearrange("p (h e) -> p h e", e=65)
                S.add("dve", (lambda e, r=r, av=av: e.reciprocal(out=rl[0:r, :], in_=av[0:r, :, 64])), reads=[b_as], writes=[b_rl])
                S.add("dve", (lambda e, r=r, av=av: e.tensor_tensor(out=attn_b[0:r, :].rearrange("p (h e) -> p h e", h=8), in0=av[0:r, :, 0:64],
                                                                    in1=rl[0:r, :].unsqueeze(2).to_broadcast([r, 8, 64]), op=ALU.mult)), reads=[b_as, b_rl], writes=[b_ab])
                trans_T(attn_b, b_ab, r, PS[6 + s % 2], b_ps[6 + s % 2], attnT, b_at, s * 128, 4)
            for m in range(8):
                pa, pb = PS[m % 2], PS[2 + m % 2]
                i2 = m % 2

                def mm(e, m=m, pa=pa, pb=pb, tt=tt):
                    ins = None
                    for k in range(4):
                        ins = e.matmul(pa[:, 0:tt], lhsT=Woa[:, k, m * 128:(m + 1) * 128], rhs=attnT[:, k, 0:tt], start=(k == 0), stop=(k == 3))
                    for k in range(8):
                        ins = e.matmul(pb[:, 0:tt], lhsT=Wga[:, k, m * 128:(m + 1) * 128], rhs=uT[:, k, 0:tt], start=(k == 0), stop=(k == 7))
                    return ins
                S.add("pe", mm, reads=[b_w, b_at, b_uT], writes=[b_ps[m % 2], b_ps[2 + m % 2]])
                S.add("act", (lambda e, pb=pb, i2=i2, tt=tt: e.activation(out=sgm[i2][:, 0:tt], in_=pb[:, 0:tt], func=AF.Sigmoid)), reads=[b_ps[2 + m % 2]], writes=[b_sg[i2]])
                S.add("dve", (lambda e, pa=pa, i2=i2, tt=tt: e.tensor_tensor(out=tmp[i2][:, 0:tt], in0=pa[:, 0:tt], in1=sgm[i2][:, 0:tt], op=ALU.mult)),
                      reads=[b_ps[m % 2], b_sg[i2]], writes=[b_tmp[i2]])
                S.add("pool", (lambda e, m=m, i2=i2, tt=tt: e.tensor_tensor(out=merged[:, m, 0:tt], in0=tmp[i2][:, 0:tt], in1=MSt[:, m, 0:tt], op=ALU.add)),
                      reads=[b_tmp[i2], b_ms], writes=[b_mg])
            for s in range(nsub):
                r = min(128, tt - s * 128)
                row = row0 + s * 128
                i2 = s % 2
                S.add("sp", (lambda e, i2=i2, row=row, r=r: e.dma_start(out=h1[i2][0:r, :], in_=H1[row:row + r, :])), writes=[b_h1[i2]], dma=True)
                for nh in range(2):
                    pd = PS[4 + nh]

                    def mmo(e, s=s, r=r, nh=nh, pd=pd):
                        ins = None
                        for k in range(8):
                            ins = e.matmul(pd[0:r, :], lhsT=merged[:, k, s * 128:s * 128 + r], rhs=Wout[:, k, nh * 512:(nh + 1) * 512], start=(k == 0), stop=(k == 7))
                        return ins
                    S.add("pe", mmo, reads=[b_w, b_mg], writes=[b_ps[4 + nh]])
                    S.add("dve", (lambda e, i2=i2, r=r, nh=nh, pd=pd: e.tensor_tensor(out=h2[i2][0:r, nh * 512:(nh + 1) * 512], in0=pd[0:r, :], in1=h1[i2][0:r, nh * 512:(nh + 1) * 512], op=ALU.add)),
                          reads=[b_ps[4 + nh], b_h1[i2]], writes=[b_h2[i2]])
                S.add("pool", (lambda e, i2=i2, row=row, r=r: e.dma_start(out=H2[row:row + r, :], in_=h2[i2][0:r, :])), reads=[b_h2[i2]], dma=True)
        S.barrier()

    if "D1" in phases:
        phase_d1()

    def d2_sub(env, s, r, row, h, b_h):
        S.add("pool", (lambda e: e.dma_start(out=H3[row:row + r, :], in_=h[0:r, :])), reads=[b_h], dma=True)

    if "D2" in phases:
        ffn_phase(H2[0:T, :], H2[T:TA, :], norm_ffn2, w2g, w2u, w2d, lambda env: None, d2_sub, lambda env, row0, tt: None)

    def phase_d3():
        A = Alloc()
        Wpg = A.bf(8 * D).rearrange("p (k n) -> p k n", k=8)
        Wpp = A.bf(2 * D).rearrange("p (k n) -> p k n", k=2)
        st = A.f32(2048)
        stage = [st[:, 0:1024], st[:, 1024:2048]]
        gple = A.f32(1024)
        gfin = A.f32(1024)
        h3 = [A.f32(1024), A.f32(1024)]
        h4 = [A.f32(1024), A.f32(1024)]
        yo = [A.f32(1024), A.f32(1024)]
        pf = [A.f32(256), A.f32(256)]
        pb16 = [A.bf(256), A.bf(256)]
        xn = [A.bf(1024), A.bf(1024)]
        u3T = [A.bf(1024).rearrange("p (k t) -> p k t", k=8) for _ in range(2)]
        pT = [A.bf(256).rearrange("p (k t) -> p k t", k=2) for _ in range(2)]
        sg = [A.f32(512), A.f32(512)]
        tm = [A.f32(512), A.f32(512)]
        junk = A.bf(1024)
        b_w = S.buf("w")
        b_st = [S.buf("st0"), S.buf("st1")]
        load_weight(lambda k, c0, c: Wpg[:, k, c0:c0 + c], w_ple_gate, 8, 0, 1024, b_w, b_st, stage)
        load_weight(lambda k, c0, c: Wpp[:, k, c0:c0 + c], w_ple_proj, 2, 0, 1024, b_w, b_st, stage)
        b_g = S.buf("g")
        load_bc(gple, norm_ple, D, b_g)
        load_bc(gfin, norm_final, D, b_g)
        b_h3 = [S.buf("h30"), S.buf("h31")]
        b_h4 = [S.buf("h40"), S.buf("h41")]
        b_yo = [S.buf("yo0"), S.buf("yo1")]
        b_pf = [S.buf("pf0"), S.buf("pf1")]
        b_pb = [S.buf("pb0"), S.buf("pb1")]
        b_xn = [S.buf("xn0"), S.buf("xn1")]
        b_u3 = [S.buf("u30"), S.buf("u31")]
        b_pT = [S.buf("pT0"), S.buf("pT1")]
        b_sg = [S.buf("sg0"), S.buf("sg1")]
        b_tm = [S.buf("tm0"), S.buf("tm1")]
        b_ss = [S.buf("ss0"), S.buf("ss1")]
        b_ss2 = [S.buf("ss20"), S.buf("ss21")]
        b_junk = S.buf("junk")
        b_ps = new_ps()
        subs = [(True, t0, 128) for t0 in range(0, T, 128)] + [(False, 0, NS)]
        for it, (is_p, t0, r) in enumerate(subs):
            i2 = it % 2
            row = t0 if is_p else T
            ssv = CP[:, 8 + i2:9 + i2]
            ssv2 = CP[:, 12 + i2:13 + i2]
            psrc = p_p[t0:t0 + r, :] if is_p else p_s[0:r, :]
            ydst = y_p[t0:t0 + r, :] if is_p else y_s[0:r, :]
            S.add("sp", (lambda e, i2=i2, row=row, r=r: e.dma_start(out=h3[i2][0:r, :], in_=H3[row:row + r, :])), writes=[b_h3[i2]], dma=True)
            S.add("sp", (lambda e, i2=i2, psrc=psrc, r=r: e.dma_start(out=pf[i2][0:r, :], in_=psrc)), writes=[b_pf[i2]], dma=True)
            norm_T(h3[i2], b_h3[i2], r, gple, b_g, ssv, b_ss[i2], junk, b_junk, xn[i2], b_xn[i2], PS[6], b_ps[6], u3T[i2], b_u3[i2], 0)
            S.add("dve", (lambda e, i2=i2, r=r: e.tensor_copy(out=pb16[i2][0:r, :], in_=pf[i2][0:r, :])), reads=[b_pf[i2]], writes=[b_pb[i2]])
            trans_T(pb16[i2], b_pb[i2], r, PS[7], b_ps[7], pT[i2], b_pT[i2], 0, 2)
            for nh in range(2):
                pg, pp = PS[nh], PS[2 + nh]

                def mm(e, i2=i2, r=r, nh=nh, pg=pg, pp=pp):
                    ins = None
                    for k in range(8):
                        ins = e.matmul(pg[0:r, :], lhsT=u3T[i2][:, k, 0:r], rhs=Wpg[:, k, nh * 512:(nh + 1) * 512], start=(k == 0), stop=(k == 7))
                    for k in range(2):
                        ins = e.matmul(pp[0:r, :], lhsT=pT[i2][:, k, 0:r], rhs=Wpp[:, k, nh * 512:(nh + 1) * 512], start=(k == 0), stop=(k == 1))
                    return ins
                S.add("pe", mm, reads=[b_w, b_u3[i2], b_pT[i2]], writes=[b_ps[nh], b_ps[2 + nh]])
                S.add("act", (lambda e, nh=nh, r=r, pg=pg: e.activation(out=sg[nh][0:r, :], in_=pg[0:r, :], func=AF.Sigmoid)), reads=[b_ps[nh]], writes=[b_sg[nh]])
                S.add("dve", (lambda e, nh=nh, r=r, pp=pp: e.tensor_tensor(out=tm[nh][0:r, :], in0=pp[0:r, :], in1=sg[nh][0:r, :], op=ALU.mult)), reads=[b_ps[2 + nh], b_sg[nh]], writes=[b_tm[nh]])
                S.add("pool", (lambda e, i2=i2, nh=nh, r=r: e.tensor_tensor(out=h4[i2][0:r, nh * 512:(nh + 1) * 512], in0=tm[nh][0:r, :], in1=h3[i2][0:r, nh * 512:(nh + 1) * 512], op=ALU.add)),
                      reads=[b_tm[nh], b_h3[i2]], writes=[b_h4[i2]])
            rstd_ops(h4[i2], b_h4[i2], r, ssv2, b_ss2[i2], junk, b_junk, D)
            S.add("dve", (lambda e, i2=i2, r=r, ssv2=ssv2: e.scalar_tensor_tensor(out=yo[i2][0:r, :], in0=h4[i2][0:r, :], scalar=ssv2[0:r, :], in1=gfin[0:r, :], op0=ALU.mult, op1=ALU.mult)),
                  reads=[b_h4[i2], b_ss2[i2], b_g], writes=[b_yo[i2]])
            S.add("pool", (lambda e, i2=i2, r=r, ydst=ydst: e.dma_start(out=ydst, in_=yo[i2][0:r, :])), reads=[b_yo[i2]], dma=True)
        S.barrier()

    if "D3" in phases:
        phase_d3()

    S.barrier()
    S.finalize()
    sems = {e: es.enter_context(nc.semaphore(f"c_{e}")) for e in ENGS}
    dsems = {e: [es.enter_context(nc.semaphore(f"d_{e}{i}")) for i in range(NDSEM)] for e in ("sp", "pool", "act")}
    with nc.Block() as block:
        @block.sync
        def _(eng):
            S.emit("sp", eng, sems, dsems)

        @block.tensor
        def _(eng):
            S.emit("pe", eng, sems, dsems)

        @block.scalar
        def _(eng):
            S.emit("act", eng, sems, dsems)

        @block.vector
        def _(eng):
            S.emit("dve", eng, sems, dsems)

        @block.gpsimd
        def _(eng):
            S.emit("pool", eng, sems, dsems)
    es.close()
    return P


def make_consts():
    c = {}
    c["c_ident"] = np.eye(128, dtype=np.float32)
    perm = np.zeros((128, 128), np.float32)
    for m in range(128):
        k = (m // 64) * 64 + ((m % 64) + 32) % 64
        perm[k, m] = 1.0
    c["c_perm"] = perm
    pos = np.concatenate([np.arange(T), np.full(NS, T)]).astype(np.float32)
    half = 32
    inv_freq = (np.float32(10000.0) ** (-np.arange(half, dtype=np.float32) / np.float32(half))).astype(np.float32)
    ang = pos[:, None] * inv_freq[None, :]
    cos = np.cos(ang).astype(np.float32)
    sin = np.sin(ang).astype(np.float32)
    cosT = np.zeros((128, TA), np.float32)
    sinT = np.zeros((128, TA), np.float32)
    for p in range(128):
        dh = p % 64
        cosT[p] = cos[:, dh % 32]
        sinT[p] = -sin[:, dh % 32] if dh < 32 else sin[:, dh % 32]
    c["c_cos"] = cosT
    c["c_sin"] = sinT
    i = np.arange(128)
    triu = (i[:, None] <= i[None, :]).astype(np.float32)
    tril = (i[:, None] >= i[None, :]).astype(np.float32)
    strict = (i[:, None] > i[None, :]).astype(np.float32)
    ones = np.ones((128, 128), np.float32)
    c["c_masks"] = np.concatenate([triu, tril, strict, triu, ones], axis=1)
    ex = np.zeros((32, 2048), np.float32)
    for h in range(32):
        ex[h, h * 64:(h + 1) * 64] = 1.0
    c["c_expand"] = ex
    return c


_CACHE = {}


def core_inputs(inputs, c, consts):
    f = lambda a: np.ascontiguousarray(a, dtype=np.float32)
    sl = slice(c * NS, (c + 1) * NS)
    m = {
        "x_prompt": f(inputs["x_prompt"][c]),
        "x_sample": f(inputs["x_sample"][sl, 0]),
        "p_prompt": f(inputs["p_prompt"][0, c]),
        "p_sample": f(inputs["p_sample"][0, sl, 0]),
        "conv_wT": f(inputs["conv_w"][0].reshape(4, 24, 128).transpose(2, 1, 0)),
        "conv_bT": f(inputs["conv_b"][0].reshape(24, 128).T),
        "state_conv": f(inputs["state_conv"][0, sl].reshape(NS * 3, 3072)),
        "state_ssm": f(inputs["state_ssm"][0, sl].reshape(NS, 2048, 128)),
        "cache_kv0": f(inputs["cache_kv_w128"][0, sl].reshape(NS, 128, 1024)),
        "cache_kv1": f(inputs["cache_kv_w512"][0, sl].reshape(NS, 512, 1024)),
        "cache_kv2": f(inputs["cache_kv_w2048"][0, sl].reshape(NS, 2048, 1024)),
        "norm_final": f(inputs["norm_final"].reshape(1, D)),
    }
    m["dt_biasT"] = f(inputs["dt_bias"].reshape(32, 1))
    m["a_logT"] = f(inputs["a_log"].reshape(32, 1))
    m["d_skipT"] = f(inputs["d_skip"].reshape(32, 1))
    m["norm_ssmT"] = f(inputs["norm_ssm"].reshape(16, 128).T)
    for k in ("norm_ffn1", "norm_mix", "dt_bias", "a_log", "d_skip", "norm_ssm", "norm_ffn2", "norm_ple"):
        m[k] = f(inputs[k].reshape(1, -1))
    for k in ("w_ffn1_gate", "w_ffn1_up", "w_ffn1_down", "w_in", "w_o_ssm", "w_o_attn", "w_out",
              "w_ffn2_gate", "w_ffn2_up", "w_ffn2_down", "w_ple_gate", "w_ple_proj"):
        m[k] = f(inputs[k][0])
    m.update(consts)
    return m


def kernel(**inputs):
    if "P" not in _CACHE:
        _CACHE["P"] = build()
    P = _CACHE["P"]
    consts = make_consts()
    in_maps = []
    for c in range(NCORES):
        m = core_inputs(inputs, c, consts)
        in_maps.append({k: m[k] for k in P.ins})
    res = run_bass_kernel_spmd(P.nc, in_maps, core_ids=list(range(NCORES)))
    R = res.results
    g = lambda name: [np.asarray(R[c][name], dtype=np.float32) for c in range(NCORES)]
    y_prompt = np.stack(g("y_prompt"), 0)
    y_sample = np.concatenate(g("y_sample"), 0).reshape(NCORES * NS, 1, D)
    outs = [y_prompt, y_sample]
    for gi in range(3):
        outs.append(np.stack(g(f"kv{gi}_p"), 0).reshape(1, NCORES, W_G[gi], 2, 8, 64))
    outs.append(np.stack(g("conv_p"), 0).reshape(1, NCORES, 3, 3072))
    outs.append(np.stack(g("ssm_p"), 0).reshape(1, NCORES, 32, 64, 128))
    for gi in range(3):
        outs.append(np.concatenate(g(f"kv{gi}_s"), 0).reshape(1, NCORES * NS, 1, 2, 8, 64))
    outs.append(np.concatenate(g("conv_s"), 0).reshape(1, NCORES * NS, 3, 3072))
    outs.append(np.concatenate(g("ssm_s"), 0).reshape(1, NCORES * NS, 32, 64, 128))
    return tuple(outs)
```
